# Optimizing a Trainium2 kernel written in Bass

```python
import math
import jax, jax.numpy as jnp
from jax import lax
import numpy as np

D_MODEL = 1024
BATCH = 8
SEQ = 2048
DEPTH = 1
DEC_BATCH = 32
DEC_SEQ = 64
PAST_LEN = 1024

CHUNK = 64
D_MIX = D_MODEL
D_S5 = D_MIX // 2
D_POOL = D_MIX - D_S5
S5_GROUP = 16
N_S5_GROUPS = D_S5 // S5_GROUP
S5_STATE = 64
POOL_WINDOWS = (2, 4, 8, 16)
N_POOL_GROUPS = len(POOL_WINDOWS)
POOL_GROUP = D_POOL // N_POOL_GROUPS
POOL_HIST = max(POOL_WINDOWS) - 1
D_FF = 4 * D_MODEL
D_PLE = 256
EPS = 1e-6
DT_MIN = 1e-3
DT_MAX = 1e-1

kernel_name = 'hymba_s5_pool_streaming_step'


def _rmsnorm(x, g):
    xf = x.astype(jnp.float32)
    y = xf * lax.rsqrt(jnp.mean(xf * xf, axis=-1, keepdims=True) + EPS)
    return (y * g.astype(jnp.float32)).astype(x.dtype)


def _s5_mixer(u, h0, lam_re, lam_im, log_dt, b_re, b_im, c_re, c_im, d_skip, w_glu):
    bsz, L, _ = u.shape
    f32 = jnp.float32
    lam = lax.complex(lam_re.astype(f32), lam_im.astype(f32))
    dt = jnp.exp(log_dt.astype(f32))[:, None]
    a_bar = jnp.exp(lam * dt)
    b = lax.complex(b_re.astype(f32), b_im.astype(f32))
    b_bar = ((a_bar - 1.0) / lam)[..., None] * b
    c = lax.complex(c_re.astype(f32), c_im.astype(f32))
    uf = u.astype(f32)
    ug = uf.reshape(bsz, L, N_S5_GROUPS, S5_GROUP)
    bu = jnp.einsum('gph,blgh->blgp', b_bar, ug)
    bu = bu.at[:, 0].add(a_bar[None] * h0)
    a = jnp.broadcast_to(a_bar, bu.shape)

    def combine(e1, e2):
        a1, b1 = e1
        a2, b2 = e2
        return a1 * a2, a2 * b1 + b2

    _, h = lax.associative_scan(combine, (a, bu), axis=1)
    y = jnp.real(jnp.einsum('ghp,blgp->blgh', c, h)).reshape(bsz, L, D_S5)
    y = jax.nn.gelu(y + d_skip.astype(f32) * uf)
    y = y * jax.nn.sigmoid(y @ w_glu.astype(f32))
    return y.astype(u.dtype), h[:, -1]


def _pool_mixer(u, hist, start_pos, w_pool, pool_scale):
    bsz, L, _ = u.shape
    f32 = jnp.float32
    xp = jnp.concatenate([hist.astype(u.dtype), u], axis=1)
    xf = xp.astype(f32)
    cs = jnp.concatenate([jnp.zeros((bsz, 1, D_POOL), f32), jnp.cumsum(xf, axis=1)], axis=1)
    pos = start_pos + jnp.arange(L)
    end = cs[:, POOL_HIST + 1:]
    outs = []
    for g, w in enumerate(POOL_WINDOWS):
        sl = slice(g * POOL_GROUP, (g + 1) * POOL_GROUP)
        begin = cs[:, POOL_HIST + 1 - w: POOL_HIST + 1 - w + L, sl]
        cnt = jnp.minimum(w, pos + 1).astype(f32)[None, :, None]
        pooled = (end[..., sl] - begin) / cnt - xf[:, POOL_HIST:, sl]
        outs.append(pooled @ w_pool[g].astype(f32))
    y = jnp.concatenate(outs, axis=-1) * pool_scale.astype(f32)
    return y.astype(u.dtype), xp[:, -POOL_HIST:]


def _layer(x, p, h0, hist, start_pos, g_mix_norm, w_in, lambda_re, lambda_im, log_dt, b_re, b_im,
           c_re, c_im, d_skip, w_glu, w_pool, pool_scale, g_s5_out, g_pool_out, w_out,
           g_mlp_norm, w_up, w_down, g_ple_norm, w_ple_gate, w_ple_proj):
    z = _rmsnorm(x, g_mix_norm) @ w_in
    y_s5, h_last = _s5_mixer(z[..., :D_S5], h0, lambda_re, lambda_im, log_dt, b_re, b_im,
                             c_re, c_im, d_skip, w_glu)
    y_pool, hist_new = _pool_mixer(z[..., D_S5:], hist, start_pos, w_pool, pool_scale)
    mixed = jnp.concatenate([_rmsnorm(y_s5, g_s5_out), _rmsnorm(y_pool, g_pool_out)], axis=-1)
    x = x + mixed @ w_out
    x = x + jnp.square(jax.nn.relu(_rmsnorm(x, g_mlp_norm) @ w_up)) @ w_down
    gate = jax.nn.sigmoid(_rmsnorm(x, g_ple_norm) @ w_ple_gate)
    x = x + (p @ w_ple_proj) * gate
    return x, h_last, hist_new


def setup_inputs(seed: int = 0) -> dict:
    key = jax.random.key(seed)
    ks = jax.random.split(key, 32)
    f32 = jnp.float32
    nrm = lambda k, s, sc: jax.random.normal(k, s, f32) * sc
    gain = lambda k, s: 1.0 + 0.05 * jax.random.normal(k, s, f32)
    n_idx = jnp.arange(S5_STATE, dtype=f32)
    return {
        'x_prompt': nrm(ks[0], (BATCH, SEQ, D_MODEL), 1.0),
        'x_sample': nrm(ks[1], (DEC_BATCH, DEC_SEQ, D_MODEL), 1.0),
        'state_s5_re': nrm(ks[2], (DEPTH, DEC_BATCH, N_S5_GROUPS, S5_STATE), 0.1),
        'state_s5_im': nrm(ks[3], (DEPTH, DEC_BATCH, N_S5_GROUPS, S5_STATE), 0.1),
        'state_pool': nrm(ks[4], (DEPTH, DEC_BATCH, POOL_HIST, D_POOL), 1.0),
        'p_prompt': nrm(ks[5], (DEPTH, BATCH, SEQ, D_PLE), 1.0),
        'p_sample': nrm(ks[6], (DEPTH, DEC_BATCH, DEC_SEQ, D_PLE), 1.0),
        'g_mix_norm': gain(ks[7], (DEPTH, D_MODEL)),
        'w_in': nrm(ks[8], (DEPTH, D_MODEL, D_MIX), D_MODEL ** -0.5),
        'lambda_re': -0.5 + 0.01 * jax.random.normal(ks[9], (DEPTH, N_S5_GROUPS, S5_STATE), f32),
        'lambda_im': math.pi * n_idx + 0.01 * jax.random.normal(ks[10], (DEPTH, N_S5_GROUPS, S5_STATE), f32),
        'log_dt': jax.random.uniform(ks[11], (DEPTH, N_S5_GROUPS), f32, math.log(DT_MIN), math.log(DT_MAX)),
        'b_re': nrm(ks[12], (DEPTH, N_S5_GROUPS, S5_STATE, S5_GROUP), (2 * S5_GROUP) ** -0.5),
        'b_im': nrm(ks[13], (DEPTH, N_S5_GROUPS, S5_STATE, S5_GROUP), (2 * S5_GROUP) ** -0.5),
        'c_re': nrm(ks[14], (DEPTH, N_S5_GROUPS, S5_GROUP, S5_STATE), S5_STATE ** -0.5),
        'c_im': nrm(ks[15], (DEPTH, N_S5_GROUPS, S5_GROUP, S5_STATE), S5_STATE ** -0.5),
        'd_skip': nrm(ks[16], (DEPTH, D_S5), 1.0),
        'w_glu': nrm(ks[17], (DEPTH, D_S5, D_S5), D_S5 ** -0.5),
        'w_pool': nrm(ks[18], (DEPTH, N_POOL_GROUPS, POOL_GROUP, POOL_GROUP), POOL_GROUP ** -0.5),
        'pool_scale': gain(ks[19], (DEPTH, D_POOL)),
        'g_s5_out': gain(ks[20], (DEPTH, D_S5)),
        'g_pool_out': gain(ks[21], (DEPTH, D_POOL)),
        'w_out': nrm(ks[22], (DEPTH, D_MIX, D_MODEL), D_MIX ** -0.5),
        'g_mlp_norm': gain(ks[23], (DEPTH, D_MODEL)),
        'w_up': nrm(ks[24], (DEPTH, D_MODEL, D_FF), D_MODEL ** -0.5),
        'w_down': nrm(ks[25], (DEPTH, D_FF, D_MODEL), D_FF ** -0.5),
        'g_ple_norm': gain(ks[26], (DEPTH, D_MODEL)),
        'w_ple_gate': nrm(ks[27], (DEPTH, D_MODEL, D_MODEL), D_MODEL ** -0.5),
        'w_ple_proj': nrm(ks[28], (DEPTH, D_PLE, D_MODEL), D_PLE ** -0.5),
        'g_final': gain(ks[29], (D_MODEL,)),
    }


def reference(x_prompt, x_sample, state_s5_re, state_s5_im, state_pool, p_prompt, p_sample,
              g_mix_norm, w_in, lambda_re, lambda_im, log_dt, b_re, b_im, c_re, c_im, d_skip,
              w_glu, w_pool, pool_scale, g_s5_out, g_pool_out, w_out, g_mlp_norm, w_up, w_down,
              g_ple_norm, w_ple_gate, w_ple_proj, g_final):
    f32 = jnp.float32
    bsz_p = x_prompt.shape[0]
    bsz_s = x_sample.shape[0]
    hp, hs = x_prompt, x_sample
    re_p, im_p, pool_p, re_s, im_s, pool_s = [], [], [], [], [], []
    for i in range(DEPTH):
        lp = (g_mix_norm[i], w_in[i], lambda_re[i], lambda_im[i], log_dt[i], b_re[i], b_im[i],
              c_re[i], c_im[i], d_skip[i], w_glu[i], w_pool[i], pool_scale[i], g_s5_out[i],
              g_pool_out[i], w_out[i], g_mlp_norm[i], w_up[i], w_down[i], g_ple_norm[i],
              w_ple_gate[i], w_ple_proj[i])
        h0_p = jnp.zeros((bsz_p, N_S5_GROUPS, S5_STATE), jnp.complex64)
        hist_p = jnp.zeros((bsz_p, POOL_HIST, D_POOL), hp.dtype)
        hp, hl_p, hn_p = _layer(hp, p_prompt[i], h0_p, hist_p, 0, *lp)
        h0_s = lax.complex(state_s5_re[i].astype(f32), state_s5_im[i].astype(f32))
        hs, hl_s, hn_s = _layer(hs, p_sample[i], h0_s, state_pool[i], PAST_LEN, *lp)
        re_p.append(jnp.real(hl_p)); im_p.append(jnp.imag(hl_p)); pool_p.append(hn_p)
        re_s.append(jnp.real(hl_s)); im_s.append(jnp.imag(hl_s)); pool_s.append(hn_s)
    y_prompt = _rmsnorm(hp, g_final)
    y_sample = _rmsnorm(hs, g_final)
    return (y_prompt, y_sample, jnp.stack(re_p), jnp.stack(im_p), jnp.stack(pool_p),
            jnp.stack(re_s), jnp.stack(im_s), jnp.stack(pool_s))
```

```python
import numpy as np
import ml_dtypes
from contextlib import ExitStack
import concourse.bass as bass
import concourse.mybir as mybir
from concourse.bass_utils import run_bass_kernel_spmd

F32, BF16 = mybir.dt.float32, mybir.dt.bfloat16
AF = mybir.ActivationFunctionType
ALU = mybir.AluOpType

NCORES = 8
D = 1024
LP = 2048
NSQ = 4
LS = 64
NTOK = LP + NSQ * LS
NTILE = NTOK // 128
TOKP = 8 + LP + NSQ * (8 + LS)
NCH = TOKP // 8
PW = 16 + LP + NSQ * (16 + LS)
EPS = 1e-6
TWO_PI = 6.283185307179586
MAGIC = 12582912.0
ENGS = ("pe", "act", "dve", "pool", "sp")
LIMIT = 100000000


def tokcol(seq, t=0):
    return 8 + t if seq == 0 else 2064 + 72 * (seq - 1) + t


def chcol_init(seq):
    return 0 if seq == 0 else 257 + 9 * (seq - 1)


def poolcol(seq, t=0):
    return 16 + t if seq == 0 else 2080 + 80 * (seq - 1) + t


class Prog:
    def __init__(self, nc, es):
        self.nc, self.es = nc, es
        self.ops = []
        self.tok = {}
        self.nid = 0
        self.sem = {e: es.enter_context(nc.semaphore("s_" + e)) for e in ENGS}
        self.cnt = {e: 0 for e in ENGS}
        self.waited = {e: {} for e in ENGS}
        self.slots = {}
        self.bank_tok = [[] for _ in range(8)]
        self.bank_ptr = 0
        self.last = {}

    def op(self, eng, fn, deps=(), inc=True, force=False):
        if self.nid >= LIMIT and not force:
            return None
        i = self.nid
        self.nid += 1
        d = []
        for x in deps:
            if x is None:
                continue
            if isinstance(x, (list, tuple)):
                d.extend([y for y in x if y is not None])
            else:
                d.append(x)
        if inc:
            self.cnt[eng] += 1
            self.tok[i] = (self.sem[eng], self.cnt[eng])
            self.last[eng] = i
        self.ops.append((eng, fn, d, inc, None))
        return i if inc else None

    def dma(self, eng, fn, deps=(), slot="misc"):
        if self.nid >= LIMIT:
            return None
        i = self.nid
        self.nid += 1
        d = []
        for x in deps:
            if x is None:
                continue
            if isinstance(x, (list, tuple)):
                d.extend([y for y in x if y is not None])
            else:
                d.append(x)
        if slot not in self.slots:
            self.slots[slot] = [self.es.enter_context(self.nc.semaphore("d_" + str(len(self.slots)))), 0]
        s = self.slots[slot]
        s[1] += 16
        self.tok[i] = (s[0], s[1])
        self.last["slot:" + str(slot)] = i
        self.ops.append((eng, fn, d, True, s[0]))
        return i

    def bank(self):
        b = self.bank_ptr
        self.bank_ptr = (b + 1) % 8
        return b, list(self.bank_tok[b])

    def bank2(self):
        if self.bank_ptr % 2:
            self.bank_ptr = (self.bank_ptr + 1) % 8
        b = self.bank_ptr
        self.bank_ptr = (b + 2) % 8
        return b, list(self.bank_tok[b]) + list(self.bank_tok[b + 1])

    def release(self, b, toks, n=1):
        if not isinstance(toks, (list, tuple)):
            toks = [toks]
        for k in range(n):
            self.bank_tok[b + k] = list(toks)

    def emit(self, barrier=True):
        nc = self.nc
        if barrier:
            lasts = [v for k, v in self.last.items() if k != "slot:wcast"]
            for eng in ENGS:
                self.op(eng, lambda e: e.nop(), lasts, inc=False, force=True)
        ops = self.ops
        self.ops = []
        with nc.Block() as block:
            def make(engname):
                def body(e):
                    w = self.waited[engname]
                    for (eng, fn, deps, inc, dsem) in ops:
                        if eng != engname:
                            continue
                        for dpt in deps:
                            sem, val = self.tok[dpt]
                            if w.get(sem.name, 0) < val:
                                e.wait_ge(sem, val)
                                w[sem.name] = val
                        ins = fn(e)
                        if dsem is not None:
                            ins.then_inc(dsem, 16)
                        elif inc:
                            ins.then_inc(self.sem[engname], 1)
                return body
            block.tensor(make("pe"))
            block.scalar(make("act"))
            block.vector(make("dve"))
            block.gpsimd(make("pool"))
            block.sync(make("sp"))


def bcast(ap, shape, axis):
    return ap.unsqueeze(axis).to_broadcast(shape)


def build(debug=None, stop=None):
    nc = bass.Bass("TRN2", target_bir_lowering=False)
    debug = debug or []

    def din(name, shape, dt=F32):
        return nc.dram_tensor(name, list(shape), dt, kind="ExternalInput").ap()

    def dout(name, shape, dt=F32):
        return nc.dram_tensor(name, list(shape), dt, kind="ExternalOutput").ap()

    x_d = din("x", [NTOK, D])
    p_d = din("p", [NTOK, 256])
    sre_d = din("st_re", [NSQ * 16, 128])
    sim_d = din("st_im", [NSQ * 16, 128])
    spool_d = din("st_pool", [NSQ, 15, 512])
    gmix_d = din("g_mix_norm", [D]); win_d = din("w_in", [D, D])
    lre_d = din("lambda_re", [16, 128]); lim_d = din("lambda_im", [16, 128]); ldt_d = din("log_dt", [32])
    bre_d = din("b_re", [16, 128, 16]); bim_d = din("b_im", [16, 128, 16])
    cre_d = din("c_re", [512, 64]); cim_d = din("c_im", [512, 64])
    dsk_d = din("d_skip", [512]); wglu_d = din("w_glu", [512, 512]); wpool_d = din("w_pool", [4, 128, 128])
    pscale_d = din("pool_scale", [512]); gs5_d = din("g_s5_out", [512]); gpo_d = din("g_pool_out", [512])
    wout_d = din("w_out", [D, D]); gmlp_d = din("g_mlp_norm", [D]); wup_d = din("w_up", [D, 4 * D])
    wdn_d = din("w_down", [4 * D, D]); gple_d = din("g_ple_norm", [D]); wgate_d = din("w_ple_gate", [D, D])
    wple_d = din("w_ple_proj", [256, D]); gfin_d = din("g_final", [D])
    identb_d = din("c_identb", [128, 128], BF16); identf_d = din("c_identf", [128, 128])
    E_d = din("c_E", [128, 64 * 128], BF16); mask_d = din("c_mask", [128, 128])
    k25_d = din("c_k25", [128, 25]); j_d = din("c_j", [128, NCH]); m0_d = din("c_m0", [128, NCH])
    icnt_d = din("c_icnt", [128, 16]); ones_d = din("c_ones", [128, 8], BF16)

    wup16 = nc.dram_tensor("wup16", [D, 4 * D], BF16, kind="Internal").ap()
    wdn16 = nc.dram_tensor("wdn16", [4 * D, D], BF16, kind="Internal").ap()
    wout16 = nc.dram_tensor("wout16", [D, D], BF16, kind="Internal").ap()
    wgate16 = nc.dram_tensor("wgate16", [D, D], BF16, kind="Internal").ap()
    wple16 = nc.dram_tensor("wple16", [256, D], BF16, kind="Internal").ap()
    y_o = dout("y", [NTOK, D])
    hre_o = dout("h_re", [80, 128]); him_o = dout("h_im", [80, 128])
    pool_o = dout("pool_new", [5, 15, 512])
    dbg_o = {}

    es = ExitStack()
    with es:
        P = Prog(nc, es)

        def sb(name, shape, dt=F32, stack=es):
            return stack.enter_context(nc.sbuf_tensor(name, list(shape), dt))

        ps = es.enter_context(nc.psum_tensor("ps", [128, 8, 512], F32))
        psb = ps[:].bitcast(BF16)

        def dump(name, ap, shape, deps):
            if name not in debug:
                return None
            o = dout("dbg_" + name, shape, ap.dtype if hasattr(ap, "dtype") else F32)
            dbg_o[name] = P.dma("sp", lambda e: e.dma_start(out=o, in_=ap), deps, slot="dbg_" + name)
            return dbg_o[name]

        identb = sb("identb", [128, 128], BF16); identf = sb("identf", [128, 128])
        ones_b = sb("ones_b", [128, 8], BF16)
        gbc = sb("gbc", [128, 4, D], F32) if False else None
        gbc3 = sb("gbc3", [128, 3, D])
        y2T = sb("y2T", [128, 4, TOKP + 256], BF16)
        ypT = sb("ypT", [128, 4, TOKP + 256], BF16)
        rstd_s5 = sb("rstd_s5", [128, NTILE]); rstd_po = sb("rstd_po", [128, NTILE])
        colv = sb("colv", [128, 3, 4])
        HL = sb("HL", [128, 2, 5, 16])

        c0 = []
        c0.append(P.dma("sp", lambda e: e.dma_start(out=identb[:], in_=identb_d), slot="c0"))
        c0.append(P.dma("sp", lambda e: e.dma_start(out=identf[:], in_=identf_d), slot="c0"))
        c0.append(P.dma("sp", lambda e: e.dma_start(out=ones_b[:], in_=ones_d), slot="c0"))
        C0 = c0[-1]

        def late_consts():
            c1 = []
            for k, g in enumerate((pscale_d, gs5_d, gpo_d)):
                c1.append(P.dma("sp", lambda e, k=k, g=g: e.dma_start(
                    out=colv[:, k, :], in_=bass.AP(g.tensor, 0, [[1, 128], [128, 4]]),
                    allow_slow_non_contiguous=True), slot="c1"))
            for k, g in enumerate((gmlp_d, gple_d, gfin_d)):
                c1.append(P.dma("sp", lambda e, k=k, g=g: e.dma_start(
                    out=gbc3[:, k, :], in_=bass.AP(g.tensor, 0, [[0, 128], [1, D]])), slot="c1"))
            return c1[-1]
        sis = ExitStack()
        lam = sb("lam", [128, 2, 16], F32, sis); ldt = sb("ldt", [128, 16], F32, sis)
        Bn = sb("Bn", [128, 2, 16, 16], F32, sis)
        Cn = sb("Cn", [128, 2, 2, 2, 64], F32, sis)
        k25 = sb("k25", [128, 25], F32, sis); maskf = sb("maskf", [128, 128], F32, sis)
        dcol = sb("dcol", [128, 32], F32, sis)
        h0n = sb("h0n", [64, 2, 128], F32, sis)

        def tile_cols(buf, kt, i):
            if i < 16:
                c = 8 + 128 * i
                return buf[:, kt, c:c + 128]
            c = tokcol(1 + 2 * (i - 16)) - 8
            return buf[:, kt, c:c + 144].rearrange("p (q c) -> p q c", c=72)[:, :, 8:72]

        def blk_cols(buf, kt, b):
            if b < 4:
                c = 8 + 512 * b
                return buf[:, kt, c:c + 512], 512
            c = tokcol(1) - 8
            return buf[:, kt, c:c + 288].rearrange("p (q c) -> p q c", c=72)[:, :, 8:72], 256

        def otile(buf, kt, i):
            if i < 16:
                c = 8 + 128 * i
                return buf[:, kt, c:c + 128]
            c = TOKP + 128 * (i - 16)
            return buf[:, kt, c:c + 128]

        def oblk(buf, kt, b, three=False):
            if b < 4:
                c = 8 + 512 * b
                return buf[:, kt, c:c + 512]
            v = buf[:, kt, TOKP:TOKP + 256]
            return v.rearrange("p (q c) -> p q c", c=64) if three else v

        def ps_cols(bk, n, b):
            if b < 4:
                return ps[:, bk, 0:n]
            return ps[:, bk, 0:256].rearrange("p (q c) -> p q c", c=64)

        def ps_tile(bk, i, width=128):
            if i < 16:
                return ps[:, bk, 0:width]
            return ps[:, bk, 0:128].rearrange("p (q c) -> p q c", c=64)

        def rstd_chain(ss_ap, out_ap, tmp_ap, scale, deps):
            t1 = P.op("dve", lambda e: e.tensor_scalar(tmp_ap, ss_ap, scale, EPS, ALU.mult, ALU.add), deps)
            t2 = P.op("act", lambda e: e.activation(tmp_ap, tmp_ap, AF.Sqrt), [t1])
            return P.op("dve", lambda e: e.reciprocal(out_ap, tmp_ap), [t2])

        def norm_transpose(x_ap, gk, ss_ap, tmp_ap, rs_ap, junk, xn_ap, xnT_dst, deps, xn_free, dst_free, act_evac=True):
            t_sq = P.op("act", lambda e: e.activation(junk, x_ap, AF.Square, accum_out=ss_ap), deps)
            t_r = rstd_chain(ss_ap, rs_ap, tmp_ap, 1.0 / D, [t_sq])
            t_xn = P.op("dve", lambda e: e.scalar_tensor_tensor(xn_ap, x_ap, rs_ap, gbc[:, gk, :], ALU.mult, ALU.mult),
                        [t_r, C0] + list(xn_free) + list(deps))
            bk, bd = P.bank()
            t_tr = None
            for j in range(8):
                t_tr = P.op("pe", lambda e, j=j: e.transpose(psb[:, bk, j * 128:(j + 1) * 128], xn_ap[:, j * 128:(j + 1) * 128], identb[:]),
                            [t_xn] + bd if j == 0 else [], inc=(j == 7))
            src = psb[:, bk, :].rearrange("p (k c) -> p k c", c=128)
            if act_evac:
                t_ev = P.op("act", lambda e: e.activation(xnT_dst, src, AF.Copy), [t_tr] + list(dst_free))
            else:
                t_ev = P.op("dve", lambda e: e.tensor_copy(xnT_dst, src), [t_tr] + list(dst_free))
            P.release(bk, t_ev)
            return t_xn, t_tr, t_ev

        zs = ExitStack()
        zT = sb("zT", [128, 4, 8, NCH], BF16, zs)
        pa = ExitStack()
        with pa:
            win_b = sb("win_b", [128, 8, D], BF16, pa)
            gmx = sb("gmx", [128, D], F32, pa)
            t_gmx = P.dma("sp", lambda e: e.dma_start(out=gmx[:], in_=bass.AP(gmix_d.tensor, 0, [[0, 128], [1, D]])), slot="gmx")
            zP = sb("zP", [128, 4, PW], F32, pa)
            xt = sb("xt", [128, 4, D], F32, pa)
            ssA = sb("ssA", [128, NTILE], F32, pa); tmpA = sb("tmpA", [128, NTILE], F32, pa); rsA = sb("rsA", [128, NTILE], F32, pa)
            xn = sb("xn", [128, 4, D], BF16, pa)
            xnT = sb("xnT", [128, 2, 8, 512], BF16, pa)
            wpool_b = sb("wpool_b", [128, 4, 128], BF16, pa)
            icnt = sb("icnt", [128, 16], F32, pa)
            pt1 = sb("pt1", [128, PW], F32, pa); pt2 = sb("pt2", [128, PW], F32, pa)
            pooledT = sb("pooledT", [128, 1, TOKP], BF16, pa)
            ysq = sb("ysq", [128, 1, TOKP + 256], BF16, pa)
            ssP = sb("ssP", [128, NTILE], F32, pa); tmpP = sb("tmpP", [128, NTILE], F32, pa)

            t_win = None
            for kt in range(0, 8, 4):
                t_win = P.dma("pool", lambda e, kt=kt: e.dma_start(
                    out=win_b[:, kt:kt + 4, :], in_=win_d.rearrange("(kt p) n -> p kt n", p=128)[:, kt:kt + 4, :]), slot="win")
            t_wpool = P.dma("pool", lambda e: e.dma_start(out=wpool_b[:], in_=wpool_d.rearrange("g k n -> k g n")), slot="wpool")
            t_icnt = P.dma("sp", lambda e: e.dma_start(out=icnt[:], in_=icnt_d), slot="icnt")
            t_z0 = P.op("dve", lambda e: e.memset(zT[:], 0.0))
            t_zp0 = P.op("dve", lambda e: e.memset(zP[:], 0.0))
            t_ss0 = P.op("dve", lambda e: e.memset(ssA[:], 0.0))
            P.op("dve", lambda e: e.memset(pt1[:], 0.0), inc=False)
            t_pt0 = P.op("dve", lambda e: e.memset(pt2[:], 0.0))
            pt_free0 = t_pt0
            deferred = []
            deferred_h = []

            def mk_thunk(eng, fn, deps=(), slot="misc", **kw):
                return lambda: P.dma(eng, fn, deps, slot=slot)
            for r, (ld_, li_) in enumerate(((lre_d, bre_d), (lim_d, bim_d))):
                deferred.append(mk_thunk("sp", lambda e, r=r, ld_=ld_: e.dma_start(
                    out=lam[:, r, :], in_=ld_.rearrange("pr q -> q pr"), allow_slow_non_contiguous=True), slot="su"))
                deferred.append(mk_thunk("sp", lambda e, r=r, li_=li_: e.dma_start(
                    out=Bn[:, r, :, :], in_=li_.rearrange("pr q h -> q pr h")), slot="su"))
            for e_ in range(2):
                deferred.append(mk_thunk("sp", lambda e, e_=e_: e.dma_start(
                    out=ldt[e_ * 64:(e_ + 1) * 64, :], in_=bass.AP(ldt_d.tensor, e_, [[0, 64], [2, 16]]),
                    allow_slow_non_contiguous=True), slot="su"))
            for r, cd in enumerate((cre_d, cim_d)):
                for e_ in range(2):
                    for hi in range(2):
                        for lo in range(8):
                            deferred.append(mk_thunk("sp", lambda e, r=r, cd=cd, e_=e_, hi=hi, lo=lo: e.dma_start(
                                out=Cn[lo * 16:(lo + 1) * 16, r, e_, hi, :],
                                in_=bass.AP(cd.tensor, (hi * 256 + lo * 32 + e_ * 16) * 64, [[64, 16], [1, 64]])), slot="su"))
            deferred.append(mk_thunk("sp", lambda e: e.dma_start(out=k25[:], in_=k25_d), slot="su"))
            deferred.append(mk_thunk("sp", lambda e: e.dma_start(out=maskf[:], in_=mask_d), slot="su"))
            for s_ in range(8):
                deferred.append(mk_thunk("sp", lambda e, s_=s_: e.dma_start(
                    out=dcol[s_ * 16:(s_ + 1) * 16, :], in_=bass.AP(dsk_d.tensor, 0, [[1, 16], [16, 32]]),
                    allow_slow_non_contiguous=True), slot="su"))
            deferred.append(mk_thunk("sp", lambda e: e.dma_start(out=h0n[:, 0, :], in_=sre_d), slot="su"))
            deferred.append(mk_thunk("sp", lambda e: e.dma_start(out=h0n[:, 1, :], in_=sim_d), slot="su"))
            for q in range(NSQ):
                for j in range(4):
                    c = poolcol(1 + q) - 15
                    deferred_h.append(mk_thunk("sp", lambda e, q=q, j=j, c=c: e.dma_start(
                        out=zP[:, j, c:c + 15], in_=spool_d[q, :, j * 128:(j + 1) * 128].rearrange("t c -> c t"),
                        allow_slow_non_contiguous=True), [t_zp0], slot="hist"))

            dq = deferred + deferred_h
            n_su = len(deferred)
            DTOK = []

            def flush(n):
                for _ in range(n):
                    if len(DTOK) < len(dq):
                        DTOK.append(dq[len(DTOK)]())
            xn_rd = [None] * 4
            xt_rd = [None] * 4
            xnT_rd = [None, None]
            z_ev = []
            XN = {}
            EVS = {}

            def chainA(hb):
                tiles = [2 * hb, 2 * hb + 1]
                sqs = []
                lds = []
                for i in tiles:
                    u = i % 4
                    t_ld = P.dma("sp", lambda e, i=i, u=u: e.dma_start(out=xt[:, u, :], in_=x_d[i * 128:(i + 1) * 128, :]),
                                 [xt_rd[u]], slot="xt%d" % u)
                    lds.append(t_ld)
                    sqs.append(P.op("act", lambda e, i=i, u=u: e.activation(xn[:, u, :], xt[:, u, :], AF.Square, accum_out=ssA[:, i:i + 1]),
                                    [t_ld, t_ss0, xn_rd[u]]))
                c0_, c1_ = tiles[0], tiles[1] + 1
                t_r = rstd_chain(ssA[:, c0_:c1_], rsA[:, c0_:c1_], tmpA[:, c0_:c1_], 1.0 / D, sqs)
                for k_, i in enumerate(tiles):
                    u = i % 4
                    t_xn = P.op("dve", lambda e, i=i, u=u: e.scalar_tensor_tensor(
                        xn[:, u, :], xt[:, u, :], rsA[:, i:i + 1], gmx[:], ALU.mult, ALU.mult), [t_r, t_gmx, xn_rd[u], lds[k_]])
                    xt_rd[u] = t_xn
                    XN[i] = t_xn

            def transA(hb):
                for i in (2 * hb, 2 * hb + 1):
                    u = i % 4
                    b = min(i // 4, 4)
                    ti = i - 4 * b
                    bb = b % 2
                    bk, bd = P.bank()
                    t_tr = None
                    for j in range(8):
                        t_tr = P.op("pe", lambda e, j=j, u=u, bk=bk: e.transpose(
                            psb[:, bk, j * 128:(j + 1) * 128], xn[:, u, j * 128:(j + 1) * 128], identb[:]),
                            ([XN[i], C0] + bd) if j == 0 else [], inc=(j == 7))
                    t_ev = P.op("act", lambda e, bk=bk, bb=bb, ti=ti: e.activation(
                        xnT[:, bb, :, ti * 128:(ti + 1) * 128], psb[:, bk, :].rearrange("p (k c) -> p k c", c=128), AF.Copy),
                        [t_tr, xnT_rd[bb]])
                    P.release(bk, t_ev)
                    xn_rd[u] = t_tr
                    EVS.setdefault(b, []).append(t_ev)

            def mmA(b):
                nt = 4 if b < 4 else 2
                bb = b % 2
                n = nt * 128
                evs = EVS[b]
                last_mm = None
                for m in range(8):
                    bk, bd = P.bank()
                    for kt in range(8):
                        last_mm = P.op("pe", lambda e, m=m, kt=kt, bk=bk, n=n, bb=bb: e.matmul(
                            ps[:, bk, 0:n], win_b[:, kt, m * 128:(m + 1) * 128], xnT[:, bb, kt, 0:n],
                            start=(kt == 0), stop=(kt == 7)),
                            (evs + bd + [t_win]) if kt == 0 else [], inc=(kt == 7))
                    if m < 4:
                        if b < 4:
                            dst = zT[:, m, :, 1 + 64 * b:1 + 64 * b + 64]
                            src = ps[:, bk, 0:512].rearrange("p (c s) -> p s c", s=8)
                        else:
                            dst = zT[:, m, :, 257:257 + 36].rearrange("p s (q c) -> p s q c", c=9)[:, :, :, 1:9]
                            src = ps[:, bk, 0:256].rearrange("p (q c s) -> p s q c", s=8, c=8)
                        t_e = P.op("act", lambda e, dst=dst, src=src: e.activation(dst, src, AF.Copy), [last_mm, t_z0])
                    else:
                        j = m - 4
                        if b < 4:
                            dst = zP[:, j, 16 + 512 * b:16 + 512 * b + 512]
                        else:
                            dst = zP[:, j, 2064:2064 + 320].rearrange("p (q c) -> p q c", c=80)[:, :, 16:80]
                        t_e = P.op("dve", lambda e, dst=dst, bk=bk, n=n, b=b: e.tensor_copy(dst, ps_cols(bk, n, b)),
                                   [last_mm, t_zp0])
                    P.release(bk, t_e)
                    z_ev.append(t_e)
                xnT_rd[bb] = last_mm

            chainA(0)
            chainA(1)
            C1 = late_consts()
            for hb in range(9):
                transA(hb)
                if hb + 2 < 9:
                    chainA(hb + 2)
                if hb >= 2:
                    flush(16)
                if hb % 2 == 1:
                    mmA(hb // 2)
            mmA(4)
            flush(len(dq))
            LD = DTOK[n_su - 1]
            t_hist = DTOK[-1]
            dump("zT", zT[:], [128, 4, 8, NCH], z_ev)
            dump("zP", zP[:], [128, 4, PW], z_ev)

            out_tok = []
            pst = xt[:].rearrange("p a d -> p (a d)")[:, 0:2560].rearrange("p (s c) -> p s c", c=512)
            for s_ in range(5):
                L = LP if s_ == 0 else LS
                c = poolcol(s_, L - 15)
                bk, bd = P.bank()
                t_q = None
                for j in range(4):
                    t_q = P.op("pe", lambda e, c=c, j=j, bk=bk: e.transpose(
                        ps[0:15, bk, j * 128:(j + 1) * 128], zP[:, j, c:c + 15], identf[:]),
                        (z_ev + bd + [C0]) if j == 0 else [], inc=(j == 3))
                t_e = P.op("act", lambda e, s_=s_, bk=bk: e.activation(pst[0:15, s_, :], ps[0:15, bk, :], AF.Copy), [t_q])
                P.release(bk, t_e)
                out_tok.append(P.dma("sp", lambda e, s_=s_: e.dma_start(out=pool_o[s_], in_=pst[0:15, s_, :]), [t_e], slot="outs"))
            t_wc = None
            for r8 in range(8):
                t_wc = P.dma("pool", lambda e, r8=r8: e.dma_start(out=wup16[r8 * 128:(r8 + 1) * 128, :], in_=wup_d[r8 * 128:(r8 + 1) * 128, :]), [out_tok[-1], t_hist], slot="wcast")
            pool_ev = []
            t_sqp = None
            pt_free = [pt_free0, pt_free0]
            pooled_free = None
            ysq_rd = None
            psg = sb("psg", [128, 4], F32, pa)
            t_psg = P.op("dve", lambda e: e.tensor_tensor(psg[:], colv[:, 0, :], colv[:, 2, :], ALU.mult), [C0, C1])
            PM = {}
            fxs = [sb("fx%d" % j, [128, 16], F32, pa) for j in range(4)]

            def pm_dve(j):
                w = (2, 4, 8, 16)[j]
                src = zP[:, j, :]
                cur = src
                sh = 1
                k = 0
                tcur = z_ev + [t_hist]
                bufs = [pt1, pt2]
                pt_free = PM.get("pt_free", [pt_free0, pt_free0])
                while sh < w:
                    dstb = bufs[k % 2]
                    tcur = [P.op("dve", lambda e, dstb=dstb, cur=cur, sh=sh: e.tensor_tensor(
                        dstb[:, sh:PW], cur[:, sh:PW], cur[:, 0:PW - sh], ALU.add), list(tcur) + [pt_free[k % 2]])]
                    cur = dstb[:]
                    sh *= 2
                    k += 1
                tp = []
                tp.append(P.op("dve", lambda e: e.scalar_tensor_tensor(
                    pooledT[:, 0, 8:8 + LP], cur[:, 16:16 + LP], 1.0 / w, src[:, 16:16 + LP], ALU.mult, ALU.subtract),
                    tcur + [PM.get("pooled_free")]))
                tp.append(P.op("dve", lambda e: e.scalar_tensor_tensor(
                    pooledT[:, 0, 2056:2056 + 288].rearrange("p (q c) -> p q c", c=72)[:, :, 8:72],
                    cur[:, 2064:2064 + 320].rearrange("p (q c) -> p q c", c=80)[:, :, 16:80], 1.0 / w,
                    src[:, 2064:2064 + 320].rearrange("p (q c) -> p q c", c=80)[:, :, 16:80], ALU.mult, ALU.subtract), tcur))
                nf = w - 1
                fx = fxs[j]
                t_f1 = P.op("dve", lambda e: e.tensor_tensor(fx[:, 0:nf], cur[:, 16:16 + nf], icnt[:, 0:nf], ALU.mult), tcur + [t_icnt])
                t_f2 = P.op("dve", lambda e: e.tensor_tensor(pooledT[:, 0, 8:8 + nf], fx[:, 0:nf], src[:, 16:16 + nf], ALU.subtract), [t_f1] + tp)
                PM["pt_free"] = [t_f2, t_f2]
                PM["f2_%d" % j] = t_f2

            def pm_pe(j):
                t_f2 = PM["f2_%d" % j]
                last_mm = None
                evs = []
                for b in range(5):
                    rhs, n = blk_cols(pooledT, 0, b)
                    bk, bd = P.bank()
                    last_mm = P.op("pe", lambda e, bk=bk, n=n, rhs=rhs, b=b: e.matmul(
                        ps_cols(bk, n, b), wpool_b[:, j, :], rhs, start=True, stop=True), [t_f2, t_wpool] + bd)
                    dst = oblk(ypT, j, b)
                    t_e = P.op("act", lambda e, dst=dst, bk=bk, n=n: e.activation(
                        dst, ps[:, bk, 0:n], AF.Copy, scale=psg[:, j:j + 1]), [last_mm, t_psg])
                    dsq = oblk(ysq, 0, b)
                    t_sqp = P.op("act", lambda e, dsq=dsq, bk=bk, n=n: e.activation(
                        dsq, ps[:, bk, 0:n], AF.Square, scale=colv[:, 0, j:j + 1]), [last_mm, C0, C1, PM.get("ysq_rd")])
                    P.release(bk, t_sqp)
                    pool_ev.append(t_sqp)
                    evs.append(t_sqp)
                PM["pooled_free"] = last_mm
                bk, bd = P.bank()
                t_st = None
                for i in range(NTILE):
                    t_st = P.op("pe", lambda e, i=i, bk=bk: e.matmul(
                        ps[:, bk, i:i + 1], otile(ysq, 0, i), ones_b[:, 0:1], start=True, stop=True),
                        (evs + bd + [C0]) if i == 0 else [], inc=(i == NTILE - 1))
                PM["ysq_rd"] = t_st
                PM["st_%d" % j] = (t_st, bk)

            def pm_acc(j):
                t_st, bk = PM["st_%d" % j]
                if j == 0:
                    t_acc = P.op("dve", lambda e: e.tensor_copy(ssP[:], ps[:, bk, 0:NTILE]), [t_st])
                else:
                    t_acc = P.op("dve", lambda e: e.tensor_tensor(ssP[:], ssP[:], ps[:, bk, 0:NTILE], ALU.add), [t_st, PM["acc"]])
                P.release(bk, t_acc)
                PM["acc"] = t_acc

            pm_dve(0)
            for j in range(4):
                pm_pe(j)
                if j + 1 < 4:
                    pm_dve(j + 1)
                pm_acc(j)
            t_acc = PM["acc"]
            t_rpo = rstd_chain(ssP[:], rstd_po[:], tmpP[:], 1.0 / 512, [t_acc])
            t_gp = pool_ev[-1]
            POOL_DONE = [t_gp, t_rpo]
            dump("ypT", ypT[:], [128, 4, TOKP + 256], POOL_DONE)
            dump("rstd_po", rstd_po[:], [128, NTILE], POOL_DONE)
            if stop == "A":
                P.op("sp", lambda e: e.nop(), [t for t in list(out_tok) + list(dbg_o.values()) if t is not None], inc=False, force=True)
            P.emit()
        if stop == "A":
            zs.close()
            return nc
        pb = ExitStack()
        with pb:
            Kin = sb("Kin", [128, 32, 128], BF16, pb)
            BT = sb("BT", [128, 2, 16, 128], BF16, pb)
            Cw = sb("Cw", [128, 3, 32, 128], BF16, pb)
            Rr = sb("Rr", [128, 16], F32, pb); THT = sb("THT", [128, 16], F32, pb)
            h0 = sb("h0", [128, 2, 64], F32, pb)
            yT = y2T

            for r8 in range(8):
                t_wc = P.dma("pool", lambda e, r8=r8: e.dma_start(out=wdn16[r8 * 512:(r8 + 1) * 512, :], in_=wdn_d[r8 * 512:(r8 + 1) * 512, :]), slot="wcast")
            t_wc = P.dma("pool", lambda e: e.dma_start(out=wout16, in_=wout_d), slot="wcast")
            t_wc = P.dma("pool", lambda e: e.dma_start(out=wgate16, in_=wgate_d), slot="wcast")
            t_wc = P.dma("pool", lambda e: e.dma_start(out=wple16, in_=wple_d), slot="wcast")
            WCAST = t_wc
            su = ExitStack()
            with su:
                Cl = sb("Cl", [128, 2, 16, 16], F32, su)
                dtt = sb("dtt", [128, 16], F32, su); lrdt = sb("lrdt", [128, 16], F32, su); th = sb("th", [128, 16], F32, su)
                amag = sb("amag", [128, 16, 25], F32, su); aph = sb("aph", [128, 16, 25], F32, su)
                trn = sb("trn", [128, 16, 25], F32, su); frs = sb("frs", [128, 16, 25], F32, su); frc = sb("frc", [128, 16, 25], F32, su)
                apw = sb("apw", [128, 2, 16, 25], F32, su)
                s1 = sb("s1", [128, 16], F32, su); s2 = sb("s2", [128, 16], F32, su); s3 = sb("s3", [128, 16], F32, su)
                qq = sb("qq", [128, 2, 16], F32, su)
                Bb = sb("Bb", [128, 2, 16, 16], F32, su)
                tA = sb("tA", [128, 16, 8, 16], F32, su); tB = sb("tB", [128, 16, 8, 16], F32, su)
                Zr = sb("Zr", [128, 16, 8, 16], F32, su); Zi = sb("Zi", [128, 16, 8, 16], F32, su)
                Xr = sb("Xr", [128, 16, 8, 16], F32, su); nXi = sb("nXi", [128, 16, 8, 16], F32, su)
                tC = sb("tC", [128, 16, 8, 16], F32, su); mk = sb("mk", [128, 2], F32, su)
                ktmp = tA[:].rearrange("q a s h -> q (a s h)")


                def dv(fn, deps):
                    return P.op("dve", fn, deps)

                bkc, bd = P.bank()
                t_c = None
                for r in range(2):
                    for e_ in range(2):
                        for hi in range(2):
                            t_c = P.op("pe", lambda e, r=r, e_=e_, hi=hi: e.matmul(
                                ps[e_ * 64:(e_ + 1) * 64, bkc, (r * 2 + hi) * 128:(r * 2 + hi + 1) * 128], Cn[:, r, e_, hi, :], identf[:],
                                start=True, stop=True), [LD, C0] + bd)
                t_cl = dv(lambda e: e.tensor_copy(Cl[:].rearrange("q r a b -> q (r a b)"), ps[:, bkc, 0:512]), [t_c])
                P.release(bkc, t_cl)
                bkh, bd = P.bank()
                t_h = None
                for r in range(2):
                    t_h = P.op("pe", lambda e, r=r: e.transpose(ps[:, bkh, r * 64:(r + 1) * 64], h0n[:, r, :], identf[0:64, 0:64]), [LD, C0] + bd)
                t_h0 = dv(lambda e: e.tensor_copy(h0[:].rearrange("q r a -> q (r a)"), ps[:, bkh, 0:128]), [t_h])
                P.release(bkh, t_h0)

                t = P.op("act", lambda e: e.activation(dtt[:], ldt[:], AF.Exp), [LD])
                t1 = dv(lambda e: e.tensor_tensor(lrdt[:], lam[:, 0, :], dtt[:], ALU.mult), [t])
                t2 = dv(lambda e: e.tensor_tensor(th[:], lam[:, 1, :], dtt[:], ALU.mult), [t])
                sh3 = [128, 16, 25]
                t3 = dv(lambda e: e.tensor_tensor(amag[:], bcast(lrdt[:], sh3, 2), bcast(k25[:], sh3, 1), ALU.mult), [t1])
                t4 = dv(lambda e: e.tensor_tensor(aph[:], bcast(th[:], sh3, 2), bcast(k25[:], sh3, 1), ALU.mult), [t2])
                t5 = P.op("act", lambda e: e.activation(amag[:], amag[:], AF.Exp), [t3])

                def frac_sin(dst, src, shift, deps, tmp):
                    a = dv(lambda e: e.tensor_scalar(trn[:] if tmp is None else tmp, src, 1.0 / TWO_PI, shift, ALU.mult, ALU.add), deps)
                    tt = trn[:] if tmp is None else tmp
                    b_ = dv(lambda e: e.tensor_scalar(dst, tt, MAGIC, MAGIC, ALU.add, ALU.subtract), [a])
                    c_ = dv(lambda e: e.tensor_tensor(dst, tt, dst, ALU.subtract), [b_])
                    return P.op("act", lambda e: e.activation(dst, dst, AF.Sin, scale=TWO_PI), [c_])
                t6 = frac_sin(frs[:], aph[:], 0.0, [t4], None)
                t7 = frac_sin(frc[:], aph[:], 0.25, [t6], None)
                t8 = dv(lambda e: e.tensor_tensor(apw[:, 0, :, :], amag[:], frc[:], ALU.mult), [t5, t7])
                t9 = dv(lambda e: e.tensor_tensor(apw[:, 1, :, :], amag[:], frs[:], ALU.mult), [t5, t7])
                AP_ = [t8, t9]
                t10 = dv(lambda e: e.tensor_copy(Rr[:], amag[:, :, 16]), [t5])
                t11 = dv(lambda e: e.tensor_scalar(THT[:], th[:], 8.0 / TWO_PI, None, ALU.mult), [t2])
                ar1 = apw[:, 0, :, 9]; ai1 = apw[:, 1, :, 9]
                lr = lam[:, 0, :]; li = lam[:, 1, :]
                a1 = dv(lambda e: e.tensor_scalar(s1[:], ar1, -1.0, None, ALU.add), AP_)
                a2 = dv(lambda e: e.tensor_tensor(s2[:], s1[:], lr, ALU.mult), [a1])
                a3 = dv(lambda e: e.tensor_tensor(s3[:], ai1, li, ALU.mult), AP_)
                a4 = dv(lambda e: e.tensor_tensor(qq[:, 0, :], s2[:], s3[:], ALU.add), [a2, a3])
                a5 = dv(lambda e: e.tensor_tensor(s2[:], ai1, lr, ALU.mult), [a4])
                a6 = dv(lambda e: e.tensor_tensor(s3[:], s1[:], li, ALU.mult), [a4])
                a7 = dv(lambda e: e.tensor_tensor(qq[:, 1, :], s2[:], s3[:], ALU.subtract), [a5, a6])
                a8 = dv(lambda e: e.tensor_tensor(s2[:], lr, lr, ALU.mult), [a7])
                a9 = dv(lambda e: e.tensor_tensor(s3[:], li, li, ALU.mult), [a7])
                a10 = dv(lambda e: e.tensor_tensor(s2[:], s2[:], s3[:], ALU.add), [a8, a9])
                a11 = dv(lambda e: e.reciprocal(s2[:], s2[:]), [a10])
                sh2 = [128, 2, 16]
                a12 = dv(lambda e: e.tensor_tensor(qq[:], qq[:], bcast(s2[:], sh2, 1), ALU.mult), [a11])
                shb = [128, 16, 16]
                qr = bcast(qq[:, 0, :], shb, 2); qi = bcast(qq[:, 1, :], shb, 2)
                b1 = dv(lambda e: e.tensor_tensor(tA[:, :, 0, :], qr, Bn[:, 0, :, :], ALU.mult), [a12])
                b2 = dv(lambda e: e.tensor_tensor(tB[:, :, 0, :], qi, Bn[:, 1, :, :], ALU.mult), [a12])
                b3 = dv(lambda e: e.tensor_tensor(Bb[:, 0, :, :], tA[:, :, 0, :], tB[:, :, 0, :], ALU.subtract), [b1, b2])
                b4 = dv(lambda e: e.tensor_tensor(tA[:, :, 0, :], qr, Bn[:, 1, :, :], ALU.mult), [b3])
                b5 = dv(lambda e: e.tensor_tensor(tB[:, :, 0, :], qi, Bn[:, 0, :, :], ALU.mult), [b3])
                b6 = dv(lambda e: e.tensor_tensor(Bb[:, 1, :, :], tA[:, :, 0, :], tB[:, :, 0, :], ALU.add), [b4, b5])
                BB = [b3, b6]
                sh4 = [128, 16, 8, 16]

                def cmul(dr, di, pk0, M, deps, neg_i=False):
                    pr_ = bcast(apw[:, 0, :, pk0:pk0 + 8], sh4, 3); pi_ = bcast(apw[:, 1, :, pk0:pk0 + 8], sh4, 3)
                    mr = bcast(M[:, 0, :, :], sh4, 2); mi = bcast(M[:, 1, :, :], sh4, 2)
                    c1 = dv(lambda e: e.tensor_tensor(tA[:], pr_, mr, ALU.mult), deps)
                    c2 = dv(lambda e: e.tensor_tensor(tB[:], pi_, mi, ALU.mult), deps)
                    c3 = dv(lambda e: e.tensor_tensor(dr, tA[:], tB[:], ALU.subtract), [c1, c2])
                    c4 = dv(lambda e: e.tensor_tensor(tA[:], pr_, mi, ALU.mult), [c3])
                    c5 = dv(lambda e: e.tensor_tensor(tB[:], pi_, mr, ALU.mult), [c3])
                    if neg_i:
                        c6 = dv(lambda e: e.scalar_tensor_tensor(di, tA[:], -1.0, tB[:], ALU.mult, ALU.subtract), [c4, c5])
                    else:
                        c6 = dv(lambda e: e.tensor_tensor(di, tA[:], tB[:], ALU.add), [c4, c5])
                    return c6
                tz = cmul(Zr[:], Zi[:], 17, Bb, AP_ + BB)
                tb_ = tz
                t_bt = tb_
                for r, Zs in enumerate((Zr, Zi)):
                    for p4 in range(4):
                        bk, bd = P.bank()
                        t_mm = None
                        for pp in range(4):
                            pr = p4 * 4 + pp
                            t_mm = P.op("pe", lambda e, bk=bk, pp=pp, pr=pr, Zs=Zs: e.transpose(
                                ps[:, bk, pp * 128:(pp + 1) * 128], Zs[:, pr, :, :].rearrange("q s h -> q (s h)"), identf[:]),
                                ([tb_] + bd) if pp == 0 else [], inc=(pp == 3))
                        t_bt = P.op("act", lambda e, bk=bk, r=r, p4=p4: e.activation(
                            BT[:, r, p4 * 4:(p4 + 1) * 4, :], ps[:, bk, :].rearrange("p (g m) -> p g m", m=128), AF.Copy), [t_mm])
                        P.release(bk, t_bt)
                tx = cmul(Xr[:], nXi[:], 0, Cl, [tz, t_cl], neg_i=True)
                m1 = dv(lambda e: e.memset(mk[:], 0.0), [])
                m2 = dv(lambda e: e.memset(mk[0:64, 0:1], 1.0), [m1])
                m3 = dv(lambda e: e.memset(mk[64:128, 1:2], 1.0), [m1])
                MK = [m2, m3]
                t_k = tx
                par_rd = None
                for e_ in range(2):
                    x1_ = dv(lambda e, e_=e_: e.tensor_scalar(tB[:].rearrange("q a s h -> q (a s h)"), Xr[:].rearrange("q a s h -> q (a s h)"),
                                                              mk[:, e_:e_ + 1], None, ALU.mult), [tx, par_rd] + MK)
                    x2_ = dv(lambda e, e_=e_: e.tensor_scalar(tC[:].rearrange("q a s h -> q (a s h)"), nXi[:].rearrange("q a s h -> q (a s h)"),
                                                              mk[:, e_:e_ + 1], None, ALU.mult), [tx, par_rd] + MK)
                    for p4 in range(4):
                        bk, bd = P.bank()
                        t_mm = None
                        for pp in range(4):
                            pr = p4 * 4 + pp
                            P.op("pe", lambda e, bk=bk, pp=pp, pr=pr: e.matmul(
                                ps[:, bk, pp * 128:(pp + 1) * 128], Zr[:, pr, :, :].rearrange("q s h -> q (s h)"),
                                tB[:, pr, :, :].rearrange("q s h -> q (s h)"), start=True, stop=False),
                                ([x1_, x2_] + bd) if pp == 0 else [], inc=False)
                            t_mm = P.op("pe", lambda e, bk=bk, pp=pp, pr=pr: e.matmul(
                                ps[:, bk, pp * 128:(pp + 1) * 128], Zi[:, pr, :, :].rearrange("q s h -> q (s h)"),
                                tC[:, pr, :, :].rearrange("q s h -> q (s h)"), start=False, stop=True), [], inc=(pp == 3))
                        t_k1 = dv(lambda e, bk=bk: e.tensor_tensor(
                            ktmp[:, 0:512].rearrange("p (g m) -> p g m", m=128), ps[:, bk, :].rearrange("p (g m) -> p g m", m=128),
                            bcast(maskf[:], [128, 4, 128], 1), ALU.mult), [t_mm, LD, t_k])
                        P.release(bk, t_k1)
                        for pp in range(4):
                            g = 2 * (p4 * 4 + pp) + e_
                            t_k = dv(lambda e, g=g, pp=pp: e.scalar_tensor_tensor(
                                Kin[:, g, :], identf[:], dcol[:, g:g + 1], ktmp[:, pp * 128:(pp + 1) * 128], ALU.mult, ALU.add), [t_k1, C0])
                    par_rd = t_mm
                tc_ = cmul(Xr[:], nXi[:], 9, Cl, [t_bt, t_k], neg_i=True)
                cws = []
                for e_ in range(2):
                    xrf = Xr[:].rearrange("q a s h -> q a (s h)"); nxf = nXi[:].rearrange("q a s h -> q a (s h)")
                    cws.append(dv(lambda e, e_=e_, xrf=xrf: e.tensor_scalar(Cw[:, 0, e_:32:2, :], xrf, mk[:, e_:e_ + 1], None, ALU.mult), [tc_] + MK))
                    cws.append(dv(lambda e, e_=e_, xrf=xrf: e.tensor_scalar(Cw[:, 1, e_:32:2, :], xrf, mk[:, e_:e_ + 1], -1.0, ALU.mult, ALU.mult), [tc_] + MK))
                    cws.append(dv(lambda e, e_=e_, nxf=nxf: e.tensor_scalar(Cw[:, 2, e_:32:2, :], nxf, mk[:, e_:e_ + 1], None, ALU.mult), [tc_] + MK))
                c1 = c2 = c3 = cws[-1]
                SETUP = [c1, c2, c3, t_bt, t_k, t10, t11, t_h0]
                dump("Kin", Kin[:], [128, 32, 128], SETUP)
                dump("BT", BT[:], [128, 2, 16, 128], SETUP)
                dump("Cw", Cw[:], [128, 3, 32, 128], SETUP)
                dump("apw", apw[:], [128, 2, 16, 25], SETUP)
                dump("h0", h0[:], [128, 2, 64], SETUP)
                if stop == "SU":
                    P.op("sp", lambda e: e.nop(), list(out_tok) + list(dbg_o.values()), inc=False)
                P.emit()
            if stop == "SU":
                pb.close(); zs.close()
                return nc

            sc = ExitStack()
            with sc:
                NB = 4
                E_b = sb("E_b", [128, 64, 128], BF16, sc)
                jtab = sb("jtab", [128, NCH], F32, sc); m0tab = sb("m0tab", [128, NCH], F32, sc)
                t_E = P.dma("sp", lambda e: e.dma_start(out=E_b[:], in_=E_d.rearrange("p (a m) -> p a m", m=128)), slot="s5c0")
                t_j = P.dma("sp", lambda e: e.dma_start(out=jtab[:], in_=j_d), slot="s5c1")
                t_m0 = P.dma("sp", lambda e: e.dma_start(out=m0tab[:], in_=m0_d), slot="s5c2")

                U = sb("U", [128, 2, 8, NCH], BF16, sc)
                ygl = sb("ygl", [128, 8, NCH], BF16, sc)
                cs = sb("cs", [128, 2, NB, NCH], F32, sc)
                Rm = sb("Rm", [128, NB, NCH], F32, sc)
                trs = sb("trs", [128, NB, NCH], F32, sc); tr2 = sb("tr2", [128, NB, NCH], F32, sc)
                W = sb("W", [128, 2, NB, NCH], F32, sc)
                G = sb("G", [128, 2, NB, NCH], F32, sc)
                w1 = trs; w2 = tr2
                PP = sb("PP", [128, 4, NB, NCH], BF16, sc)
                e0 = sb("e0", [128, 2, NB, NSQ], F32, sc); e1 = sb("e1", [128, NB, NSQ], F32, sc)
                ss5 = sb("ss5", [128, NTILE], F32, sc)
                hl1 = sb("hl1", [128, NB, 5], F32, sc); hl2 = sb("hl2", [128, NB, 5], F32, sc)

                def dv(fn, deps):
                    return P.op("dve", fn, deps)
                shn = [128, NB, NCH]
                prev_batch = []
                yT_w = []
                UEV = {}
                y_done = {}

                def shuf_in(bt):
                    ub = bt % 2
                    free = [y_done.get(bt - 2)]
                    evs = []
                    for gi in range(8):
                        bk, bd = P.bank()
                        t_mm = None
                        for s_ in range(8):
                            t_mm = P.op("pe", lambda e, bk=bk, gi=gi, s_=s_, bt=bt: e.matmul(
                                ps[:, bk, 0:NCH], E_b[:, gi * 8 + s_, :], zT[:, bt, s_, :], start=(s_ == 0), stop=(s_ == 7)),
                                (z_ev + [t_E] + bd) if s_ == 0 else [], inc=(s_ == 7))
                        t_e = P.op("act", lambda e, bk=bk, gi=gi, ub=ub: e.activation(U[:, ub, gi, :], ps[:, bk, 0:NCH], AF.Copy), [t_mm] + free)
                        P.release(bk, t_e)
                        evs.append(t_e)
                    return evs
                B = {}

                def prevd(bt, *keys):
                    out = []
                    if bt - 1 in B:
                        for k in keys:
                            v = B[bt - 1].get(k)
                            if v is None:
                                continue
                            out += v if isinstance(v, list) else [v]
                    return out

                def s1a(bt):
                    p0 = bt * NB
                    ub = bt % 2
                    D_ = B.setdefault(bt, {})
                    dprev = prevd(bt, "pd", "fin", "w_done", "E0", "g_done", "inj")
                    a = dv(lambda e, p0=p0: e.tensor_tensor(trs[:], bcast(THT[:, p0:p0 + NB], shn, 2), bcast(jtab[:], shn, 1), ALU.mult),
                           SETUP + [t_j] + dprev)
                    b_ = dv(lambda e: e.tensor_scalar(tr2[:], trs[:], MAGIC, MAGIC, ALU.add, ALU.subtract), [a])
                    c_ = dv(lambda e: e.tensor_tensor(tr2[:], trs[:], tr2[:], ALU.subtract), [b_])
                    t_sin = P.op("act", lambda e: e.activation(cs[:, 1, :, :], tr2[:], AF.Sin, scale=TWO_PI), [c_] + dprev)
                    a2 = dv(lambda e: e.tensor_scalar(trs[:], trs[:], 0.25, None, ALU.add), [c_])
                    b2 = dv(lambda e: e.tensor_scalar(tr2[:], trs[:], MAGIC, MAGIC, ALU.add, ALU.subtract), [a2, t_sin])
                    c2_ = dv(lambda e: e.tensor_tensor(tr2[:], trs[:], tr2[:], ALU.subtract), [b2])
                    t_cos = P.op("act", lambda e: e.activation(cs[:, 0, :, :], tr2[:], AF.Sin, scale=TWO_PI), [c2_])
                    t_rm = dv(lambda e, p0=p0: e.tensor_tensor(Rm[:], bcast(Rr[:, p0:p0 + NB], shn, 2), bcast(m0tab[:], shn, 1), ALU.mult),
                              SETUP + [t_m0] + dprev)
                    TAB = [t_sin, t_cos, t_rm]
                    hre = h0[:, 0, :].rearrange("q (s a) -> q a s", a=16)[:, p0:p0 + NB, :]
                    him = h0[:, 1, :].rearrange("q (s a) -> q a s", a=16)[:, p0:p0 + NB, :]
                    shq = [128, NB, NSQ]
                    c1b = bcast(cs[:, 0, :, 2], shq, 2); s1b_ = bcast(cs[:, 1, :, 2], shq, 2)
                    x1 = dv(lambda e: e.tensor_tensor(e0[:, 0, :, :], c1b, hre, ALU.mult), TAB + dprev)
                    x2 = dv(lambda e: e.tensor_tensor(e1[:], s1b_, him, ALU.mult), TAB + dprev)
                    x3 = dv(lambda e: e.tensor_tensor(e0[:, 0, :, :], e0[:, 0, :, :], e1[:], ALU.subtract), [x1, x2])
                    x4 = dv(lambda e: e.tensor_tensor(e0[:, 1, :, :], s1b_, hre, ALU.mult), TAB + dprev)
                    x5 = dv(lambda e: e.tensor_tensor(e1[:], c1b, him, ALU.mult), [x3])
                    x6 = dv(lambda e: e.tensor_tensor(e0[:, 1, :, :], e0[:, 1, :, :], e1[:], ALU.add), [x4, x5])
                    E0 = [x3, x6]
                    u_ev = UEV[bt]
                    w_done = []
                    for pp in range(NB):
                        pr = p0 + pp
                        bk2, bd = P.bank2()
                        t_mm = None
                        for r in range(2):
                            for e_ in range(2):
                                t_mm = P.op("pe", lambda e, bk2=bk2, r=r, e_=e_, pr=pr, pp=pp: e.matmul(
                                    ps[e_ * 64:(e_ + 1) * 64, bk2 + r, 0:NCH], BT[:, r, pr, e_ * 64:(e_ + 1) * 64], U[:, ub, 2 * pp + e_, :],
                                    start=True, stop=True), ([u_ev[2 * pp], u_ev[2 * pp + 1]] + SETUP + bd) if (r == 0 and e_ == 0) else [],
                                    inc=(r == 1 and e_ == 1))
                        sre = ps[:, bk2, 0:NCH]; sim = ps[:, bk2 + 1, 0:NCH]
                        co = cs[:, 0, pp, :]; si = cs[:, 1, pp, :]
                        r1 = dv(lambda e, pp=pp, sre=sre, co=co: e.tensor_tensor(w1[:, pp, :], co, sre, ALU.mult), [t_mm] + TAB)
                        r2 = dv(lambda e, pp=pp, sim=sim, si=si: e.tensor_tensor(w2[:, pp, :], si, sim, ALU.mult), [t_mm] + TAB)
                        r3 = dv(lambda e, pp=pp: e.tensor_tensor(W[:, 0, pp, :], w1[:, pp, :], w2[:, pp, :], ALU.add), [r1, r2] + dprev)
                        r4 = dv(lambda e, pp=pp, sim=sim, co=co: e.tensor_tensor(w1[:, pp, :], co, sim, ALU.mult), [r3])
                        r5 = dv(lambda e, pp=pp, sre=sre, si=si: e.tensor_tensor(w2[:, pp, :], si, sre, ALU.mult), [r3])
                        P.release(bk2, r5, 2)
                        r6 = dv(lambda e, pp=pp: e.tensor_tensor(W[:, 1, pp, :], w1[:, pp, :], w2[:, pp, :], ALU.subtract), [r4, r5])
                        w_done += [r3, r6]
                    icols = slice(257, 257 + 36, 9)
                    i1 = dv(lambda e: e.tensor_copy(W[:, 0, :, icols], e0[:, 0, :, :]), w_done + E0)
                    i2 = dv(lambda e: e.tensor_copy(W[:, 1, :, icols], e0[:, 1, :, :]), w_done + E0)
                    g_done = []
                    for r in range(2):
                        for pp in range(NB):
                            g_done.append(dv(lambda e, r=r, pp=pp: e.tensor_tensor_scan(
                                G[:, r, pp, :], Rm[:, pp, :], W[:, r, pp, :], 0.0, ALU.mult, ALU.add), [i1, i2] + TAB + dprev))
                    D_.update(TAB=TAB, E0=E0, w_done=w_done, inj=[i1, i2], g_done=g_done, u_ev=u_ev)

                def s1b(bt):
                    p0 = bt * NB
                    D_ = B[bt]
                    g_done = D_["g_done"]
                    yprev = [y_done[bt - 1]] if (bt - 1) in y_done else []
                    pd = []
                    pd.append(dv(lambda e: e.tensor_tensor(PP[:, 0, :, :], cs[:, 0, :, :], G[:, 0, :, :], ALU.mult), g_done + yprev))
                    pd.append(dv(lambda e: e.tensor_tensor(PP[:, 1, :, :], cs[:, 1, :, :], G[:, 1, :, :], ALU.mult), g_done + yprev))
                    pd.append(dv(lambda e: e.tensor_tensor(PP[:, 2, :, :], cs[:, 1, :, :], G[:, 0, :, :], ALU.mult), g_done + yprev))
                    pd.append(dv(lambda e: e.tensor_tensor(PP[:, 3, :, :], cs[:, 0, :, :], G[:, 1, :, :], ALU.mult), g_done + yprev))
                    lsl = slice(256, NCH, 9)
                    fprev = prevd(bt, "fin")
                    f1 = dv(lambda e: e.tensor_tensor(hl1[:], cs[:, 0, :, lsl], G[:, 0, :, lsl], ALU.mult), g_done + fprev)
                    f2 = dv(lambda e: e.tensor_tensor(hl2[:], cs[:, 1, :, lsl], G[:, 1, :, lsl], ALU.mult), g_done + fprev)
                    f3 = dv(lambda e, p0=p0: e.tensor_tensor(HL[:, 0, :, p0:p0 + NB].rearrange("q s a -> q a s"), hl1[:], hl2[:], ALU.subtract), [f1, f2])
                    f4 = dv(lambda e: e.tensor_tensor(hl1[:], cs[:, 1, :, lsl], G[:, 0, :, lsl], ALU.mult), [f3])
                    f5 = dv(lambda e: e.tensor_tensor(hl2[:], cs[:, 0, :, lsl], G[:, 1, :, lsl], ALU.mult), [f3])
                    f6 = dv(lambda e, p0=p0: e.tensor_tensor(HL[:, 1, :, p0:p0 + NB].rearrange("q s a -> q a s"), hl1[:], hl2[:], ALU.add), [f4, f5])
                    D_.update(pd=pd, fin=[f6])

                def s2(bt):
                    p0 = bt * NB
                    ub = bt % 2
                    D_ = B[bt]
                    pd = D_["pd"]
                    soprev = prevd(bt, "t_so")
                    y_ev = []
                    t_mm = None
                    for gi in range(8):
                        pp = gi // 2
                        bk, bd = P.bank()
                        P.op("pe", lambda e, bk=bk, gi=gi: e.matmul(
                            ps[:, bk, 1:NCH], Kin[:, bt * 8 + gi, :], U[:, ub, gi, 1:NCH], start=True, stop=False), pd + bd + SETUP, inc=False)
                        for k, (ci, pi) in enumerate(((0, 0), (1, 1), (2, 2), (2, 3))):
                            t_mm = P.op("pe", lambda e, bk=bk, ci=ci, pi=pi, gg_=bt * 8 + gi, pp=pp, k=k: e.matmul(
                                ps[:, bk, 1:NCH], Cw[:, ci, gg_, :], PP[:, pi, pp, 0:NCH - 1], start=False, stop=(k == 3)), [], inc=(k == 3))
                        t_e = P.op("act", lambda e, bk=bk, gi=gi: e.activation(ygl[:, gi, 1:NCH], ps[:, bk, 1:NCH], AF.Gelu_apprx_tanh), [t_mm] + soprev)
                        P.release(bk, t_e)
                        y_ev.append(t_e)
                    y_done[bt] = t_mm
                    last = []
                    for t_ in range(8):
                        bk, bd = P.bank()
                        for gi in range(8):
                            t_mm = P.op("pe", lambda e, bk=bk, gi=gi, t_=t_: e.matmul(
                                ps[:, bk, 1:NCH], E_b[:, t_ * 8 + gi, :], ygl[:, gi, 1:NCH], start=(gi == 0), stop=(gi == 7)),
                                (y_ev + bd) if gi == 0 else [], inc=(gi == 7))
                        t_e = P.op("act", lambda e, bk=bk, t_=t_: e.activation(yT[:, bt, 8 + t_:TOKP:8], ps[:, bk, 1:NCH], AF.Copy), [t_mm])
                        P.release(bk, t_e)
                        last.append(t_e)
                    D_.update(y_ev=y_ev, last=last, t_so=[t_mm])
                    return last

                UEV[0] = shuf_in(0)
                s1a(0)
                s1b(0)
                UEV[1] = shuf_in(1)
                for bt in range(4):
                    if bt + 1 < 4:
                        s1a(bt + 1)
                    yT_w += s2(bt)
                    if bt + 1 < 4:
                        s1b(bt + 1)
                    if bt + 2 < 4:
                        UEV[bt + 2] = shuf_in(bt + 2)
                prev_batch = []
                for bt in range(4):
                    for k in ("last", "fin", "t_so", "pd", "g_done", "w_done"):
                        prev_batch += B[bt][k]
                dump("yT", yT[:], [128, 4, TOKP + 256], yT_w)
                dump("HL", HL[:], [128, 2, 5, 16], prev_batch)
                hlo = trs[:].rearrange("p a c -> p (a c)")[:, 0:256].rearrange("p (r m) -> p r m", m=128)
                bkh, bd = P.bank()
                t_h = None
                for r in range(2):
                    t_h = P.op("pe", lambda e, r=r: e.transpose(ps[0:80, bkh, r * 128:(r + 1) * 128], HL[:, r, :, :].rearrange("q s a -> q (s a)"), identf[:]),
                               prev_batch + bd)
                t_ho = P.op("act", lambda e: e.activation(hlo[0:80, :, :], ps[0:80, bkh, 0:256].rearrange("p (r m) -> p r m", m=128), AF.Copy), [t_h])
                P.release(bkh, t_ho)
                out_tok.append(P.dma("sp", lambda e: e.dma_start(out=hre_o, in_=hlo[0:80, 0, :]), [t_ho], slot="outs"))
                out_tok.append(P.dma("sp", lambda e: e.dma_start(out=him_o, in_=hlo[0:80, 1, :]), [t_ho], slot="outs"))

                wglu_b = W[:].rearrange("p a b c -> p (a b c)").bitcast(BF16)[:, 0:2048].rearrange("p (k n) -> p k n", n=512)
                t_wglu = P.dma("pool", lambda e: e.dma_start(out=wglu_b, in_=wglu_d.rearrange("(kt p) n -> p kt n", p=128)), prev_batch, slot="wglu")
                sg = PP[:].rearrange("p a b c -> p (a b c)")[:, 0:4096].rearrange("p (u m n) -> p u m n", m=4, n=512)
                ysq5 = G[:].rearrange("p a b c -> p (a b c)").bitcast(BF16)[:, 0:4096].rearrange("p (u m n) -> p u m n", m=4, n=512)
                glu_ev = []
                bks, bds = P.bank()
                G1 = {}
                TST = {}
                SGEV = {}

                def glu_gate(b):
                    n = 512 if b < 4 else 256
                    u = b % 2
                    free = G1.get(b - 2, [])
                    evs = []
                    for m in range(4):
                        bk, bd = P.bank()
                        if bk == bks:
                            bk, bd = P.bank()
                        t_mm = None
                        for kt in range(4):
                            rhs, _ = blk_cols(yT, kt, b)
                            t_mm = P.op("pe", lambda e, bk=bk, m=m, kt=kt, rhs=rhs, n=n, b=b: e.matmul(
                                ps_cols(bk, n, b), wglu_b[:, kt, m * 128:(m + 1) * 128], rhs, start=(kt == 0), stop=(kt == 3)),
                                (yT_w + [t_wglu] + bd) if kt == 0 else [], inc=(kt == 3))
                        t_e = P.op("act", lambda e, bk=bk, m=m, n=n, u=u: e.activation(sg[:, u, m, 0:n], ps[:, bk, 0:n], AF.Sigmoid),
                                   [t_mm] + free + prev_batch)
                        P.release(bk, t_e)
                        evs.append(t_e)
                    SGEV[b] = evs

                def glu_dve(b):
                    n = 512 if b < 4 else 256
                    u = b % 2
                    blk_ev = []
                    g1s = []
                    tst_prev = [TST[b - 2]] if (b - 2) in TST else []
                    for m in range(4):
                        src, _ = blk_cols(yT, m, b)
                        dst = oblk(y2T, m, b)
                        dst3 = oblk(y2T, m, b, three=True)
                        sgv = sg[:, u, m, 0:n] if b < 4 else sg[:, u, m, 0:256].rearrange("p (q c) -> p q c", c=64)
                        g1 = dv(lambda e, src=src, dst3=dst3, sgv=sgv: e.tensor_tensor(dst3, src, sgv, ALU.mult), SGEV[b])
                        g2 = dv(lambda e, dst=dst, m=m, n=n, u=u: e.tensor_tensor(ysq5[:, u, m, 0:n], dst, dst, ALU.mult), [g1] + tst_prev + prev_batch)
                        g3 = P.op("act", lambda e, dst=dst, m=m: e.activation(dst, dst, AF.Copy, scale=colv[:, 1, m:m + 1]), [g2, C0, C1])
                        glu_ev.append(g3)
                        blk_ev.append(g2)
                        g1s.append(g1)
                    G1[b] = g1s
                    return blk_ev

                def glu_stats(b, blk_ev, tile0):
                    n = 512 if b < 4 else 256
                    nt = n // 128
                    u = b % 2
                    t_st = None
                    for ti in range(nt):
                        i = tile0 + ti
                        for j in range(4):
                            t_st = P.op("pe", lambda e, i=i, j=j, ti=ti, u=u: e.matmul(
                                ps[:, bks, i:i + 1], ysq5[:, u, j, ti * 128:(ti + 1) * 128], ones_b[:, 0:1], start=(j == 0), stop=(j == 3)),
                                (blk_ev + bds + [C0]) if (ti == 0 and j == 0) else [], inc=(ti == nt - 1 and j == 3))
                    TST[b] = t_st
                    return t_st

                glu_gate(0)
                tile0 = 0
                t_st = None
                for b in range(5):
                    if b + 1 < 5:
                        glu_gate(b + 1)
                    bev = glu_dve(b)
                    t_st = glu_stats(b, bev, tile0)
                    tile0 += 4 if b < 4 else 2
                t_rs5 = rstd_chain(ps[:, bks, 0:NTILE], rstd_s5[:], ss5[:], 1.0 / 512, [t_st])
                P.release(bks, t_rs5)
                S5_DONE = glu_ev + [t_rs5]
                dump("y2T", y2T[:], [128, 4, TOKP + 256], S5_DONE)
                dump("rstd_s5", rstd_s5[:], [128, NTILE], S5_DONE)
                if stop == "B":
                    P.op("sp", lambda e: e.nop(), list(out_tok) + list(dbg_o.values()), inc=False)
                P.emit()
            if stop == "B":
                pb.close(); zs.close()
                return nc
        zs.close()
        sis.close()
        pc = ExitStack()
        with pc:
            wout_b = sb("wout_b", [128, 8, D], BF16, pc); wgate_b = sb("wgate_b", [128, 8, D], BF16, pc)
            wple_b = sb("wple_b", [128, 2, D], BF16, pc)
            wres = []
            for kt in range(0, 8, 4):
                wres.append(P.dma("sp", lambda e, kt=kt: e.dma_start(
                    out=wout_b[:, kt:kt + 4, :], in_=wout16.rearrange("(kt p) n -> p kt n", p=128)[:, kt:kt + 4, :]), [WCAST], slot="wout"))
            WOUT = wres[-1]
            xr = sb("xr", [128, 2, 4, D], F32, pc)
            ssC = sb("ssC", [128, 4 * NTILE], F32, pc); tmpC = sb("tmpC", [128, 4 * NTILE], F32, pc); rsC = sb("rsC", [128, 4 * NTILE], F32, pc)
            xnc = sb("xnc", [128, 2, D], BF16, pc)
            xnT2 = sb("xnT2", [128, 8, 512], BF16, pc)
            xnT3 = sb("xnT3", [128, 2, 8, 128], BF16, pc)
            aT = sb("aT", [128, 32, 512], BF16, pc)
            wupb = sb("wupb", [128, 3, 8, 256], BF16, pc)
            wdnb = sb("wdnb", [128, 3, 4, 512], BF16, pc)
            pb16 = sb("pb16", [128, 4, 256], BF16, pc)
            pT = sb("pT", [128, 2, 512], BF16, pc)
            sgc = sb("sgc", [128, 2, D], F32, pc)
            t_ssc = P.op("dve", lambda e: e.memset(ssC[:], 0.0))
            wup_r = wup16.rearrange("(kt p) n -> p kt n", p=128)
            wdn_r = wdn16.rearrange("(f p) n -> p f n", p=128)

            class St:
                pass
            st = St()
            st.wup_rd = [None] * 3; st.wdn_rd = [None] * 3
            st.nup = 0; st.ndn = 0
            st.xr_free = [[None] * 4, [None] * 4]
            st.xnc_rd = [None] * 2
            st.xnc_sq = [None] * 2
            st.xnT2_rd = None; st.xnT3_rd = [None, None]; st.aT_rd = None; st.pT_rd = None; st.pb_rd = [None] * 4
            st.sgc_rd = [None, None]
            st.wq = []

            blocks = []
            t0_ = 0
            for b in range(5):
                nt = 4 if b < 4 else 2
                blocks.append((b, nt, list(range(t0_, t0_ + nt))))
                t0_ += nt

            def take(bk):
                return list(P.bank_tok[bk])

            def take2(bk):
                return list(P.bank_tok[bk]) + list(P.bank_tok[bk + 1])

            pieces = []
            for (b, nt, tiles) in blocks:
                for pc_ in range(16):
                    pieces.append(("up", b, pc_))
                for h in range(2):
                    for pc_ in range(8):
                        pieces.append(("dn", b, h, pc_))
            st.pidx = 0
            st.ptok = {}

            def issue_piece():
                if st.pidx >= len(pieces):
                    return
                pz = pieces[st.pidx]
                st.pidx += 1
                if pz[0] == "up":
                    u = st.nup % 3
                    st.nup += 1
                    pc_ = pz[2]
                    t = P.dma("pool", lambda e, u=u, pc_=pc_: e.dma_start(out=wupb[:, u, :, :], in_=wup_r[:, :, pc_ * 256:(pc_ + 1) * 256]),
                              [st.wup_rd[u], WCAST], slot="wup%d" % u)
                    st.ptok[pz] = (t, u)
                else:
                    u = st.ndn % 3
                    st.ndn += 1
                    h, pc_ = pz[2], pz[3]
                    t = P.dma("pool", lambda e, u=u, pc_=pc_, h=h: e.dma_start(
                        out=wdnb[:, u, :, :], in_=wdn_r[:, pc_ * 4:(pc_ + 1) * 4, h * 512:(h + 1) * 512]), [st.wdn_rd[u], WCAST], slot="wdn%d" % u)
                    st.ptok[pz] = (t, u)

            def load_tile(blk, ti):
                b, nt, tiles = blk
                i = tiles[ti]
                return P.dma("sp", lambda e, ti=ti, i=i, b=b: e.dma_start(out=xr[:, b % 2, ti, :], in_=x_d[i * 128:(i + 1) * 128, :]),
                             [st.xr_free[b % 2][ti]], slot="xr%d_%d" % (b % 2, ti))

            def load_block(blk):
                return [load_tile(blk, ti) for ti in range(blk[1])]

            def load_p(blk):
                b, nt, tiles = blk
                tp = []
                for ti, i in enumerate(tiles):
                    tp.append(P.dma("pool", lambda e, ti=ti, i=i: e.dma_start(out=pb16[:, ti, :], in_=p_d[i * 128:(i + 1) * 128, :]),
                                    [st.pb_rd[ti]], slot="pb%d" % ti))
                return tp

            def p_transposes(blk, tp, bk):
                b, nt, tiles = blk
                evs = []
                for ti, i in enumerate(tiles):
                    bd = take(bk)
                    t_tr = None
                    for j in range(2):
                        t_tr = P.op("pe", lambda e, j=j, ti=ti: e.transpose(
                            psb[:, bk, j * 128:(j + 1) * 128], pb16[:, ti, j * 128:(j + 1) * 128], identb[:]),
                            ([tp[ti], C0] + bd) if j == 0 else [], inc=(j == 1))
                    t_e = P.op("act", lambda e, ti=ti: e.activation(
                        pT[:, :, ti * 128:(ti + 1) * 128], psb[:, bk, 0:256].rearrange("p (k c) -> p k c", c=128), AF.Copy), [t_tr, st.pT_rd])
                    P.release(bk, t_e)
                    st.pb_rd[ti] = t_tr
                    evs.append(t_e)
                return evs

            def s1a(blk, ti, tx, bk2):
                b, nt, tiles = blk
                i = tiles[ti]
                bd = take2(bk2)
                t_a = None
                for h in range(2):
                    for kt in range(4):
                        t_a = P.op("pe", lambda e, h=h, kt=kt, i=i: e.matmul(
                            ps[:, bk2 + h, :], otile(y2T, kt, i), wout_b[:, kt, h * 512:(h + 1) * 512],
                            start=(kt == 0), stop=(kt == 3)), (S5_DONE + POOL_DONE + [WOUT] + bd) if (h == 0 and kt == 0) else [],
                            inc=(h == 1 and kt == 3))
                pa_ = ps[:, bk2:bk2 + 2, :].rearrange("p a c -> p (a c)")
                t1 = P.op("dve", lambda e, ti=ti, i=i, b=b: e.scalar_tensor_tensor(
                    xr[:, b % 2, ti, :], pa_, rstd_s5[:, i:i + 1], xr[:, b % 2, ti, :], ALU.mult, ALU.add), [t_a, tx[ti]])
                P.release(bk2, t1, 2)
                return t1

            def s1b(blk, ti, t1, bk2):
                b, nt, tiles = blk
                i = tiles[ti]
                bd = take2(bk2)
                t_b = None
                for h in range(2):
                    for kt in range(4):
                        t_b = P.op("pe", lambda e, h=h, kt=kt, i=i: e.matmul(
                            ps[:, bk2 + h, :], otile(ypT, kt, i), wout_b[:, 4 + kt, h * 512:(h + 1) * 512],
                            start=(kt == 0), stop=(kt == 3)), bd if (h == 0 and kt == 0) else [], inc=(h == 1 and kt == 3))
                pb_ = ps[:, bk2:bk2 + 2, :].rearrange("p a c -> p (a c)")
                t2 = P.op("dve", lambda e, ti=ti, i=i, b=b: e.scalar_tensor_tensor(
                    xr[:, b % 2, ti, :], pb_, rstd_po[:, i:i + 1], xr[:, b % 2, ti, :], ALU.mult, ALU.add), [t_b, t1])
                P.release(bk2, t2, 2)
                return t2

            def norm_a1(x_ap, k, xn_ap, deps, xn_free):
                t_sq = P.op("dve", lambda e: e.scalar_tensor_tensor(xn_ap, x_ap, 1.0, x_ap, ALU.mult, ALU.mult, accum_out=ssC[:, k:k + 1]),
                            deps + [t_ssc] + list(xn_free))
                return P.op("dve", lambda e: e.tensor_scalar(tmpC[:, k:k + 1], ssC[:, k:k + 1], 1.0 / D, EPS, ALU.mult, ALU.add), [t_sq])

            def norm_a2(x_ap, gk, k, xn_ap, t1):
                t2 = P.op("act", lambda e: e.activation(tmpC[:, k:k + 1], tmpC[:, k:k + 1], AF.Sqrt), [t1])
                t_r = P.op("dve", lambda e: e.reciprocal(rsC[:, k:k + 1], tmpC[:, k:k + 1]), [t2])
                return P.op("dve", lambda e: e.scalar_tensor_tensor(xn_ap, x_ap, rsC[:, k:k + 1], gbc3[:, gk - 1, :], ALU.mult, ALU.mult),
                            [t_r, C0, C1])

            def norm_a(x_ap, gk, k, xn_ap, deps, xn_free):
                return norm_a2(x_ap, gk, k, xn_ap, norm_a1(x_ap, k, xn_ap, deps, xn_free))

            def norm_b(xn_ap, dst, t_xn, bk, dst_free):
                bd = take(bk)
                t_tr = None
                for j in range(8):
                    t_tr = P.op("pe", lambda e, j=j: e.transpose(psb[:, bk, j * 128:(j + 1) * 128], xn_ap[:, j * 128:(j + 1) * 128], identb[:]),
                                ([t_xn] + bd) if j == 0 else [], inc=(j == 7))
                t_ev = P.op("act", lambda e: e.activation(dst, psb[:, bk, :].rearrange("p (k c) -> p k c", c=128), AF.Copy), [t_tr] + list(dst_free))
                P.release(bk, t_ev)
                return t_tr, t_ev

            def s4_gate(blk, ti, t_ev, pt_ev, bkg, bkp):
                b, nt, tiles = blk
                u = ti % 2
                bdg = take2(bkg)
                t_g = None
                for h in range(2):
                    for kt in range(8):
                        t_g = P.op("pe", lambda e, h=h, kt=kt, u=u: e.matmul(
                            ps[:, bkg + h, :], xnT3[:, u, kt, :], wgate_b[:, kt, h * 512:(h + 1) * 512],
                            start=(kt == 0), stop=(kt == 7)), ([t_ev, WRES] + bdg) if (h == 0 and kt == 0) else [], inc=(h == 1 and kt == 7))
                st.xnT3_rd[u] = t_g
                bdp = take2(bkp)
                t_pp = None
                for h in range(2):
                    for kt in range(2):
                        t_pp = P.op("pe", lambda e, h=h, kt=kt, ti=ti: e.matmul(
                            ps[:, bkp + h, :], pT[:, kt, ti * 128:(ti + 1) * 128], wple_b[:, kt, h * 512:(h + 1) * 512],
                            start=(kt == 0), stop=(kt == 1)), ([pt_ev[ti], WRES] + bdp) if (h == 0 and kt == 0) else [], inc=(h == 1 and kt == 1))
                st.pT_rd = t_pp
                return t_g, t_pp

            def s4_tail1(blk, ti, t_g, t_pp, t_xn, bkg, bkp):
                b, nt, tiles = blk
                i = tiles[ti]
                u = ti % 2
                xa = xr[:, b % 2, ti, :]
                t_sg = P.op("act", lambda e: e.activation(
                    sgc[:, u, :], ps[:, bkg:bkg + 2, :].rearrange("p a c -> p (a c)"), AF.Sigmoid), [t_g, st.sgc_rd[u]])
                P.release(bkg, t_sg, 2)
                t_m = P.op("dve", lambda e: e.tensor_tensor(
                    sgc[:, u, :], sgc[:, u, :], ps[:, bkp:bkp + 2, :].rearrange("p a c -> p (a c)"), ALU.mult), [t_pp, t_sg])
                P.release(bkp, t_m, 2)
                t_x3 = P.op("dve", lambda e: e.tensor_tensor(xa, xa, sgc[:, u, :], ALU.add), [t_m, t_xn])
                k = 4 * i + 2
                t_sq = P.op("dve", lambda e: e.scalar_tensor_tensor(xnc[:, ti % 2, :], xa, 1.0, xa, ALU.mult, ALU.mult, accum_out=ssC[:, k:k + 1]),
                            [t_x3, t_ssc, st.xnc_rd[ti % 2]])
                return P.op("dve", lambda e: e.tensor_scalar(tmpC[:, k:k + 1], ssC[:, k:k + 1], 1.0 / D, EPS, ALU.mult, ALU.add), [t_sq])

            def s4_tail2(blk, ti, t1):
                b, nt, tiles = blk
                i = tiles[ti]
                u = ti % 2
                xa = xr[:, b % 2, ti, :]
                k = 4 * i + 2
                t2 = P.op("act", lambda e: e.activation(tmpC[:, k:k + 1], tmpC[:, k:k + 1], AF.Sqrt), [t1])
                t_r = P.op("dve", lambda e: e.reciprocal(rsC[:, k:k + 1], tmpC[:, k:k + 1]), [t2])
                t_y = P.op("dve", lambda e: e.scalar_tensor_tensor(
                    sgc[:, u, :], xa, rsC[:, k:k + 1], gbc3[:, 2, :], ALU.mult, ALU.mult), [t_r, C0, C1])
                st.xr_free[b % 2][ti] = t_y
                t_o = P.dma("sp", lambda e: e.dma_start(out=y_o[i * 128:(i + 1) * 128, :], in_=sgc[:, u, :]), [t_y], slot="yo%d" % u)
                st.sgc_rd[u] = t_o
                out_tok.append(t_o)

            def s4_tail(blk, ti, t_g, t_pp, t_xn, bkg, bkp):
                s4_tail2(blk, ti, s4_tail1(blk, ti, t_g, t_pp, t_xn, bkg, bkp))

            for kt in range(0, 8, 4):
                wres.append(P.dma("sp", lambda e, kt=kt: e.dma_start(
                    out=wgate_b[:, kt:kt + 4, :], in_=wgate16.rearrange("(kt p) n -> p kt n", p=128)[:, kt:kt + 4, :]), [WCAST], slot="wres"))
            wres.append(P.dma("sp", lambda e: e.dma_start(
                out=wple_b[:], in_=wple16.rearrange("(kt p) n -> p kt n", p=128)), [WCAST], slot="wres"))
            WRES = wres[-1]
            for _ in range(3):
                issue_piece()
            TX = {0: dict(enumerate(load_block(blocks[0])))}
            X1 = {}
            N2 = {}

            def s1_full(blk, ti, part, ctx):
                b, nt, tiles = blk
                i = tiles[ti]
                if part == 0:
                    ctx["t1"] = s1a(blk, ti, TX[b], 0)
                elif part == 1:
                    ctx["t2"] = s1b(blk, ti, ctx["t1"], 6)
                elif part == 2:
                    ctx["a1"] = norm_a1(xr[:, b % 2, ti, :], 4 * i, xnc[:, ti % 2, :], [ctx["t2"]], [st.xnc_rd[ti % 2]])
                elif part == 3:
                    ctx["xn"] = norm_a2(xr[:, b % 2, ti, :], 1, 4 * i, xnc[:, ti % 2, :], ctx["a1"])
                else:
                    t_tr, t_ev = norm_b(xnc[:, ti % 2, :], xnT2[:, :, ti * 128:(ti + 1) * 128], ctx["xn"], 0, [st.xnT2_rd])
                    st.xnc_rd[ti % 2] = t_tr
                    N2.setdefault(b, []).append(t_ev)
                    X1.setdefault(b, {})[ti] = ctx["t2"]

            b0 = blocks[0]
            c0x = [{} for _ in range(b0[1])]
            for t0p in range(0, b0[1], 2):
                for part in range(5):
                    for ti in range(t0p, min(t0p + 2, b0[1])):
                        s1_full(b0, ti, part, c0x[ti])

            PT_EV = {}
            X2 = {}
            for bi, blk in enumerate(blocks):
                b, nt, tiles = blk
                n = nt * 128
                prev = blocks[bi - 1] if bi > 0 else None
                nxt = blocks[bi + 1] if bi + 1 < len(blocks) else None
                s4ctx = {}
                if prev is not None:
                    pnt = prev[1]
                    sched = {}
                    for ti in range(pnt):
                        sched.setdefault(5 * ti, []).append(("a1", ti))
                        sched.setdefault(5 * ti + 3, []).append(("a2", ti))
                        sched.setdefault(5 * ti + 7, []).append(("b1", ti))
                        sched.setdefault(5 * ti + 8, []).append(("b2", ti))
                        sched.setdefault(5 * ti + 11, []).append(("c1", ti))
                        sched.setdefault(5 * ti + 16, []).append(("c2", ti))
                else:
                    sched = {}
                a_ev = []
                mm_last = None
                for pc_ in range(16):
                    t_w, u = st.ptok[("up", b, pc_)]
                    for fl in range(2):
                        f = pc_ * 2 + fl
                        bk = (0, 1, 7)[f % 3]
                        bd = take(bk)
                        for kt in range(8):
                            mm_last = P.op("pe", lambda e, bk=bk, u=u, fl=fl, kt=kt, n=n: e.matmul(
                                ps[:, bk, 0:n], wupb[:, u, kt, fl * 128:(fl + 1) * 128], xnT2[:, kt, 0:n], start=(kt == 0), stop=(kt == 7)),
                                (N2[b] + [t_w] + bd) if kt == 0 else [], inc=(kt == 7))
                        t_r = P.op("act", lambda e, bk=bk, f=f, n=n: e.activation(aT[:, f, 0:n], ps[:, bk, 0:n], AF.Relu), [mm_last, st.aT_rd])
                        P.release(bk, t_r)
                        t_e = P.op("dve", lambda e, f=f, n=n: e.tensor_tensor(aT[:, f, 0:n], aT[:, f, 0:n], aT[:, f, 0:n], ALU.mult), [t_r])
                        a_ev.append(t_e)
                        for (kind, ti) in sched.get(f, []):
                            pb_, pnt_, ptiles = prev
                            pi = ptiles[ti]
                            u2 = ti % 2
                            if kind == "a1":
                                s4ctx[ti] = {}
                                s4ctx[ti]["a1"] = norm_a1(xr[:, pb_ % 2, ti, :], 4 * pi + 1, xnc[:, ti % 2, :], [X2[pb_][ti]], [st.xnc_rd[ti % 2]])
                            elif kind == "a2":
                                s4ctx[ti]["xn"] = norm_a2(xr[:, pb_ % 2, ti, :], 2, 4 * pi + 1, xnc[:, ti % 2, :], s4ctx[ti]["a1"])
                            elif kind == "b1":
                                t_tr, t_ev = norm_b(xnc[:, ti % 2, :], xnT3[:, u2, :, :], s4ctx[ti]["xn"], 6, [st.xnT3_rd[u2]])
                                st.xnc_rd[ti % 2] = t_tr
                                s4ctx[ti]["ev"] = t_ev
                            elif kind == "b2":
                                s4ctx[ti]["g"], s4ctx[ti]["pp"] = s4_gate(prev, ti, s4ctx[ti]["ev"], PT_EV[pb_], 2, 4)
                            elif kind == "c1":
                                s4ctx[ti]["c1"] = s4_tail1(prev, ti, s4ctx[ti]["g"], s4ctx[ti]["pp"], s4ctx[ti]["xn"], 2, 4)
                            else:
                                s4_tail2(prev, ti, s4ctx[ti]["c1"])
                                if nxt is not None and ti < nxt[1]:
                                    TX.setdefault(nxt[0], {})[ti] = load_tile(nxt, ti)
                    st.wup_rd[u] = mm_last
                    issue_piece()
                st.xnT2_rd = mm_last
                tp = load_p(blk)
                if nxt is not None:
                    d_ = TX.setdefault(nxt[0], {})
                    for ti in range(nxt[1]):
                        if ti not in d_:
                            d_[ti] = load_tile(nxt, ti)
                x2_ev = [None] * nt
                if nxt is not None:
                    sched1 = {}
                    for ti in range(nxt[1]):
                        for part, off in enumerate((0, 6, 12, 16, 26)):
                            sched1.setdefault(2 + 10 * ti + off, []).append((ti, part))
                else:
                    sched1 = {}
                s1ctx = {}
                step = 0
                for h in range(2):
                    banks = [2, 3, 4, 5][:nt]
                    bdall = []
                    for bk in banks:
                        bdall += take(bk)
                    for pc_ in range(8):
                        t_w, u = st.ptok[("dn", b, h, pc_)]
                        for fl in range(4):
                            f = pc_ * 4 + fl
                            for ti in range(nt):
                                first = (pc_ == 0 and fl == 0 and ti == 0)
                                mm_last = P.op("pe", lambda e, bk=banks[ti], u=u, fl=fl, f=f, ti=ti: e.matmul(
                                    ps[:, bk, :], aT[:, f, ti * 128:(ti + 1) * 128], wdnb[:, u, fl, :], start=(f == 0), stop=(f == 31)),
                                    (a_ev + bdall + [t_w]) if first else ([t_w] if (fl == 0 and ti == 0) else []),
                                    inc=((fl == 3 and ti == nt - 1)))
                            for (ti1, part) in sched1.get(step, []):
                                s1_full(nxt, ti1, part, s1ctx.setdefault(ti1, {}))
                            step += 1
                        st.wdn_rd[u] = mm_last
                        issue_piece()
                    for ti in range(nt):
                        xa = xr[:, b % 2, ti, h * 512:(h + 1) * 512]
                        t_e = P.op("dve", lambda e, xa=xa, bk=banks[ti]: e.tensor_tensor(xa, xa, ps[:, bk, :], ALU.add), [mm_last, X1[b][ti]])
                        P.release(banks[ti], t_e)
                        x2_ev[ti] = t_e
                for stp in sorted(k for k in sched1 if k >= step):
                    for (ti1, part) in sched1[stp]:
                        s1_full(nxt, ti1, part, s1ctx.setdefault(ti1, {}))
                st.aT_rd = mm_last
                X2[b] = x2_ev
                PT_EV[b] = p_transposes(blk, tp, 6)
            last = blocks[-1]
            lb, lnt, ltiles = last
            ec = [{} for _ in range(lnt)]
            for ti in range(lnt):
                ec[ti]["a1"] = norm_a1(xr[:, lb % 2, ti, :], 4 * ltiles[ti] + 1, xnc[:, ti % 2, :], [X2[lb][ti]], [st.xnc_rd[ti % 2]])
            for ti in range(lnt):
                ec[ti]["xn"] = norm_a2(xr[:, lb % 2, ti, :], 2, 4 * ltiles[ti] + 1, xnc[:, ti % 2, :], ec[ti]["a1"])
            for ti in range(lnt):
                u2 = ti % 2
                t_tr, t_ev = norm_b(xnc[:, ti % 2, :], xnT3[:, u2, :, :], ec[ti]["xn"], 6 + ti % 2, [st.xnT3_rd[u2]])
                st.xnc_rd[ti % 2] = t_tr
                ec[ti]["g"], ec[ti]["pp"] = s4_gate(last, ti, t_ev, PT_EV[lb], (2, 0)[ti % 2], 4 if ti % 2 == 0 else 4)
                ec[ti]["c1"] = s4_tail1(last, ti, ec[ti]["g"], ec[ti]["pp"], ec[ti]["xn"], (2, 0)[ti % 2], 4)
            for ti in range(lnt):
                s4_tail2(last, ti, ec[ti]["c1"])
            fin_deps = [t for t in list(out_tok) + list(dbg_o.values()) if t is not None]
            P.op("sp", lambda e: e.nop(), fin_deps, inc=False)
            P.emit()
    return nc


def host_consts():
    E = np.zeros((128, 64, 128), np.float32)
    for a in range(8):
        for b in range(8):
            for h in range(16):
                E[a * 16 + h, a * 8 + b, b * 16 + h] = 1.0
    mask = np.zeros((128, 128), np.float32)
    for s in range(8):
        for t in range(s, 8):
            mask[s * 16:(s + 1) * 16, t * 16:(t + 1) * 16] = 1.0
    k25 = np.array([-7, -6, -5, -4, -3, -2, -1, 0, 0, 1, 2, 3, 4, 5, 6, 7, 8, 7, 6, 5, 4, 3, 2, 1, 0], np.float32)
    j = np.zeros(NCH, np.float32)
    m0 = np.ones(NCH, np.float32)
    for s in range(5):
        c0 = chcol_init(s)
        n = 256 if s == 0 else 8
        j[c0] = -1.0
        m0[c0] = 0.0
        j[c0 + 1:c0 + 1 + n] = np.arange(n)
    icnt = (1.0 / np.arange(1, 17)).astype(np.float32)
    rep = lambda v: np.ascontiguousarray(np.broadcast_to(v[None, :], (128, v.shape[0]))).astype(np.float32)
    return {
        "c_identb": np.eye(128, dtype=np.float32).astype(ml_dtypes.bfloat16),
        "c_identf": np.eye(128, dtype=np.float32),
        "c_E": E.reshape(128, 64 * 128).astype(ml_dtypes.bfloat16),
        "c_mask": mask, "c_k25": rep(k25), "c_j": rep(j), "c_m0": rep(m0), "c_icnt": rep(icnt),
        "c_ones": np.ones((128, 8), np.float32).astype(ml_dtypes.bfloat16),
    }


_NC_CACHE = {}


def make_in_maps(inputs):
    f = lambda a: np.ascontiguousarray(np.asarray(a, dtype=np.float32))
    consts = host_consts()
    shared = {
        "g_mix_norm": f(inputs["g_mix_norm"][0]), "w_in": f(inputs["w_in"][0]),
        "lambda_re": f(inputs["lambda_re"][0]).reshape(16, 128), "lambda_im": f(inputs["lambda_im"][0]).reshape(16, 128),
        "log_dt": f(inputs["log_dt"][0]),
        "b_re": f(inputs["b_re"][0]).reshape(16, 128, 16), "b_im": f(inputs["b_im"][0]).reshape(16, 128, 16),
        "c_re": f(inputs["c_re"][0]).reshape(512, 64), "c_im": f(inputs["c_im"][0]).reshape(512, 64),
        "d_skip": f(inputs["d_skip"][0]), "w_glu": f(inputs["w_glu"][0]), "w_pool": f(inputs["w_pool"][0]),
        "pool_scale": f(inputs["pool_scale"][0]), "g_s5_out": f(inputs["g_s5_out"][0]), "g_pool_out": f(inputs["g_pool_out"][0]),
        "w_out": f(inputs["w_out"][0]), "g_mlp_norm": f(inputs["g_mlp_norm"][0]), "w_up": f(inputs["w_up"][0]),
        "w_down": f(inputs["w_down"][0]), "g_ple_norm": f(inputs["g_ple_norm"][0]), "w_ple_gate": f(inputs["w_ple_gate"][0]),
        "w_ple_proj": f(inputs["w_ple_proj"][0]), "g_final": f(inputs["g_final"]),
    }
    shared.update(consts)
    xp, xs = f(inputs["x_prompt"]), f(inputs["x_sample"])
    pp, psm = f(inputs["p_prompt"][0]), f(inputs["p_sample"][0])
    sre, sim, spl = f(inputs["state_s5_re"][0]), f(inputs["state_s5_im"][0]), f(inputs["state_pool"][0])
    maps = []
    for c in range(NCORES):
        m = dict(shared)
        m["x"] = np.concatenate([xp[c], xs[4 * c:4 * c + 4].reshape(NSQ * LS, D)], axis=0)
        m["p"] = np.concatenate([pp[c], psm[4 * c:4 * c + 4].reshape(NSQ * LS, 256)], axis=0)
        m["st_re"] = np.ascontiguousarray(sre[4 * c:4 * c + 4].reshape(NSQ * 16, 128))
        m["st_im"] = np.ascontiguousarray(sim[4 * c:4 * c + 4].reshape(NSQ * 16, 128))
        m["st_pool"] = np.ascontiguousarray(spl[4 * c:4 * c + 4])
        maps.append(m)
    return maps


def kernel(**inputs):
    if "nc" not in _NC_CACHE:
        _NC_CACHE["nc"] = build()
    nc = _NC_CACHE["nc"]
    maps = make_in_maps(inputs)
    res = run_bass_kernel_spmd(nc, maps, core_ids=list(range(NCORES)))
    R = res.results
    y_p = np.stack([R[c]["y"][:LP] for c in range(NCORES)], 0)
    y_s = np.concatenate([R[c]["y"][LP:].reshape(NSQ, LS, D) for c in range(NCORES)], 0)
    hre = [R[c]["h_re"].reshape(5, 32, 64) for c in range(NCORES)]
    him = [R[c]["h_im"].reshape(5, 32, 64) for c in range(NCORES)]
    pl = [R[c]["pool_new"] for c in range(NCORES)]
    re_p = np.stack([h[0] for h in hre], 0)[None]
    im_p = np.stack([h[0] for h in him], 0)[None]
    pool_p = np.stack([q[0] for q in pl], 0)[None]
    re_s = np.concatenate([h[1:] for h in hre], 0)[None]
    im_s = np.concatenate([h[1:] for h in him], 0)[None]
    pool_s = np.concatenate([q[1:] for q in pl], 0)[None]
    a = lambda v: np.ascontiguousarray(v, dtype=np.float32)
    return (a(y_p), a(y_s), a(re_p), a(im_p), a(pool_p), a(re_s), a(im_s), a(pool_s))
```

```python
import numpy as np
import ml_dtypes
from contextlib import ExitStack
import concourse.bass as bass
import concourse.mybir as mybir
from concourse.bass_utils import run_bass_kernel_spmd

F32, BF16 = mybir.dt.float32, mybir.dt.bfloat16
AF = mybir.ActivationFunctionType
ALU = mybir.AluOpType

NCORES = 8
D = 1024
LP = 2048
NSQ = 4
LS = 64
NTOK = LP + NSQ * LS
NTILE = NTOK // 128
TOKP = 8 + LP + NSQ * (8 + LS)
NCH = TOKP // 8
PW = 16 + LP + NSQ * (16 + LS)
EPS = 1e-6
TWO_PI = 6.283185307179586
MAGIC = 12582912.0
ENGS = ("pe", "act", "dve", "pool", "sp")
LIMIT = 100000000


def tokcol(seq, t=0):
    return 8 + t if seq == 0 else 2064 + 72 * (seq - 1) + t


def chcol_init(seq):
    return 0 if seq == 0 else 257 + 9 * (seq - 1)


def poolcol(seq, t=0):
    return 16 + t if seq == 0 else 2080 + 80 * (seq - 1) + t


class Prog:
    def __init__(self, nc, es):
        self.nc, self.es = nc, es
        self.ops = []
        self.tok = {}
        self.nid = 0
        self.sem = {e: es.enter_context(nc.semaphore("s_" + e)) for e in ENGS}
        self.cnt = {e: 0 for e in ENGS}
        self.waited = {e: {} for e in ENGS}
        self.slots = {}
        self.bank_tok = [[] for _ in range(8)]
        self.bank_ptr = 0
        self.last = {}

    def op(self, eng, fn, deps=(), inc=True, force=False):
        if self.nid >= LIMIT and not force:
            return None
        i = self.nid
        self.nid += 1
        d = []
        for x in deps:
            if x is None:
                continue
            if isinstance(x, (list, tuple)):
                d.extend([y for y in x if y is not None])
            else:
                d.append(x)
        if inc:
            self.cnt[eng] += 1
            self.tok[i] = (self.sem[eng], self.cnt[eng])
            self.last[eng] = i
        self.ops.append((eng, fn, d, inc, None))
        return i if inc else None

    def dma(self, eng, fn, deps=(), slot="misc"):
        if self.nid >= LIMIT:
            return None
        i = self.nid
        self.nid += 1
        d = []
        for x in deps:
            if x is None:
                continue
            if isinstance(x, (list, tuple)):
                d.extend([y for y in x if y is not None])
            else:
                d.append(x)
        if slot not in self.slots:
            self.slots[slot] = [self.es.enter_context(self.nc.semaphore("d_" + str(len(self.slots)))), 0]
        s = self.slots[slot]
        s[1] += 16
        self.tok[i] = (s[0], s[1])
        self.last["slot:" + str(slot)] = i
        self.ops.append((eng, fn, d, True, s[0]))
        return i

    def bank(self):
        b = self.bank_ptr
        self.bank_ptr = (b + 1) % 8
        return b, list(self.bank_tok[b])

    def bank2(self):
        if self.bank_ptr % 2:
            self.bank_ptr = (self.bank_ptr + 1) % 8
        b = self.bank_ptr
        self.bank_ptr = (b + 2) % 8
        return b, list(self.bank_tok[b]) + list(self.bank_tok[b + 1])

    def release(self, b, toks, n=1):
        if not isinstance(toks, (list, tuple)):
            toks = [toks]
        for k in range(n):
            self.bank_tok[b + k] = list(toks)

    def emit(self, barrier=True):
        nc = self.nc
        if barrier:
            lasts = [v for k, v in self.last.items() if k != "slot:wcast"]
            for eng in ENGS:
                self.op(eng, lambda e: e.nop(), lasts, inc=False, force=True)
        ops = self.ops
        self.ops = []
        with nc.Block() as block:
            def make(engname):
                def body(e):
                    w = self.waited[engname]
                    for (eng, fn, deps, inc, dsem) in ops:
                        if eng != engname:
                            continue
                        for dpt in deps:
                            sem, val = self.tok[dpt]
                            if w.get(sem.name, 0) < val:
                                e.wait_ge(sem, val)
                                w[sem.name] = val
                        ins = fn(e)
                        if dsem is not None:
                            ins.then_inc(dsem, 16)
                        elif inc:
                            ins.then_inc(self.sem[engname], 1)
                return body
            block.tensor(make("pe"))
            block.scalar(make("act"))
            block.vector(make("dve"))
            block.gpsimd(make("pool"))
            block.sync(make("sp"))


def bcast(ap, shape, axis):
    return ap.unsqueeze(axis).to_broadcast(shape)


def build(debug=None, stop=None):
    nc = bass.Bass("TRN2", target_bir_lowering=False)
    debug = debug or []

    def din(name, shape, dt=F32):
        return nc.dram_tensor(name, list(shape), dt, kind="ExternalInput").ap()

    def dout(name, shape, dt=F32):
        return nc.dram_tensor(name, list(shape), dt, kind="ExternalOutput").ap()

    x_d = din("x", [NTOK, D])
    p_d = din("p", [NTOK, 256])
    sre_d = din("st_re", [NSQ * 16, 128])
    sim_d = din("st_im", [NSQ * 16, 128])
    spool_d = din("st_pool", [NSQ, 15, 512])
    gmix_d = din("g_mix_norm", [D]); win_d = din("w_in", [D, D])
    lre_d = din("lambda_re", [16, 128]); lim_d = din("lambda_im", [16, 128]); ldt_d = din("log_dt", [32])
    bre_d = din("b_re", [16, 128, 16]); bim_d = din("b_im", [16, 128, 16])
    cre_d = din("c_re", [512, 64]); cim_d = din("c_im", [512, 64])
    dsk_d = din("d_skip", [512]); wglu_d = din("w_glu", [512, 512]); wpool_d = din("w_pool", [4, 128, 128])
    pscale_d = din("pool_scale", [512]); gs5_d = din("g_s5_out", [512]); gpo_d = din("g_pool_out", [512])
    wout_d = din("w_out", [D, D]); gmlp_d = din("g_mlp_norm", [D]); wup_d = din("w_up", [D, 4 * D])
    wdn_d = din("w_down", [4 * D, D]); gple_d = din("g_ple_norm", [D]); wgate_d = din("w_ple_gate", [D, D])
    wple_d = din("w_ple_proj", [256, D]); gfin_d = din("g_final", [D])
    identb_d = din("c_identb", [128, 128], BF16); identf_d = din("c_identf", [128, 128])
    E_d = din("c_E", [128, 64 * 128], BF16); mask_d = din("c_mask", [128, 128])
    k25_d = din("c_k25", [128, 25]); j_d = din("c_j", [128, NCH]); m0_d = din("c_m0", [128, NCH])
    icnt_d = din("c_icnt", [128, 16]); ones_d = din("c_ones", [128, 8], BF16)

    wup16 = nc.dram_tensor("wup16", [D, 4 * D], BF16, kind="Internal").ap()
    wdn16 = nc.dram_tensor("wdn16", [4 * D, D], BF16, kind="Internal").ap()
    wout16 = nc.dram_tensor("wout16", [D, D], BF16, kind="Internal").ap()
    wgate16 = nc.dram_tensor("wgate16", [D, D], BF16, kind="Internal").ap()
    wple16 = nc.dram_tensor("wple16", [256, D], BF16, kind="Internal").ap()
    y_o = dout("y", [NTOK, D])
    hre_o = dout("h_re", [80, 128]); him_o = dout("h_im", [80, 128])
    pool_o = dout("pool_new", [5, 15, 512])
    dbg_o = {}

    es = ExitStack()
    with es:
        P = Prog(nc, es)

        def sb(name, shape, dt=F32, stack=es):
            return stack.enter_context(nc.sbuf_tensor(name, list(shape), dt))

        ps = es.enter_context(nc.psum_tensor("ps", [128, 8, 512], F32))
        psb = ps[:].bitcast(BF16)

        def dump(name, ap, shape, deps):
            if name not in debug:
                return None
            o = dout("dbg_" + name, shape, ap.dtype if hasattr(ap, "dtype") else F32)
            dbg_o[name] = P.dma("sp", lambda e: e.dma_start(out=o, in_=ap), deps, slot="dbg_" + name)
            return dbg_o[name]

        identb = sb("identb", [128, 128], BF16); identf = sb("identf", [128, 128])
        ones_b = sb("ones_b", [128, 8], BF16)
        gbc = sb("gbc", [128, 4, D], F32) if False else None
        gbc3 = sb("gbc3", [128, 3, D])
        y2T = sb("y2T", [128, 4, TOKP + 256], BF16)
        ypT = sb("ypT", [128, 4, TOKP + 256], BF16)
        rstd_s5 = sb("rstd_s5", [128, NTILE]); rstd_po = sb("rstd_po", [128, NTILE])
        colv = sb("colv", [128, 3, 4])
        HL = sb("HL", [128, 2, 5, 16])

        c0 = []
        c0.append(P.dma("sp", lambda e: e.dma_start(out=identb[:], in_=identb_d), slot="c0"))
        c0.append(P.dma("sp", lambda e: e.dma_start(out=identf[:], in_=identf_d), slot="c0"))
        c0.append(P.dma("sp", lambda e: e.dma_start(out=ones_b[:], in_=ones_d), slot="c0"))
        C0 = c0[-1]

        def late_consts():
            c1 = []
            for k, g in enumerate((pscale_d, gs5_d, gpo_d)):
                c1.append(P.dma("sp", lambda e, k=k, g=g: e.dma_start(
                    out=colv[:, k, :], in_=bass.AP(g.tensor, 0, [[1, 128], [128, 4]]),
                    allow_slow_non_contiguous=True), slot="c1"))
            for k, g in enumerate((gmlp_d, gple_d, gfin_d)):
                c1.append(P.dma("sp", lambda e, k=k, g=g: e.dma_start(
                    out=gbc3[:, k, :], in_=bass.AP(g.tensor, 0, [[0, 128], [1, D]])), slot="c1"))
            return c1[-1]
        sis = ExitStack()
        lam = sb("lam", [128, 2, 16], F32, sis); ldt = sb("ldt", [128, 16], F32, sis)
        Bn = sb("Bn", [128, 2, 16, 16], F32, sis)
        Cn = sb("Cn", [128, 2, 2, 2, 64], F32, sis)
        k25 = sb("k25", [128, 25], F32, sis); maskf = sb("maskf", [128, 128], F32, sis)
        dcol = sb("dcol", [128, 32], F32, sis)
        h0n = sb("h0n", [64, 2, 128], F32, sis)

        def tile_cols(buf, kt, i):
            if i < 16:
                c = 8 + 128 * i
                return buf[:, kt, c:c + 128]
            c = tokcol(1 + 2 * (i - 16)) - 8
            return buf[:, kt, c:c + 144].rearrange("p (q c) -> p q c", c=72)[:, :, 8:72]

        def blk_cols(buf, kt, b):
            if b < 4:
                c = 8 + 512 * b
                return buf[:, kt, c:c + 512], 512
            c = tokcol(1) - 8
            return buf[:, kt, c:c + 288].rearrange("p (q c) -> p q c", c=72)[:, :, 8:72], 256

        def otile(buf, kt, i):
            if i < 16:
                c = 8 + 128 * i
                return buf[:, kt, c:c + 128]
            c = TOKP + 128 * (i - 16)
            return buf[:, kt, c:c + 128]

        def oblk(buf, kt, b, three=False):
            if b < 4:
                c = 8 + 512 * b
                return buf[:, kt, c:c + 512]
            v = buf[:, kt, TOKP:TOKP + 256]
            return v.rearrange("p (q c) -> p q c", c=64) if three else v

        def ps_cols(bk, n, b):
            if b < 4:
                return ps[:, bk, 0:n]
            return ps[:, bk, 0:256].rearrange("p (q c) -> p q c", c=64)

        def ps_tile(bk, i, width=128):
            if i < 16:
                return ps[:, bk, 0:width]
            return ps[:, bk, 0:128].rearrange("p (q c) -> p q c", c=64)

        def rstd_chain(ss_ap, out_ap, tmp_ap, scale, deps):
            t1 = P.op("dve", lambda e: e.tensor_scalar(tmp_ap, ss_ap, scale, EPS, ALU.mult, ALU.add), deps)
            t2 = P.op("act", lambda e: e.activation(tmp_ap, tmp_ap, AF.Sqrt), [t1])
            return P.op("dve", lambda e: e.reciprocal(out_ap, tmp_ap), [t2])

        def norm_transpose(x_ap, gk, ss_ap, tmp_ap, rs_ap, junk, xn_ap, xnT_dst, deps, xn_free, dst_free, act_evac=True):
            t_sq = P.op("act", lambda e: e.activation(junk, x_ap, AF.Square, accum_out=ss_ap), deps)
            t_r = rstd_chain(ss_ap, rs_ap, tmp_ap, 1.0 / D, [t_sq])
            t_xn = P.op("dve", lambda e: e.scalar_tensor_tensor(xn_ap, x_ap, rs_ap, gbc[:, gk, :], ALU.mult, ALU.mult),
                        [t_r, C0] + list(xn_free) + list(deps))
            bk, bd = P.bank()
            t_tr = None
            for j in range(8):
                t_tr = P.op("pe", lambda e, j=j: e.transpose(psb[:, bk, j * 128:(j + 1) * 128], xn_ap[:, j * 128:(j + 1) * 128], identb[:]),
                            [t_xn] + bd if j == 0 else [], inc=(j == 7))
            src = psb[:, bk, :].rearrange("p (k c) -> p k c", c=128)
            if act_evac:
                t_ev = P.op("act", lambda e: e.activation(xnT_dst, src, AF.Copy), [t_tr] + list(dst_free))
            else:
                t_ev = P.op("dve", lambda e: e.tensor_copy(xnT_dst, src), [t_tr] + list(dst_free))
            P.release(bk, t_ev)
            return t_xn, t_tr, t_ev

        zs = ExitStack()
        zT = sb("zT", [128, 4, 8, NCH], BF16, zs)
        pa = ExitStack()
        with pa:
            win_b = sb("win_b", [128, 8, D], BF16, pa)
            gmx = sb("gmx", [128, D], F32, pa)
            t_gmx = P.dma("sp", lambda e: e.dma_start(out=gmx[:], in_=bass.AP(gmix_d.tensor, 0, [[0, 128], [1, D]])), slot="gmx")
            zP = sb("zP", [128, 4, PW], F32, pa)
            xt = sb("xt", [128, 4, D], F32, pa)
            ssA = sb("ssA", [128, NTILE], F32, pa); tmpA = sb("tmpA", [128, NTILE], F32, pa); rsA = sb("rsA", [128, NTILE], F32, pa)
            xn = sb("xn", [128, 4, D], BF16, pa)
            xnT = sb("xnT", [128, 2, 8, 512], BF16, pa)
            wpool_b = sb("wpool_b", [128, 4, 128], BF16, pa)
            icnt = sb("icnt", [128, 16], F32, pa)
            pt1 = sb("pt1", [128, PW], F32, pa); pt2 = sb("pt2", [128, PW], F32, pa)
            pooledT = sb("pooledT", [128, 1, TOKP], BF16, pa)
            ysq = sb("ysq", [128, 1, TOKP + 256], BF16, pa)
            ssP = sb("ssP", [128, NTILE], F32, pa); tmpP = sb("tmpP", [128, NTILE], F32, pa)

            t_win = None
            for kt in range(0, 8, 4):
                t_win = P.dma("pool", lambda e, kt=kt: e.dma_start(
                    out=win_b[:, kt:kt + 4, :], in_=win_d.rearrange("(kt p) n -> p kt n", p=128)[:, kt:kt + 4, :]), slot="win")
            t_wpool = P.dma("pool", lambda e: e.dma_start(out=wpool_b[:], in_=wpool_d.rearrange("g k n -> k g n")), slot="wpool")
            t_icnt = P.dma("sp", lambda e: e.dma_start(out=icnt[:], in_=icnt_d), slot="icnt")
            t_z0 = P.op("dve", lambda e: e.memset(zT[:], 0.0))
            t_zp0 = P.op("dve", lambda e: e.memset(zP[:], 0.0))
            t_ss0 = P.op("dve", lambda e: e.memset(ssA[:], 0.0))
            P.op("dve", lambda e: e.memset(pt1[:], 0.0), inc=False)
            t_pt0 = P.op("dve", lambda e: e.memset(pt2[:], 0.0))
            pt_free0 = t_pt0
            deferred = []
            deferred_h = []

            def mk_thunk(eng, fn, deps=(), slot="misc", **kw):
                return lambda: P.dma(eng, fn, deps, slot=slot)
            for r, (ld_, li_) in enumerate(((lre_d, bre_d), (lim_d, bim_d))):
                deferred.append(mk_thunk("sp", lambda e, r=r, ld_=ld_: e.dma_start(
                    out=lam[:, r, :], in_=ld_.rearrange("pr q -> q pr"), allow_slow_non_contiguous=True), slot="su"))
                deferred.append(mk_thunk("sp", lambda e, r=r, li_=li_: e.dma_start(
                    out=Bn[:, r, :, :], in_=li_.rearrange("pr q h -> q pr h")), slot="su"))
            for e_ in range(2):
                deferred.append(mk_thunk("sp", lambda e, e_=e_: e.dma_start(
                    out=ldt[e_ * 64:(e_ + 1) * 64, :], in_=bass.AP(ldt_d.tensor, e_, [[0, 64], [2, 16]]),
                    allow_slow_non_contiguous=True), slot="su"))
            for r, cd in enumerate((cre_d, cim_d)):
                for e_ in range(2):
                    for hi in range(2):
                        for lo in range(8):
                            deferred.append(mk_thunk("sp", lambda e, r=r, cd=cd, e_=e_, hi=hi, lo=lo: e.dma_start(
                                out=Cn[lo * 16:(lo + 1) * 16, r, e_, hi, :],
                                in_=bass.AP(cd.tensor, (hi * 256 + lo * 32 + e_ * 16) * 64, [[64, 16], [1, 64]])), slot="su"))
            deferred.append(mk_thunk("sp", lambda e: e.dma_start(out=k25[:], in_=k25_d), slot="su"))
            deferred.append(mk_thunk("sp", lambda e: e.dma_start(out=maskf[:], in_=mask_d), slot="su"))
            for s_ in range(8):
                deferred.append(mk_thunk("sp", lambda e, s_=s_: e.dma_start(
                    out=dcol[s_ * 16:(s_ + 1) * 16, :], in_=bass.AP(dsk_d.tensor, 0, [[1, 16], [16, 32]]),
                    allow_slow_non_contiguous=True), slot="su"))
            deferred.append(mk_thunk("sp", lambda e: e.dma_start(out=h0n[:, 0, :], in_=sre_d), slot="su"))
            deferred.append(mk_thunk("sp", lambda e: e.dma_start(out=h0n[:, 1, :], in_=sim_d), slot="su"))
            for q in range(NSQ):
                for j in range(4):
                    c = poolcol(1 + q) - 15
                    deferred_h.append(mk_thunk("sp", lambda e, q=q, j=j, c=c: e.dma_start(
                        out=zP[:, j, c:c + 15], in_=spool_d[q, :, j * 128:(j + 1) * 128].rearrange("t c -> c t"),
                        allow_slow_non_contiguous=True), [t_zp0], slot="hist"))

            dq = deferred + deferred_h
            n_su = len(deferred)
            DTOK = []

            def flush(n):
                for _ in range(n):
                    if len(DTOK) < len(dq):
                        DTOK.append(dq[len(DTOK)]())
            xn_rd = [None] * 4
            xt_rd = [None] * 4
            xnT_rd = [None, None]
            z_ev = []
            XN = {}
            EVS = {}

            def chainA(hb):
                tiles = [2 * hb, 2 * hb + 1]
                sqs = []
                lds = []
                for i in tiles:
                    u = i % 4
                    t_ld = P.dma("sp", lambda e, i=i, u=u: e.dma_start(out=xt[:, u, :], in_=x_d[i * 128:(i + 1) * 128, :]),
                                 [xt_rd[u]], slot="xt%d" % u)
                    lds.append(t_ld)
                    sqs.append(P.op("act", lambda e, i=i, u=u: e.activation(xn[:, u, :], xt[:, u, :], AF.Square, accum_out=ssA[:, i:i + 1]),
                                    [t_ld, t_ss0, xn_rd[u]]))
                c0_, c1_ = tiles[0], tiles[1] + 1
                t_r = rstd_chain(ssA[:, c0_:c1_], rsA[:, c0_:c1_], tmpA[:, c0_:c1_], 1.0 / D, sqs)
                for k_, i in enumerate(tiles):
                    u = i % 4
                    t_xn = P.op("dve", lambda e, i=i, u=u: e.scalar_tensor_tensor(
                        xn[:, u, :], xt[:, u, :], rsA[:, i:i + 1], gmx[:], ALU.mult, ALU.mult), [t_r, t_gmx, xn_rd[u], lds[k_]])
                    xt_rd[u] = t_xn
                    XN[i] = t_xn

            def transA(hb):
                for i in (2 * hb, 2 * hb + 1):
                    u = i % 4
                    b = min(i // 4, 4)
                    ti = i - 4 * b
                    bb = b % 2
                    bk, bd = P.bank()
                    t_tr = None
                    for j in range(8):
                        t_tr = P.op("pe", lambda e, j=j, u=u, bk=bk: e.transpose(
                            psb[:, bk, j * 128:(j + 1) * 128], xn[:, u, j * 128:(j + 1) * 128], identb[:]),
                            ([XN[i], C0] + bd) if j == 0 else [], inc=(j == 7))
                    t_ev = P.op("act", lambda e, bk=bk, bb=bb, ti=ti: e.activation(
                        xnT[:, bb, :, ti * 128:(ti + 1) * 128], psb[:, bk, :].rearrange("p (k c) -> p k c", c=128), AF.Copy),
                        [t_tr, xnT_rd[bb]])
                    P.release(bk, t_ev)
                    xn_rd[u] = t_tr
                    EVS.setdefault(b, []).append(t_ev)

            def mmA(b):
                nt = 4 if b < 4 else 2
                bb = b % 2
                n = nt * 128
                evs = EVS[b]
                last_mm = None
                for m in range(8):
                    bk, bd = P.bank()
                    for kt in range(8):
                        last_mm = P.op("pe", lambda e, m=m, kt=kt, bk=bk, n=n, bb=bb: e.matmul(
                            ps[:, bk, 0:n], win_b[:, kt, m * 128:(m + 1) * 128], xnT[:, bb, kt, 0:n],
                            start=(kt == 0), stop=(kt == 7)),
                            (evs + bd + [t_win]) if kt == 0 else [], inc=(kt == 7))
                    if m < 4:
                        if b < 4:
                            dst = zT[:, m, :, 1 + 64 * b:1 + 64 * b + 64]
                            src = ps[:, bk, 0:512].rearrange("p (c s) -> p s c", s=8)
                        else:
                            dst = zT[:, m, :, 257:257 + 36].rearrange("p s (q c) -> p s q c", c=9)[:, :, :, 1:9]
                            src = ps[:, bk, 0:256].rearrange("p (q c s) -> p s q c", s=8, c=8)
                        t_e = P.op("act", lambda e, dst=dst, src=src: e.activation(dst, src, AF.Copy), [last_mm, t_z0])
                    else:
                        j = m - 4
                        if b < 4:
                            dst = zP[:, j, 16 + 512 * b:16 + 512 * b + 512]
                        else:
                            dst = zP[:, j, 2064:2064 + 320].rearrange("p (q c) -> p q c", c=80)[:, :, 16:80]
                        t_e = P.op("dve", lambda e, dst=dst, bk=bk, n=n, b=b: e.tensor_copy(dst, ps_cols(bk, n, b)),
                                   [last_mm, t_zp0])
                    P.release(bk, t_e)
                    z_ev.append(t_e)
                xnT_rd[bb] = last_mm

            chainA(0)
            chainA(1)
            C1 = late_consts()
            for hb in range(9):
                transA(hb)
                if hb + 2 < 9:
                    chainA(hb + 2)
                if hb >= 2:
                    flush(16)
                if hb % 2 == 1:
                    mmA(hb // 2)
            mmA(4)
            flush(len(dq))
            LD = DTOK[n_su - 1]
            t_hist = DTOK[-1]
            dump("zT", zT[:], [128, 4, 8, NCH], z_ev)
            dump("zP", zP[:], [128, 4, PW], z_ev)

            out_tok = []
            pst = xt[:].rearrange("p a d -> p (a d)")[:, 0:2560].rearrange("p (s c) -> p s c", c=512)
            for s_ in range(5):
                L = LP if s_ == 0 else LS
                c = poolcol(s_, L - 15)
                bk, bd = P.bank()
                t_q = None
                for j in range(4):
                    t_q = P.op("pe", lambda e, c=c, j=j, bk=bk: e.transpose(
                        ps[0:15, bk, j * 128:(j + 1) * 128], zP[:, j, c:c + 15], identf[:]),
                        (z_ev + bd + [C0]) if j == 0 else [], inc=(j == 3))
                t_e = P.op("act", lambda e, s_=s_, bk=bk: e.activation(pst[0:15, s_, :], ps[0:15, bk, :], AF.Copy), [t_q])
                P.release(bk, t_e)
                out_tok.append(P.dma("sp", lambda e, s_=s_: e.dma_start(out=pool_o[s_], in_=pst[0:15, s_, :]), [t_e], slot="outs"))
            t_wc = None
            for r8 in range(8):
                t_wc = P.dma("pool", lambda e, r8=r8: e.dma_start(out=wup16[r8 * 128:(r8 + 1) * 128, :], in_=wup_d[r8 * 128:(r8 + 1) * 128, :]), [out_tok[-1], t_hist], slot="wcast")
            pool_ev = []
            t_sqp = None
            pt_free = [pt_free0, pt_free0]
            pooled_free = None
            ysq_rd = None
            psg = sb("psg", [128, 4], F32, pa)
            t_psg = P.op("dve", lambda e: e.tensor_tensor(psg[:], colv[:, 0, :], colv[:, 2, :], ALU.mult), [C0, C1])
            PM = {}
            fxs = [sb("fx%d" % j, [128, 16], F32, pa) for j in range(4)]

            def pm_dve(j):
                w = (2, 4, 8, 16)[j]
                src = zP[:, j, :]
                cur = src
                sh = 1
                k = 0
                tcur = z_ev + [t_hist]
                bufs = [pt1, pt2]
                pt_free = PM.get("pt_free", [pt_free0, pt_free0])
                while sh < w:
                    dstb = bufs[k % 2]
                    tcur = [P.op("dve", lambda e, dstb=dstb, cur=cur, sh=sh: e.tensor_tensor(
                        dstb[:, sh:PW], cur[:, sh:PW], cur[:, 0:PW - sh], ALU.add), list(tcur) + [pt_free[k % 2]])]
                    cur = dstb[:]
                    sh *= 2
                    k += 1
                tp = []
                tp.append(P.op("dve", lambda e: e.scalar_tensor_tensor(
                    pooledT[:, 0, 8:8 + LP], cur[:, 16:16 + LP], 1.0 / w, src[:, 16:16 + LP], ALU.mult, ALU.subtract),
                    tcur + [PM.get("pooled_free")]))
                tp.append(P.op("dve", lambda e: e.scalar_tensor_tensor(
                    pooledT[:, 0, 2056:2056 + 288].rearrange("p (q c) -> p q c", c=72)[:, :, 8:72],
                    cur[:, 2064:2064 + 320].rearrange("p (q c) -> p q c", c=80)[:, :, 16:80], 1.0 / w,
                    src[:, 2064:2064 + 320].rearrange("p (q c) -> p q c", c=80)[:, :, 16:80], ALU.mult, ALU.subtract), tcur))
                nf = w - 1
                fx = fxs[j]
                t_f1 = P.op("dve", lambda e: e.tensor_tensor(fx[:, 0:nf], cur[:, 16:16 + nf], icnt[:, 0:nf], ALU.mult), tcur + [t_icnt])
                t_f2 = P.op("dve", lambda e: e.tensor_tensor(pooledT[:, 0, 8:8 + nf], fx[:, 0:nf], src[:, 16:16 + nf], ALU.subtract), [t_f1] + tp)
                PM["pt_free"] = [t_f2, t_f2]
                PM["f2_%d" % j] = t_f2

            def pm_pe(j):
                t_f2 = PM["f2_%d" % j]
                last_mm = None
                evs = []
                for b in range(5):
                    rhs, n = blk_cols(pooledT, 0, b)
                    bk, bd = P.bank()
                    last_mm = P.op("pe", lambda e, bk=bk, n=n, rhs=rhs, b=b: e.matmul(
                        ps_cols(bk, n, b), wpool_b[:, j, :], rhs, start=True, stop=True), [t_f2, t_wpool] + bd)
                    dst = oblk(ypT, j, b)
                    t_e = P.op("act", lambda e, dst=dst, bk=bk, n=n: e.activation(
                        dst, ps[:, bk, 0:n], AF.Copy, scale=psg[:, j:j + 1]), [last_mm, t_psg])
                    dsq = oblk(ysq, 0, b)
                    t_sqp = P.op("act", lambda e, dsq=dsq, bk=bk, n=n: e.activation(
                        dsq, ps[:, bk, 0:n], AF.Square, scale=colv[:, 0, j:j + 1]), [last_mm, C0, C1, PM.get("ysq_rd")])
                    P.release(bk, t_sqp)
                    pool_ev.append(t_sqp)
                    evs.append(t_sqp)
                PM["pooled_free"] = last_mm
                bk, bd = P.bank()
                t_st = None
                for i in range(NTILE):
                    t_st = P.op("pe", lambda e, i=i, bk=bk: e.matmul(
                        ps[:, bk, i:i + 1], otile(ysq, 0, i), ones_b[:, 0:1], start=True, stop=True),
                        (evs + bd + [C0]) if i == 0 else [], inc=(i == NTILE - 1))
                PM["ysq_rd"] = t_st
                PM["st_%d" % j] = (t_st, bk)

            def pm_acc(j):
                t_st, bk = PM["st_%d" % j]
                if j == 0:
                    t_acc = P.op("dve", lambda e: e.tensor_copy(ssP[:], ps[:, bk, 0:NTILE]), [t_st])
                else:
                    t_acc = P.op("dve", lambda e: e.tensor_tensor(ssP[:], ssP[:], ps[:, bk, 0:NTILE], ALU.add), [t_st, PM["acc"]])
                P.release(bk, t_acc)
                PM["acc"] = t_acc

            pm_dve(0)
            for j in range(4):
                pm_pe(j)
                if j + 1 < 4:
                    pm_dve(j + 1)
                pm_acc(j)
            t_acc = PM["acc"]
            t_rpo = rstd_chain(ssP[:], rstd_po[:], tmpP[:], 1.0 / 512, [t_acc])
            t_gp = pool_ev[-1]
            POOL_DONE = [t_gp, t_rpo]
            dump("ypT", ypT[:], [128, 4, TOKP + 256], POOL_DONE)
            dump("rstd_po", rstd_po[:], [128, NTILE], POOL_DONE)
            if stop == "A":
                P.op("sp", lambda e: e.nop(), [t for t in list(out_tok) + list(dbg_o.values()) if t is not None], inc=False, force=True)
            P.emit()
        if stop == "A":
            zs.close()
            return nc
        pb = ExitStack()
        with pb:
            Kin = sb("Kin", [128, 32, 128], BF16, pb)
            BT = sb("BT", [128, 2, 16, 128], BF16, pb)
            Cw = sb("Cw", [128, 3, 32, 128], BF16, pb)
            Rr = sb("Rr", [128, 16], F32, pb); THT = sb("THT", [128, 16], F32, pb)
            h0 = sb("h0", [128, 2, 64], F32, pb)
            yT = y2T

            for r8 in range(8):
                t_wc = P.dma("pool", lambda e, r8=r8: e.dma_start(out=wdn16[r8 * 512:(r8 + 1) * 512, :], in_=wdn_d[r8 * 512:(r8 + 1) * 512, :]), slot="wcast")
            t_wc = P.dma("pool", lambda e: e.dma_start(out=wout16, in_=wout_d), slot="wcast")
            t_wc = P.dma("pool", lambda e: e.dma_start(out=wgate16, in_=wgate_d), slot="wcast")
            t_wc = P.dma("pool", lambda e: e.dma_start(out=wple16, in_=wple_d), slot="wcast")
            WCAST = t_wc
            su = ExitStack()
            with su:
                Cl = sb("Cl", [128, 2, 16, 16], F32, su)
                dtt = sb("dtt", [128, 16], F32, su); lrdt = sb("lrdt", [128, 16], F32, su); th = sb("th", [128, 16], F32, su)
                amag = sb("amag", [128, 16, 25], F32, su); aph = sb("aph", [128, 16, 25], F32, su)
                trn = sb("trn", [128, 16, 25], F32, su); frs = sb("frs", [128, 16, 25], F32, su); frc = sb("frc", [128, 16, 25], F32, su)
                apw = sb("apw", [128, 2, 16, 25], F32, su)
                s1 = sb("s1", [128, 16], F32, su); s2 = sb("s2", [128, 16], F32, su); s3 = sb("s3", [128, 16], F32, su)
                qq = sb("qq", [128, 2, 16], F32, su)
                Bb = sb("Bb", [128, 2, 16, 16], F32, su)
                tA = sb("tA", [128, 16, 8, 16], F32, su); tB = sb("tB", [128, 16, 8, 16], F32, su)
                Zr = sb("Zr", [128, 16, 8, 16], F32, su); Zi = sb("Zi", [128, 16, 8, 16], F32, su)
                Xr = sb("Xr", [128, 16, 8, 16], F32, su); nXi = sb("nXi", [128, 16, 8, 16], F32, su)
                tC = sb("tC", [128, 16, 8, 16], F32, su); mk = sb("mk", [128, 2], F32, su)
                ktmp = tA[:].rearrange("q a s h -> q (a s h)")


                def dv(fn, deps):
                    return P.op("dve", fn, deps)

                bkc, bd = P.bank()
                t_c = None
                for r in range(2):
                    for e_ in range(2):
                        for hi in range(2):
                            t_c = P.op("pe", lambda e, r=r, e_=e_, hi=hi: e.matmul(
                                ps[e_ * 64:(e_ + 1) * 64, bkc, (r * 2 + hi) * 128:(r * 2 + hi + 1) * 128], Cn[:, r, e_, hi, :], identf[:],
                                start=True, stop=True), [LD, C0] + bd)
                t_cl = dv(lambda e: e.tensor_copy(Cl[:].rearrange("q r a b -> q (r a b)"), ps[:, bkc, 0:512]), [t_c])
                P.release(bkc, t_cl)
                bkh, bd = P.bank()
                t_h = None
                for r in range(2):
                    t_h = P.op("pe", lambda e, r=r: e.transpose(ps[:, bkh, r * 64:(r + 1) * 64], h0n[:, r, :], identf[0:64, 0:64]), [LD, C0] + bd)
                t_h0 = dv(lambda e: e.tensor_copy(h0[:].rearrange("q r a -> q (r a)"), ps[:, bkh, 0:128]), [t_h])
                P.release(bkh, t_h0)

                t = P.op("act", lambda e: e.activation(dtt[:], ldt[:], AF.Exp), [LD])
                t1 = dv(lambda e: e.tensor_tensor(lrdt[:], lam[:, 0, :], dtt[:], ALU.mult), [t])
                t2 = dv(lambda e: e.tensor_tensor(th[:], lam[:, 1, :], dtt[:], ALU.mult), [t])
                sh3 = [128, 16, 25]
                t3 = dv(lambda e: e.tensor_tensor(amag[:], bcast(lrdt[:], sh3, 2), bcast(k25[:], sh3, 1), ALU.mult), [t1])
                t4 = dv(lambda e: e.tensor_tensor(aph[:], bcast(th[:], sh3, 2), bcast(k25[:], sh3, 1), ALU.mult), [t2])
                t5 = P.op("act", lambda e: e.activation(amag[:], amag[:], AF.Exp), [t3])

                def frac_sin(dst, src, shift, deps, tmp):
                    a = dv(lambda e: e.tensor_scalar(trn[:] if tmp is None else tmp, src, 1.0 / TWO_PI, shift, ALU.mult, ALU.add), deps)
                    tt = trn[:] if tmp is None else tmp
                    b_ = dv(lambda e: e.tensor_scalar(dst, tt, MAGIC, MAGIC, ALU.add, ALU.subtract), [a])
                    c_ = dv(lambda e: e.tensor_tensor(dst, tt, dst, ALU.subtract), [b_])
                    return P.op("act", lambda e: e.activation(dst, dst, AF.Sin, scale=TWO_PI), [c_])
                t6 = frac_sin(frs[:], aph[:], 0.0, [t4], None)
                t7 = frac_sin(frc[:], aph[:], 0.25, [t6], None)
                t8 = dv(lambda e: e.tensor_tensor(apw[:, 0, :, :], amag[:], frc[:], ALU.mult), [t5, t7])
                t9 = dv(lambda e: e.tensor_tensor(apw[:, 1, :, :], amag[:], frs[:], ALU.mult), [t5, t7])
                AP_ = [t8, t9]
                t10 = dv(lambda e: e.tensor_copy(Rr[:], amag[:, :, 16]), [t5])
                t11 = dv(lambda e: e.tensor_scalar(THT[:], th[:], 8.0 / TWO_PI, None, ALU.mult), [t2])
                ar1 = apw[:, 0, :, 9]; ai1 = apw[:, 1, :, 9]
                lr = lam[:, 0, :]; li = lam[:, 1, :]
                a1 = dv(lambda e: e.tensor_scalar(s1[:], ar1, -1.0, None, ALU.add), AP_)
                a2 = dv(lambda e: e.tensor_tensor(s2[:], s1[:], lr, ALU.mult), [a1])
                a3 = dv(lambda e: e.tensor_tensor(s3[:], ai1, li, ALU.mult), AP_)
                a4 = dv(lambda e: e.tensor_tensor(qq[:, 0, :], s2[:], s3[:], ALU.add), [a2, a3])
                a5 = dv(lambda e: e.tensor_tensor(s2[:], ai1, lr, ALU.mult), [a4])
                a6 = dv(lambda e: e.tensor_tensor(s3[:], s1[:], li, ALU.mult), [a4])
                a7 = dv(lambda e: e.tensor_tensor(qq[:, 1, :], s2[:], s3[:], ALU.subtract), [a5, a6])
                a8 = dv(lambda e: e.tensor_tensor(s2[:], lr, lr, ALU.mult), [a7])
                a9 = dv(lambda e: e.tensor_tensor(s3[:], li, li, ALU.mult), [a7])
                a10 = dv(lambda e: e.tensor_tensor(s2[:], s2[:], s3[:], ALU.add), [a8, a9])
                a11 = dv(lambda e: e.reciprocal(s2[:], s2[:]), [a10])
                sh2 = [128, 2, 16]
                a12 = dv(lambda e: e.tensor_tensor(qq[:], qq[:], bcast(s2[:], sh2, 1), ALU.mult), [a11])
                shb = [128, 16, 16]
                qr = bcast(qq[:, 0, :], shb, 2); qi = bcast(qq[:, 1, :], shb, 2)
                b1 = dv(lambda e: e.tensor_tensor(tA[:, :, 0, :], qr, Bn[:, 0, :, :], ALU.mult), [a12])
                b2 = dv(lambda e: e.tensor_tensor(tB[:, :, 0, :], qi, Bn[:, 1, :, :], ALU.mult), [a12])
                b3 = dv(lambda e: e.tensor_tensor(Bb[:, 0, :, :], tA[:, :, 0, :], tB[:, :, 0, :], ALU.subtract), [b1, b2])
                b4 = dv(lambda e: e.tensor_tensor(tA[:, :, 0, :], qr, Bn[:, 1, :, :], ALU.mult), [b3])
                b5 = dv(lambda e: e.tensor_tensor(tB[:, :, 0, :], qi, Bn[:, 0, :, :], ALU.mult), [b3])
                b6 = dv(lambda e: e.tensor_tensor(Bb[:, 1, :, :], tA[:, :, 0, :], tB[:, :, 0, :], ALU.add), [b4, b5])
                BB = [b3, b6]
                sh4 = [128, 16, 8, 16]

                def cmul(dr, di, pk0, M, deps, neg_i=False):
                    pr_ = bcast(apw[:, 0, :, pk0:pk0 + 8], sh4, 3); pi_ = bcast(apw[:, 1, :, pk0:pk0 + 8], sh4, 3)
                    mr = bcast(M[:, 0, :, :], sh4, 2); mi = bcast(M[:, 1, :, :], sh4, 2)
                    c1 = dv(lambda e: e.tensor_tensor(tA[:], pr_, mr, ALU.mult), deps)
                    c2 = dv(lambda e: e.tensor_tensor(tB[:], pi_, mi, ALU.mult), deps)
                    c3 = dv(lambda e: e.tensor_tensor(dr, tA[:], tB[:], ALU.subtract), [c1, c2])
                    c4 = dv(lambda e: e.tensor_tensor(tA[:], pr_, mi, ALU.mult), [c3])
                    c5 = dv(lambda e: e.tensor_tensor(tB[:], pi_, mr, ALU.mult), [c3])
                    if neg_i:
                        c6 = dv(lambda e: e.scalar_tensor_tensor(di, tA[:], -1.0, tB[:], ALU.mult, ALU.subtract), [c4, c5])
                    else:
                        c6 = dv(lambda e: e.tensor_tensor(di, tA[:], tB[:], ALU.add), [c4, c5])
                    return c6
                tz = cmul(Zr[:], Zi[:], 17, Bb, AP_ + BB)
                tb_ = tz
                t_bt = tb_
                for r, Zs in enumerate((Zr, Zi)):
                    for p4 in range(4):
                        bk, bd = P.bank()
                        t_mm = None
                        for pp in range(4):
                            pr = p4 * 4 + pp
                            t_mm = P.op("pe", lambda e, bk=bk, pp=pp, pr=pr, Zs=Zs: e.transpose(
                                ps[:, bk, pp * 128:(pp + 1) * 128], Zs[:, pr, :, :].rearrange("q s h -> q (s h)"), identf[:]),
                                ([tb_] + bd) if pp == 0 else [], inc=(pp == 3))
                        t_bt = P.op("act", lambda e, bk=bk, r=r, p4=p4: e.activation(
                            BT[:, r, p4 * 4:(p4 + 1) * 4, :], ps[:, bk, :].rearrange("p (g m) -> p g m", m=128), AF.Copy), [t_mm])
                        P.release(bk, t_bt)
                tx = cmul(Xr[:], nXi[:], 0, Cl, [tz, t_cl], neg_i=True)
                m1 = dv(lambda e: e.memset(mk[:], 0.0), [])
                m2 = dv(lambda e: e.memset(mk[0:64, 0:1], 1.0), [m1])
                m3 = dv(lambda e: e.memset(mk[64:128, 1:2], 1.0), [m1])
                MK = [m2, m3]
                t_k = tx
                par_rd = None
                for e_ in range(2):
                    x1_ = dv(lambda e, e_=e_: e.tensor_scalar(tB[:].rearrange("q a s h -> q (a s h)"), Xr[:].rearrange("q a s h -> q (a s h)"),
                                                              mk[:, e_:e_ + 1], None, ALU.mult), [tx, par_rd] + MK)
                    x2_ = dv(lambda e, e_=e_: e.tensor_scalar(tC[:].rearrange("q a s h -> q (a s h)"), nXi[:].rearrange("q a s h -> q (a s h)"),
                                                              mk[:, e_:e_ + 1], None, ALU.mult), [tx, par_rd] + MK)
                    for p4 in range(4):
                        bk, bd = P.bank()
                        t_mm = None
                        for pp in range(4):
                            pr = p4 * 4 + pp
                            P.op("pe", lambda e, bk=bk, pp=pp, pr=pr: e.matmul(
                                ps[:, bk, pp * 128:(pp + 1) * 128], Zr[:, pr, :, :].rearrange("q s h -> q (s h)"),
                                tB[:, pr, :, :].rearrange("q s h -> q (s h)"), start=True, stop=False),
                                ([x1_, x2_] + bd) if pp == 0 else [], inc=False)
                            t_mm = P.op("pe", lambda e, bk=bk, pp=pp, pr=pr: e.matmul(
                                ps[:, bk, pp * 128:(pp + 1) * 128], Zi[:, pr, :, :].rearrange("q s h -> q (s h)"),
                                tC[:, pr, :, :].rearrange("q s h -> q (s h)"), start=False, stop=True), [], inc=(pp == 3))
                        t_k1 = dv(lambda e, bk=bk: e.tensor_tensor(
                            ktmp[:, 0:512].rearrange("p (g m) -> p g m", m=128), ps[:, bk, :].rearrange("p (g m) -> p g m", m=128),
                            bcast(maskf[:], [128, 4, 128], 1), ALU.mult), [t_mm, LD, t_k])
                        P.release(bk, t_k1)
                        for pp in range(4):
                            g = 2 * (p4 * 4 + pp) + e_
                            t_k = dv(lambda e, g=g, pp=pp: e.scalar_tensor_tensor(
                                Kin[:, g, :], identf[:], dcol[:, g:g + 1], ktmp[:, pp * 128:(pp + 1) * 128], ALU.mult, ALU.add), [t_k1, C0])
                    par_rd = t_mm
                tc_ = cmul(Xr[:], nXi[:], 9, Cl, [t_bt, t_k], neg_i=True)
                cws = []
                for e_ in range(2):
                    xrf = Xr[:].rearrange("q a s h -> q a (s h)"); nxf = nXi[:].rearrange("q a s h -> q a (s h)")
                    cws.append(dv(lambda e, e_=e_, xrf=xrf: e.tensor_scalar(Cw[:, 0, e_:32:2, :], xrf, mk[:, e_:e_ + 1], None, ALU.mult), [tc_] + MK))
                    cws.append(dv(lambda e, e_=e_, xrf=xrf: e.tensor_scalar(Cw[:, 1, e_:32:2, :], xrf, mk[:, e_:e_ + 1], -1.0, ALU.mult, ALU.mult), [tc_] + MK))
                    cws.append(dv(lambda e, e_=e_, nxf=nxf: e.tensor_scalar(Cw[:, 2, e_:32:2, :], nxf, mk[:, e_:e_ + 1], None, ALU.mult), [tc_] + MK))
                c1 = c2 = c3 = cws[-1]
                SETUP = [c1, c2, c3, t_bt, t_k, t10, t11, t_h0]
                dump("Kin", Kin[:], [128, 32, 128], SETUP)
                dump("BT", BT[:], [128, 2, 16, 128], SETUP)
                dump("Cw", Cw[:], [128, 3, 32, 128], SETUP)
                dump("apw", apw[:], [128, 2, 16, 25], SETUP)
                dump("h0", h0[:], [128, 2, 64], SETUP)
                if stop == "SU":
                    P.op("sp", lambda e: e.nop(), list(out_tok) + list(dbg_o.values()), inc=False)
                P.emit()
            if stop == "SU":
                pb.close(); zs.close()
                return nc

            sc = ExitStack()
            with sc:
                NB = 4
                E_b = sb("E_b", [128, 64, 128], BF16, sc)
                jtab = sb("jtab", [128, NCH], F32, sc); m0tab = sb("m0tab", [128, NCH], F32, sc)
                t_E = P.dma("sp", lambda e: e.dma_start(out=E_b[:], in_=E_d.rearrange("p (a m) -> p a m", m=128)), slot="s5c0")
                t_j = P.dma("sp", lambda e: e.dma_start(out=jtab[:], in_=j_d), slot="s5c1")
                t_m0 = P.dma("sp", lambda e: e.dma_start(out=m0tab[:], in_=m0_d), slot="s5c2")

                U = sb("U", [128, 2, 8, NCH], BF16, sc)
                ygl = sb("ygl", [128, 8, NCH], BF16, sc)
                cs = sb("cs", [128, 2, NB, NCH], F32, sc)
                Rm = sb("Rm", [128, NB, NCH], F32, sc)
                trs = sb("trs", [128, NB, NCH], F32, sc); tr2 = sb("tr2", [128, NB, NCH], F32, sc)
                W = sb("W", [128, 2, NB, NCH], F32, sc)
                G = sb("G", [128, 2, NB, NCH], F32, sc)
                w1 = trs; w2 = tr2
                PP = sb("PP", [128, 4, NB, NCH], BF16, sc)
                e0 = sb("e0", [128, 2, NB, NSQ], F32, sc); e1 = sb("e1", [128, NB, NSQ], F32, sc)
                ss5 = sb("ss5", [128, NTILE], F32, sc)
                hl1 = sb("hl1", [128, NB, 5], F32, sc); hl2 = sb("hl2", [128, NB, 5], F32, sc)

                def dv(fn, deps):
                    return P.op("dve", fn, deps)
                shn = [128, NB, NCH]
                prev_batch = []
                yT_w = []
                UEV = {}
                y_done = {}

                def shuf_in(bt):
                    ub = bt % 2
                    free = [y_done.get(bt - 2)]
                    evs = []
                    for gi in range(8):
                        bk, bd = P.bank()
                        t_mm = None
                        for s_ in range(8):
                            t_mm = P.op("pe", lambda e, bk=bk, gi=gi, s_=s_, bt=bt: e.matmul(
                                ps[:, bk, 0:NCH], E_b[:, gi * 8 + s_, :], zT[:, bt, s_, :], start=(s_ == 0), stop=(s_ == 7)),
                                (z_ev + [t_E] + bd) if s_ == 0 else [], inc=(s_ == 7))
                        t_e = P.op("act", lambda e, bk=bk, gi=gi, ub=ub: e.activation(U[:, ub, gi, :], ps[:, bk, 0:NCH], AF.Copy), [t_mm] + free)
                        P.release(bk, t_e)
                        evs.append(t_e)
                    return evs
                B = {}

                def prevd(bt, *keys):
                    out = []
                    if bt - 1 in B:
                        for k in keys:
                            v = B[bt - 1].get(k)
                            if v is None:
                                continue
                            out += v if isinstance(v, list) else [v]
                    return out

                def s1a(bt):
                    p0 = bt * NB
                    ub = bt % 2
                    D_ = B.setdefault(bt, {})
                    dprev = prevd(bt, "pd", "fin", "w_done", "E0", "g_done", "inj")
                    a = dv(lambda e, p0=p0: e.tensor_tensor(trs[:], bcast(THT[:, p0:p0 + NB], shn, 2), bcast(jtab[:], shn, 1), ALU.mult),
                           SETUP + [t_j] + dprev)
                    b_ = dv(lambda e: e.tensor_scalar(tr2[:], trs[:], MAGIC, MAGIC, ALU.add, ALU.subtract), [a])
                    c_ = dv(lambda e: e.tensor_tensor(tr2[:], trs[:], tr2[:], ALU.subtract), [b_])
                    t_sin = P.op("act", lambda e: e.activation(cs[:, 1, :, :], tr2[:], AF.Sin, scale=TWO_PI), [c_] + dprev)
                    a2 = dv(lambda e: e.tensor_scalar(trs[:], trs[:], 0.25, None, ALU.add), [c_])
                    b2 = dv(lambda e: e.tensor_scalar(tr2[:], trs[:], MAGIC, MAGIC, ALU.add, ALU.subtract), [a2, t_sin])
                    c2_ = dv(lambda e: e.tensor_tensor(tr2[:], trs[:], tr2[:], ALU.subtract), [b2])
                    t_cos = P.op("act", lambda e: e.activation(cs[:, 0, :, :], tr2[:], AF.Sin, scale=TWO_PI), [c2_])
                    t_rm = dv(lambda e, p0=p0: e.tensor_tensor(Rm[:], bcast(Rr[:, p0:p0 + NB], shn, 2), bcast(m0tab[:], shn, 1), ALU.mult),
                              SETUP + [t_m0] + dprev)
                    TAB = [t_sin, t_cos, t_rm]
                    hre = h0[:, 0, :].rearrange("q (s a) -> q a s", a=16)[:, p0:p0 + NB, :]
                    him = h0[:, 1, :].rearrange("q (s a) -> q a s", a=16)[:, p0:p0 + NB, :]
                    shq = [128, NB, NSQ]
                    c1b = bcast(cs[:, 0, :, 2], shq, 2); s1b_ = bcast(cs[:, 1, :, 2], shq, 2)
                    x1 = dv(lambda e: e.tensor_tensor(e0[:, 0, :, :], c1b, hre, ALU.mult), TAB + dprev)
                    x2 = dv(lambda e: e.tensor_tensor(e1[:], s1b_, him, ALU.mult), TAB + dprev)
                    x3 = dv(lambda e: e.tensor_tensor(e0[:, 0, :, :], e0[:, 0, :, :], e1[:], ALU.subtract), [x1, x2])
                    x4 = dv(lambda e: e.tensor_tensor(e0[:, 1, :, :], s1b_, hre, ALU.mult), TAB + dprev)
                    x5 = dv(lambda e: e.tensor_tensor(e1[:], c1b, him, ALU.mult), [x3])
                    x6 = dv(lambda e: e.tensor_tensor(e0[:, 1, :, :], e0[:, 1, :, :], e1[:], ALU.add), [x4, x5])
                    E0 = [x3, x6]
                    u_ev = UEV[bt]
                    w_done = []
                    for pp in range(NB):
                        pr = p0 + pp
                        bk2, bd = P.bank2()
                        t_mm = None
                        for r in range(2):
                            for e_ in range(2):
                                t_mm = P.op("pe", lambda e, bk2=bk2, r=r, e_=e_, pr=pr, pp=pp: e.matmul(
                                    ps[e_ * 64:(e_ + 1) * 64, bk2 + r, 0:NCH], BT[:, r, pr, e_ * 64:(e_ + 1) * 64], U[:, ub, 2 * pp + e_, :],
                                    start=True, stop=True), ([u_ev[2 * pp], u_ev[2 * pp + 1]] + SETUP + bd) if (r == 0 and e_ == 0) else [],
                                    inc=(r == 1 and e_ == 1))
                        sre = ps[:, bk2, 0:NCH]; sim = ps[:, bk2 + 1, 0:NCH]
                        co = cs[:, 0, pp, :]; si = cs[:, 1, pp, :]
                        r1 = dv(lambda e, pp=pp, sre=sre, co=co: e.tensor_tensor(w1[:, pp, :], co, sre, ALU.mult), [t_mm] + TAB)
                        r2 = dv(lambda e, pp=pp, sim=sim, si=si: e.tensor_tensor(w2[:, pp, :], si, sim, ALU.mult), [t_mm] + TAB)
                        r3 = dv(lambda e, pp=pp: e.tensor_tensor(W[:, 0, pp, :], w1[:, pp, :], w2[:, pp, :], ALU.add), [r1, r2] + dprev)
                        r4 = dv(lambda e, pp=pp, sim=sim, co=co: e.tensor_tensor(w1[:, pp, :], co, sim, ALU.mult), [r3])
                        r5 = dv(lambda e, pp=pp, sre=sre, si=si: e.tensor_tensor(w2[:, pp, :], si, sre, ALU.mult), [r3])
                        P.release(bk2, r5, 2)
                        r6 = dv(lambda e, pp=pp: e.tensor_tensor(W[:, 1, pp, :], w1[:, pp, :], w2[:, pp, :], ALU.subtract), [r4, r5])
                        w_done += [r3, r6]
                    icols = slice(257, 257 + 36, 9)
                    i1 = dv(lambda e: e.tensor_copy(W[:, 0, :, icols], e0[:, 0, :, :]), w_done + E0)
                    i2 = dv(lambda e: e.tensor_copy(W[:, 1, :, icols], e0[:, 1, :, :]), w_done + E0)
                    g_done = []
                    for r in range(2):
                        for pp in range(NB):
                            g_done.append(dv(lambda e, r=r, pp=pp: e.tensor_tensor_scan(
                                G[:, r, pp, :], Rm[:, pp, :], W[:, r, pp, :], 0.0, ALU.mult, ALU.add), [i1, i2] + TAB + dprev))
                    D_.update(TAB=TAB, E0=E0, w_done=w_done, inj=[i1, i2], g_done=g_done, u_ev=u_ev)

                def s1b(bt):
                    p0 = bt * NB
                    D_ = B[bt]
                    g_done = D_["g_done"]
                    yprev = [y_done[bt - 1]] if (bt - 1) in y_done else []
                    pd = []
                    pd.append(dv(lambda e: e.tensor_tensor(PP[:, 0, :, :], cs[:, 0, :, :], G[:, 0, :, :], ALU.mult), g_done + yprev))
                    pd.append(dv(lambda e: e.tensor_tensor(PP[:, 1, :, :], cs[:, 1, :, :], G[:, 1, :, :], ALU.mult), g_done + yprev))
                    pd.append(dv(lambda e: e.tensor_tensor(PP[:, 2, :, :], cs[:, 1, :, :], G[:, 0, :, :], ALU.mult), g_done + yprev))
                    pd.append(dv(lambda e: e.tensor_tensor(PP[:, 3, :, :], cs[:, 0, :, :], G[:, 1, :, :], ALU.mult), g_done + yprev))
                    lsl = slice(256, NCH, 9)
                    fprev = prevd(bt, "fin")
                    f1 = dv(lambda e: e.tensor_tensor(hl1[:], cs[:, 0, :, lsl], G[:, 0, :, lsl], ALU.mult), g_done + fprev)
                    f2 = dv(lambda e: e.tensor_tensor(hl2[:], cs[:, 1, :, lsl], G[:, 1, :, lsl], ALU.mult), g_done + fprev)
                    f3 = dv(lambda e, p0=p0: e.tensor_tensor(HL[:, 0, :, p0:p0 + NB].rearrange("q s a -> q a s"), hl1[:], hl2[:], ALU.subtract), [f1, f2])
                    f4 = dv(lambda e: e.tensor_tensor(hl1[:], cs[:, 1, :, lsl], G[:, 0, :, lsl], ALU.mult), [f3])
                    f5 = dv(lambda e: e.tensor_tensor(hl2[:], cs[:, 0, :, lsl], G[:, 1, :, lsl], ALU.mult), [f3])
                    f6 = dv(lambda e, p0=p0: e.tensor_tensor(HL[:, 1, :, p0:p0 + NB].rearrange("q s a -> q a s"), hl1[:], hl2[:], ALU.add), [f4, f5])
                    D_.update(pd=pd, fin=[f6])

                def s2(bt):
                    p0 = bt * NB
                    ub = bt % 2
                    D_ = B[bt]
                    pd = D_["pd"]
                    soprev = prevd(bt, "t_so")
                    y_ev = []
                    t_mm = None
                    for gi in range(8):
                        pp = gi // 2
                        bk, bd = P.bank()
                        P.op("pe", lambda e, bk=bk, gi=gi: e.matmul(
                            ps[:, bk, 1:NCH], Kin[:, bt * 8 + gi, :], U[:, ub, gi, 1:NCH], start=True, stop=False), pd + bd + SETUP, inc=False)
                        for k, (ci, pi) in enumerate(((0, 0), (1, 1), (2, 2), (2, 3))):
                            t_mm = P.op("pe", lambda e, bk=bk, ci=ci, pi=pi, gg_=bt * 8 + gi, pp=pp, k=k: e.matmul(
                                ps[:, bk, 1:NCH], Cw[:, ci, gg_, :], PP[:, pi, pp, 0:NCH - 1], start=False, stop=(k == 3)), [], inc=(k == 3))
                        t_e = P.op("act", lambda e, bk=bk, gi=gi: e.activation(ygl[:, gi, 1:NCH], ps[:, bk, 1:NCH], AF.Gelu_apprx_tanh), [t_mm] + soprev)
                        P.release(bk, t_e)
                        y_ev.append(t_e)
                    y_done[bt] = t_mm
                    last = []
                    for t_ in range(8):
                        bk, bd = P.bank()
                        for gi in range(8):
                            t_mm = P.op("pe", lambda e, bk=bk, gi=gi, t_=t_: e.matmul(
                                ps[:, bk, 1:NCH], E_b[:, t_ * 8 + gi, :], ygl[:, gi, 1:NCH], start=(gi == 0), stop=(gi == 7)),
                                (y_ev + bd) if gi == 0 else [], inc=(gi == 7))
                        t_e = P.op("act", lambda e, bk=bk, t_=t_: e.activation(yT[:, bt, 8 + t_:TOKP:8], ps[:, bk, 1:NCH], AF.Copy), [t_mm])
                        P.release(bk, t_e)
                        last.append(t_e)
                    D_.update(y_ev=y_ev, last=last, t_so=[t_mm])
                    return last

                UEV[0] = shuf_in(0)
                s1a(0)
                s1b(0)
                UEV[1] = shuf_in(1)
                for bt in range(4):
                    if bt + 1 < 4:
                        s1a(bt + 1)
                    yT_w += s2(bt)
                    if bt + 1 < 4:
                        s1b(bt + 1)
                    if bt + 2 < 4:
                        UEV[bt + 2] = shuf_in(bt + 2)
                prev_batch = []
                for bt in range(4):
                    for k in ("last", "fin", "t_so", "pd", "g_done", "w_done"):
                        prev_batch += B[bt][k]
                dump("yT", yT[:], [128, 4, TOKP + 256], yT_w)
                dump("HL", HL[:], [128, 2, 5, 16], prev_batch)
                hlo = trs[:].rearrange("p a c -> p (a c)")[:, 0:256].rearrange("p (r m) -> p r m", m=128)
                bkh, bd = P.bank()
                t_h = None
                for r in range(2):
                    t_h = P.op("pe", lambda e, r=r: e.transpose(ps[0:80, bkh, r * 128:(r + 1) * 128], HL[:, r, :, :].rearrange("q s a -> q (s a)"), identf[:]),
                               prev_batch + bd)
                t_ho = P.op("act", lambda e: e.activation(hlo[0:80, :, :], ps[0:80, bkh, 0:256].rearrange("p (r m) -> p r m", m=128), AF.Copy), [t_h])
                P.release(bkh, t_ho)
                out_tok.append(P.dma("sp", lambda e: e.dma_start(out=hre_o, in_=hlo[0:80, 0, :]), [t_ho], slot="outs"))
                out_tok.append(P.dma("sp", lambda e: e.dma_start(out=him_o, in_=hlo[0:80, 1, :]), [t_ho], slot="outs"))

                wglu_b = W[:].rearrange("p a b c -> p (a b c)").bitcast(BF16)[:, 0:2048].rearrange("p (k n) -> p k n", n=512)
                t_wglu = P.dma("pool", lambda e: e.dma_start(out=wglu_b, in_=wglu_d.rearrange("(kt p) n -> p kt n", p=128)), prev_batch, slot="wglu")
                sg = PP[:].rearrange("p a b c -> p (a b c)")[:, 0:4096].rearrange("p (u m n) -> p u m n", m=4, n=512)
                ysq5 = G[:].rearrange("p a b c -> p (a b c)").bitcast(BF16)[:, 0:4096].rearrange("p (u m n) -> p u m n", m=4, n=512)
                glu_ev = []
                bks, bds = P.bank()
                G1 = {}
                TST = {}
                SGEV = {}

                def glu_gate(b):
                    n = 512 if b < 4 else 256
                    u = b % 2
                    free = G1.get(b - 2, [])
                    evs = []
                    for m in range(4):
                        bk, bd = P.bank()
                        if bk == bks:
                            bk, bd = P.bank()
                        t_mm = None
                        for kt in range(4):
                            rhs, _ = blk_cols(yT, kt, b)
                            t_mm = P.op("pe", lambda e, bk=bk, m=m, kt=kt, rhs=rhs, n=n, b=b: e.matmul(
                                ps_cols(bk, n, b), wglu_b[:, kt, m * 128:(m + 1) * 128], rhs, start=(kt == 0), stop=(kt == 3)),
                                (yT_w + [t_wglu] + bd) if kt == 0 else [], inc=(kt == 3))
                        t_e = P.op("act", lambda e, bk=bk, m=m, n=n, u=u: e.activation(sg[:, u, m, 0:n], ps[:, bk, 0:n], AF.Sigmoid),
                                   [t_mm] + free + prev_batch)
                        P.release(bk, t_e)
                        evs.append(t_e)
                    SGEV[b] = evs

                def glu_dve(b):
                    n = 512 if b < 4 else 256
                    u = b % 2
                    blk_ev = []
                    g1s = []
                    tst_prev = [TST[b - 2]] if (b - 2) in TST else []
                    for m in range(4):
                        src, _ = blk_cols(yT, m, b)
                        dst = oblk(y2T, m, b)
                        dst3 = oblk(y2T, m, b, three=True)
                        sgv = sg[:, u, m, 0:n] if b < 4 else sg[:, u, m, 0:256].rearrange("p (q c) -> p q c", c=64)
                        g1 = dv(lambda e, src=src, dst3=dst3, sgv=sgv: e.tensor_tensor(dst3, src, sgv, ALU.mult), SGEV[b])
                        g2 = dv(lambda e, dst=dst, m=m, n=n, u=u: e.tensor_tensor(ysq5[:, u, m, 0:n], dst, dst, ALU.mult), [g1] + tst_prev + prev_batch)
                        g3 = P.op("act", lambda e, dst=dst, m=m: e.activation(dst, dst, AF.Copy, scale=colv[:, 1, m:m + 1]), [g2, C0, C1])
                        glu_ev.append(g3)
                        blk_ev.append(g2)
                        g1s.append(g1)
                    G1[b] = g1s
                    return blk_ev

                def glu_stats(b, blk_ev, tile0):
                    n = 512 if b < 4 else 256
                    nt = n // 128
                    u = b % 2
                    t_st = None
                    for ti in range(nt):
                        i = tile0 + ti
                        for j in range(4):
                            t_st = P.op("pe", lambda e, i=i, j=j, ti=ti, u=u: e.matmul(
                                ps[:, bks, i:i + 1], ysq5[:, u, j, ti * 128:(ti + 1) * 128], ones_b[:, 0:1], start=(j == 0), stop=(j == 3)),
                                (blk_ev + bds + [C0]) if (ti == 0 and j == 0) else [], inc=(ti == nt - 1 and j == 3))
                    TST[b] = t_st
                    return t_st

                glu_gate(0)
                tile0 = 0
                t_st = None
                for b in range(5):
                    if b + 1 < 5:
                        glu_gate(b + 1)
                    bev = glu_dve(b)
                    t_st = glu_stats(b, bev, tile0)
                    tile0 += 4 if b < 4 else 2
                t_rs5 = rstd_chain(ps[:, bks, 0:NTILE], rstd_s5[:], ss5[:], 1.0 / 512, [t_st])
                P.release(bks, t_rs5)
                S5_DONE = glu_ev + [t_rs5]
                dump("y2T", y2T[:], [128, 4, TOKP + 256], S5_DONE)
                dump("rstd_s5", rstd_s5[:], [128, NTILE], S5_DONE)
                if stop == "B":
                    P.op("sp", lambda e: e.nop(), list(out_tok) + list(dbg_o.values()), inc=False)
                P.emit()
            if stop == "B":
                pb.close(); zs.close()
                return nc
        zs.close()
        sis.close()
        pc = ExitStack()
        with pc:
            wout_b = sb("wout_b", [128, 8, D], BF16, pc); wgate_b = sb("wgate_b", [128, 8, D], BF16, pc)
            wple_b = sb("wple_b", [128, 2, D], BF16, pc)
            wres = []
            for kt in range(0, 8, 4):
                wres.append(P.dma("sp", lambda e, kt=kt: e.dma_start(
                    out=wout_b[:, kt:kt + 4, :], in_=wout16.rearrange("(kt p) n -> p kt n", p=128)[:, kt:kt + 4, :]), [WCAST], slot="wout"))
            WOUT = wres[-1]
            xr = sb("xr", [128, 2, 4, D], F32, pc)
            ssC = sb("ssC", [128, 4 * NTILE], F32, pc); tmpC = sb("tmpC", [128, 4 * NTILE], F32, pc); rsC = sb("rsC", [128, 4 * NTILE], F32, pc)
            xnc = sb("xnc", [128, 2, D], BF16, pc)
            xnT2 = sb("xnT2", [128, 8, 512], BF16, pc)
            xnT3 = sb("xnT3", [128, 2, 8, 128], BF16, pc)
            aT = sb("aT", [128, 32, 512], BF16, pc)
            wupb = sb("wupb", [128, 3, 8, 256], BF16, pc)
            wdnb = sb("wdnb", [128, 3, 4, 512], BF16, pc)
            pb16 = sb("pb16", [128, 4, 256], BF16, pc)
            pT = sb("pT", [128, 2, 512], BF16, pc)
            sgc = sb("sgc", [128, 2, D], F32, pc)
            t_ssc = P.op("dve", lambda e: e.memset(ssC[:], 0.0))
            wup_r = wup16.rearrange("(kt p) n -> p kt n", p=128)
            wdn_r = wdn16.rearrange("(f p) n -> p f n", p=128)

            class St:
                pass
            st = St()
            st.wup_rd = [None] * 3; st.wdn_rd = [None] * 3
            st.nup = 0; st.ndn = 0
            st.xr_free = [[None] * 4, [None] * 4]
            st.xnc_rd = [None] * 2
            st.xnc_sq = [None] * 2
            st.xnT2_rd = None; st.xnT3_rd = [None, None]; st.aT_rd = None; st.pT_rd = None; st.pb_rd = [None] * 4
            st.sgc_rd = [None, None]
            st.wq = []

            blocks = []
            t0_ = 0
            for b in range(5):
                nt = 4 if b < 4 else 2
                blocks.append((b, nt, list(range(t0_, t0_ + nt))))
                t0_ += nt

            def take(bk):
                return list(P.bank_tok[bk])

            def take2(bk):
                return list(P.bank_tok[bk]) + list(P.bank_tok[bk + 1])

            pieces = []
            for (b, nt, tiles) in blocks:
                for pc_ in range(16):
                    pieces.append(("up", b, pc_))
                for h in range(2):
                    for pc_ in range(8):
                        pieces.append(("dn", b, h, pc_))
            st.pidx = 0
            st.ptok = {}

            def issue_piece():
                if st.pidx >= len(pieces):
                    return
                pz = pieces[st.pidx]
                st.pidx += 1
                if pz[0] == "up":
                    u = st.nup % 3
                    st.nup += 1
                    pc_ = pz[2]
                    t = P.dma("pool", lambda e, u=u, pc_=pc_: e.dma_start(out=wupb[:, u, :, :], in_=wup_r[:, :, pc_ * 256:(pc_ + 1) * 256]),
                              [st.wup_rd[u], WCAST], slot="wup%d" % u)
                    st.ptok[pz] = (t, u)
                else:
                    u = st.ndn % 3
                    st.ndn += 1
                    h, pc_ = pz[2], pz[3]
                    t = P.dma("pool", lambda e, u=u, pc_=pc_, h=h: e.dma_start(
                        out=wdnb[:, u, :, :], in_=wdn_r[:, pc_ * 4:(pc_ + 1) * 4, h * 512:(h + 1) * 512]), [st.wdn_rd[u], WCAST], slot="wdn%d" % u)
                    st.ptok[pz] = (t, u)

            def load_tile(blk, ti):
                b, nt, tiles = blk
                i = tiles[ti]
                return P.dma("sp", lambda e, ti=ti, i=i, b=b: e.dma_start(out=xr[:, b % 2, ti, :], in_=x_d[i * 128:(i + 1) * 128, :]),
                             [st.xr_free[b % 2][ti]], slot="xr%d_%d" % (b % 2, ti))

            def load_block(blk):
                return [load_tile(blk, ti) for ti in range(blk[1])]

            def load_p(blk):
                b, nt, tiles = blk
                tp = []
                for ti, i in enumerate(tiles):
                    tp.append(P.dma("pool", lambda e, ti=ti, i=i: e.dma_start(out=pb16[:, ti, :], in_=p_d[i * 128:(i + 1) * 128, :]),
                                    [st.pb_rd[ti]], slot="pb%d" % ti))
                return tp

            def p_transposes(blk, tp, bk):
                b, nt, tiles = blk
                evs = []
                for ti, i in enumerate(tiles):
                    bd = take(bk)
                    t_tr = None
                    for j in range(2):
                        t_tr = P.op("pe", lambda e, j=j, ti=ti: e.transpose(
                            psb[:, bk, j * 128:(j + 1) * 128], pb16[:, ti, j * 128:(j + 1) * 128], identb[:]),
                            ([tp[ti], C0] + bd) if j == 0 else [], inc=(j == 1))
                    t_e = P.op("act", lambda e, ti=ti: e.activation(
                        pT[:, :, ti * 128:(ti + 1) * 128], psb[:, bk, 0:256].rearrange("p (k c) -> p k c", c=128), AF.Copy), [t_tr, st.pT_rd])
                    P.release(bk, t_e)
                    st.pb_rd[ti] = t_tr
                    evs.append(t_e)
                return evs

            def s1a(blk, ti, tx, bk2):
                b, nt, tiles = blk
                i = tiles[ti]
                bd = take2(bk2)
                t_a = None
                for h in range(2):
                    for kt in range(4):
                        t_a = P.op("pe", lambda e, h=h, kt=kt, i=i: e.matmul(
                            ps[:, bk2 + h, :], otile(y2T, kt, i), wout_b[:, kt, h * 512:(h + 1) * 512],
                            start=(kt == 0), stop=(kt == 3)), (S5_DONE + POOL_DONE + [WOUT] + bd) if (h == 0 and kt == 0) else [],
                            inc=(h == 1 and kt == 3))
                pa_ = ps[:, bk2:bk2 + 2, :].rearrange("p a c -> p (a c)")
                t1 = P.op("dve", lambda e, ti=ti, i=i, b=b: e.scalar_tensor_tensor(
                    xr[:, b % 2, ti, :], pa_, rstd_s5[:, i:i + 1], xr[:, b % 2, ti, :], ALU.mult, ALU.add), [t_a, tx[ti]])
                P.release(bk2, t1, 2)
                return t1

            def s1b(blk, ti, t1, bk2):
                b, nt, tiles = blk
                i = tiles[ti]
                bd = take2(bk2)
                t_b = None
                for h in range(2):
                    for kt in range(4):
                        t_b = P.op("pe", lambda e, h=h, kt=kt, i=i: e.matmul(
                            ps[:, bk2 + h, :], otile(ypT, kt, i), wout_b[:, 4 + kt, h * 512:(h + 1) * 512],
                            start=(kt == 0), stop=(kt == 3)), bd if (h == 0 and kt == 0) else [], inc=(h == 1 and kt == 3))
                pb_ = ps[:, bk2:bk2 + 2, :].rearrange("p a c -> p (a c)")
                t2 = P.op("dve", lambda e, ti=ti, i=i, b=b: e.scalar_tensor_tensor(
                    xr[:, b % 2, ti, :], pb_, rstd_po[:, i:i + 1], xr[:, b % 2, ti, :], ALU.mult, ALU.add), [t_b, t1])
                P.release(bk2, t2, 2)
                return t2

            def norm_a1(x_ap, k, xn_ap, deps, xn_free):
                t_sq = P.op("dve", lambda e: e.scalar_tensor_tensor(xn_ap, x_ap, 1.0, x_ap, ALU.mult, ALU.mult, accum_out=ssC[:, k:k + 1]),
                            deps + [t_ssc] + list(xn_free))
                return P.op("dve", lambda e: e.tensor_scalar(tmpC[:, k:k + 1], ssC[:, k:k + 1], 1.0 / D, EPS, ALU.mult, ALU.add), [t_sq])

            def norm_a2(x_ap, gk, k, xn_ap, t1):
                t2 = P.op("act", lambda e: e.activation(tmpC[:, k:k + 1], tmpC[:, k:k + 1], AF.Sqrt), [t1])
                t_r = P.op("dve", lambda e: e.reciprocal(rsC[:, k:k + 1], tmpC[:, k:k + 1]), [t2])
                return P.op("dve", lambda e: e.scalar_tensor_tensor(xn_ap, x_ap, rsC[:, k:k + 1], gbc3[:, gk - 1, :], ALU.mult, ALU.mult),
                            [t_r, C0, C1])

            def norm_a(x_ap, gk, k, xn_ap, deps, xn_free):
                return norm_a2(x_ap, gk, k, xn_ap, norm_a1(x_ap, k, xn_ap, deps, xn_free))

            def norm_b(xn_ap, dst, t_xn, bk, dst_free):
                bd = take(bk)
                t_tr = None
                for j in range(8):
                    t_tr = P.op("pe", lambda e, j=j: e.transpose(psb[:, bk, j * 128:(j + 1) * 128], xn_ap[:, j * 128:(j + 1) * 128], identb[:]),
                                ([t_xn] + bd) if j == 0 else [], inc=(j == 7))
                t_ev = P.op("act", lambda e: e.activation(dst, psb[:, bk, :].rearrange("p (k c) -> p k c", c=128), AF.Copy), [t_tr] + list(dst_free))
                P.release(bk, t_ev)
                return t_tr, t_ev

            def s4_gate(blk, ti, t_ev, pt_ev, bkg, bkp):
                b, nt, tiles = blk
                u = ti % 2
                bdg = take2(bkg)
                t_g = None
                for h in range(2):
                    for kt in range(8):
                        t_g = P.op("pe", lambda e, h=h, kt=kt, u=u: e.matmul(
                            ps[:, bkg + h, :], xnT3[:, u, kt, :], wgate_b[:, kt, h * 512:(h + 1) * 512],
                            start=(kt == 0), stop=(kt == 7)), ([t_ev, WRES] + bdg) if (h == 0 and kt == 0) else [], inc=(h == 1 and kt == 7))
                st.xnT3_rd[u] = t_g
                bdp = take2(bkp)
                t_pp = None
                for h in range(2):
                    for kt in range(2):
                        t_pp = P.op("pe", lambda e, h=h, kt=kt, ti=ti: e.matmul(
                            ps[:, bkp + h, :], pT[:, kt, ti * 128:(ti + 1) * 128], wple_b[:, kt, h * 512:(h + 1) * 512],
                            start=(kt == 0), stop=(kt == 1)), ([pt_ev[ti], WRES] + bdp) if (h == 0 and kt == 0) else [], inc=(h == 1 and kt == 1))
                st.pT_rd = t_pp
                return t_g, t_pp

            def s4_tail1(blk, ti, t_g, t_pp, t_xn, bkg, bkp):
                b, nt, tiles = blk
                i = tiles[ti]
                u = ti % 2
                xa = xr[:, b % 2, ti, :]
                t_sg = P.op("act", lambda e: e.activation(
                    sgc[:, u, :], ps[:, bkg:bkg + 2, :].rearrange("p a c -> p (a c)"), AF.Sigmoid), [t_g, st.sgc_rd[u]])
                P.release(bkg, t_sg, 2)
                t_m = P.op("dve", lambda e: e.tensor_tensor(
                    sgc[:, u, :], sgc[:, u, :], ps[:, bkp:bkp + 2, :].rearrange("p a c -> p (a c)"), ALU.mult), [t_pp, t_sg])
                P.release(bkp, t_m, 2)
                t_x3 = P.op("dve", lambda e: e.tensor_tensor(xa, xa, sgc[:, u, :], ALU.add), [t_m, t_xn])
                k = 4 * i + 2
                t_sq = P.op("dve", lambda e: e.scalar_tensor_tensor(xnc[:, ti % 2, :], xa, 1.0, xa, ALU.mult, ALU.mult, accum_out=ssC[:, k:k + 1]),
                            [t_x3, t_ssc, st.xnc_rd[ti % 2]])
                return P.op("dve", lambda e: e.tensor_scalar(tmpC[:, k:k + 1], ssC[:, k:k + 1], 1.0 / D, EPS, ALU.mult, ALU.add), [t_sq])

            def s4_tail2(blk, ti, t1):
                b, nt, tiles = blk
                i = tiles[ti]
                u = ti % 2
                xa = xr[:, b % 2, ti, :]
                k = 4 * i + 2
                t2 = P.op("act", lambda e: e.activation(tmpC[:, k:k + 1], tmpC[:, k:k + 1], AF.Sqrt), [t1])
                t_r = P.op("dve", lambda e: e.reciprocal(rsC[:, k:k + 1], tmpC[:, k:k + 1]), [t2])
                t_y = P.op("dve", lambda e: e.scalar_tensor_tensor(
                    sgc[:, u, :], xa, rsC[:, k:k + 1], gbc3[:, 2, :], ALU.mult, ALU.mult), [t_r, C0, C1])
                st.xr_free[b % 2][ti] = t_y
                t_o = P.dma("sp", lambda e: e.dma_start(out=y_o[i * 128:(i + 1) * 128, :], in_=sgc[:, u, :]), [t_y], slot="yo%d" % u)
                st.sgc_rd[u] = t_o
                out_tok.append(t_o)

            def s4_tail(blk, ti, t_g, t_pp, t_xn, bkg, bkp):
                s4_tail2(blk, ti, s4_tail1(blk, ti, t_g, t_pp, t_xn, bkg, bkp))

            for kt in range(0, 8, 4):
                wres.append(P.dma("sp", lambda e, kt=kt: e.dma_start(
                    out=wgate_b[:, kt:kt + 4, :], in_=wgate16.rearrange("(kt p) n -> p kt n", p=128)[:, kt:kt + 4, :]), [WCAST], slot="wres"))
            wres.append(P.dma("sp", lambda e: e.dma_start(
                out=wple_b[:], in_=wple16.rearrange("(kt p) n -> p kt n", p=128)), [WCAST], slot="wres"))
            WRES = wres[-1]
            for _ in range(3):
                issue_piece()
            TX = {0: dict(enumerate(load_block(blocks[0])))}
            X1 = {}
            N2 = {}

            def s1_full(blk, ti, part, ctx):
                b, nt, tiles = blk
                i = tiles[ti]
                if part == 0:
                    ctx["t1"] = s1a(blk, ti, TX[b], 0)
                elif part == 1:
                    ctx["t2"] = s1b(blk, ti, ctx["t1"], 6)
                elif part == 2:
                    ctx["a1"] = norm_a1(xr[:, b % 2, ti, :], 4 * i, xnc[:, ti % 2, :], [ctx["t2"]], [st.xnc_rd[ti % 2]])
                elif part == 3:
                    ctx["xn"] = norm_a2(xr[:, b % 2, ti, :], 1, 4 * i, xnc[:, ti % 2, :], ctx["a1"])
                else:
                    t_tr, t_ev = norm_b(xnc[:, ti % 2, :], xnT2[:, :, ti * 128:(ti + 1) * 128], ctx["xn"], 0, [st.xnT2_rd])
                    st.xnc_rd[ti % 2] = t_tr
                    N2.setdefault(b, []).append(t_ev)
                    X1.setdefault(b, {})[ti] = ctx["t2"]

            b0 = blocks[0]
            c0x = [{} for _ in range(b0[1])]
            for t0p in range(0, b0[1], 2):
                for part in range(5):
                    for ti in range(t0p, min(t0p + 2, b0[1])):
                        s1_full(b0, ti, part, c0x[ti])

            PT_EV = {}
            X2 = {}
            for bi, blk in enumerate(blocks):
                b, nt, tiles = blk
                n = nt * 128
                prev = blocks[bi - 1] if bi > 0 else None
                nxt = blocks[bi + 1] if bi + 1 < len(blocks) else None
                s4ctx = {}
                if prev is not None:
                    pnt = prev[1]
                    sched = {}
                    for ti in range(pnt):
                        sched.setdefault(5 * ti, []).append(("a1", ti))
                        sched.setdefault(5 * ti + 3, []).append(("a2", ti))
                        sched.setdefault(5 * ti + 7, []).append(("b1", ti))
                        sched.setdefault(5 * ti + 8, []).append(("b2", ti))
                        sched.setdefault(5 * ti + 11, []).append(("c1", ti))
                        sched.setdefault(5 * ti + 16, []).append(("c2", ti))
                else:
                    sched = {}
                a_ev = []
                mm_last = None
                for pc_ in range(16):
                    t_w, u = st.ptok[("up", b, pc_)]
                    for fl in range(2):
                        f = pc_ * 2 + fl
                        bk = (0, 1, 7)[f % 3]
                        bd = take(bk)
                        for kt in range(8):
                            mm_last = P.op("pe", lambda e, bk=bk, u=u, fl=fl, kt=kt, n=n: e.matmul(
                                ps[:, bk, 0:n], wupb[:, u, kt, fl * 128:(fl + 1) * 128], xnT2[:, kt, 0:n], start=(kt == 0), stop=(kt == 7)),
                                (N2[b] + [t_w] + bd) if kt == 0 else [], inc=(kt == 7))
                        t_r = P.op("act", lambda e, bk=bk, f=f, n=n: e.activation(aT[:, f, 0:n], ps[:, bk, 0:n], AF.Relu), [mm_last, st.aT_rd])
                        P.release(bk, t_r)
                        t_e = P.op("dve", lambda e, f=f, n=n: e.tensor_tensor(aT[:, f, 0:n], aT[:, f, 0:n], aT[:, f, 0:n], ALU.mult), [t_r])
                        a_ev.append(t_e)
                        for (kind, ti) in sched.get(f, []):
                            pb_, pnt_, ptiles = prev
                            pi = ptiles[ti]
                            u2 = ti % 2
                            if kind == "a1":
                                s4ctx[ti] = {}
                                s4ctx[ti]["a1"] = norm_a1(xr[:, pb_ % 2, ti, :], 4 * pi + 1, xnc[:, ti % 2, :], [X2[pb_][ti]], [st.xnc_rd[ti % 2]])
                            elif kind == "a2":
                                s4ctx[ti]["xn"] = norm_a2(xr[:, pb_ % 2, ti, :], 2, 4 * pi + 1, xnc[:, ti % 2, :], s4ctx[ti]["a1"])
                            elif kind == "b1":
                                t_tr, t_ev = norm_b(xnc[:, ti % 2, :], xnT3[:, u2, :, :], s4ctx[ti]["xn"], 6, [st.xnT3_rd[u2]])
                                st.xnc_rd[ti % 2] = t_tr
                                s4ctx[ti]["ev"] = t_ev
                            elif kind == "b2":
                                s4ctx[ti]["g"], s4ctx[ti]["pp"] = s4_gate(prev, ti, s4ctx[ti]["ev"], PT_EV[pb_], 2, 4)
                            elif kind == "c1":
                                s4ctx[ti]["c1"] = s4_tail1(prev, ti, s4ctx[ti]["g"], s4ctx[ti]["pp"], s4ctx[ti]["xn"], 2, 4)
                            else:
                                s4_tail2(prev, ti, s4ctx[ti]["c1"])
                                if nxt is not None and ti < nxt[1]:
                                    TX.setdefault(nxt[0], {})[ti] = load_tile(nxt, ti)
                    st.wup_rd[u] = mm_last
                    issue_piece()
                st.xnT2_rd = mm_last
                tp = load_p(blk)
                if nxt is not None:
                    d_ = TX.setdefault(nxt[0], {})
                    for ti in range(nxt[1]):
                        if ti not in d_:
                            d_[ti] = load_tile(nxt, ti)
                x2_ev = [None] * nt
                if nxt is not None:
                    sched1 = {}
                    for ti in range(nxt[1]):
                        for part, off in enumerate((0, 6, 12, 16, 26)):
                            sched1.setdefault(2 + 10 * ti + off, []).append((ti, part))
                else:
                    sched1 = {}
                s1ctx = {}
                step = 0
                for h in range(2):
                    banks = [2, 3, 4, 5][:nt]
                    bdall = []
                    for bk in banks:
                        bdall += take(bk)
                    for pc_ in range(8):
                        t_w, u = st.ptok[("dn", b, h, pc_)]
                        if pc_ < 7:
                            order = [(fl, ti) for fl in range(4) for ti in range(nt)]
                        else:
                            order = [(fl, ti) for ti in range(nt) for fl in range(4)]
                        tile_done = {}
                        for oi, (fl, ti) in enumerate(order):
                            f = pc_ * 4 + fl
                            first = (pc_ == 0 and oi == 0)
                            is_inc = (oi == len(order) - 1) or (pc_ == 7 and fl == 3)
                            mm_last = P.op("pe", lambda e, bk=banks[ti], u=u, fl=fl, f=f, ti=ti: e.matmul(
                                ps[:, bk, :], aT[:, f, ti * 128:(ti + 1) * 128], wdnb[:, u, fl, :], start=(f == 0), stop=(f == 31)),
                                ((a_ev + [t_w]) if first else ([t_w] if oi == 0 else [])) + (take(banks[ti]) if (pc_ == 0 and fl == 0) else []),
                                inc=is_inc)
                            if pc_ == 7 and fl == 3:
                                tile_done[ti] = mm_last
                            if oi % nt == nt - 1:
                                for (ti1, part) in sched1.get(step, []):
                                    s1_full(nxt, ti1, part, s1ctx.setdefault(ti1, {}))
                                step += 1
                        st.wdn_rd[u] = mm_last
                        issue_piece()
                    for ti in range(nt):
                        xa = xr[:, b % 2, ti, h * 512:(h + 1) * 512]
                        t_e = P.op("dve", lambda e, xa=xa, bk=banks[ti]: e.tensor_tensor(xa, xa, ps[:, bk, :], ALU.add), [tile_done[ti], X1[b][ti]])
                        P.release(banks[ti], t_e)
                        x2_ev[ti] = t_e
                for stp in sorted(k for k in sched1 if k >= step):
                    for (ti1, part) in sched1[stp]:
                        s1_full(nxt, ti1, part, s1ctx.setdefault(ti1, {}))
                st.aT_rd = mm_last
                X2[b] = x2_ev
                PT_EV[b] = p_transposes(blk, tp, 6)
            last = blocks[-1]
            lb, lnt, ltiles = last
            ec = [{} for _ in range(lnt)]
            for ti in range(lnt):
                ec[ti]["a1"] = norm_a1(xr[:, lb % 2, ti, :], 4 * ltiles[ti] + 1, xnc[:, ti % 2, :], [X2[lb][ti]], [st.xnc_rd[ti % 2]])
            for ti in range(lnt):
                ec[ti]["xn"] = norm_a2(xr[:, lb % 2, ti, :], 2, 4 * ltiles[ti] + 1, xnc[:, ti % 2, :], ec[ti]["a1"])
            for ti in range(lnt):
                u2 = ti % 2
                t_tr, t_ev = norm_b(xnc[:, ti % 2, :], xnT3[:, u2, :, :], ec[ti]["xn"], 6 + ti % 2, [st.xnT3_rd[u2]])
                st.xnc_rd[ti % 2] = t_tr
                ec[ti]["g"], ec[ti]["pp"] = s4_gate(last, ti, t_ev, PT_EV[lb], (2, 0)[ti % 2], 4 if ti % 2 == 0 else 4)
                ec[ti]["c1"] = s4_tail1(last, ti, ec[ti]["g"], ec[ti]["pp"], ec[ti]["xn"], (2, 0)[ti % 2], 4)
            for ti in range(lnt):
                s4_tail2(last, ti, ec[ti]["c1"])
            fin_deps = [t for t in list(out_tok) + list(dbg_o.values()) if t is not None]
            P.op("sp", lambda e: e.nop(), fin_deps, inc=False)
            P.emit()
    return nc


def host_consts():
    E = np.zeros((128, 64, 128), np.float32)
    for a in range(8):
        for b in range(8):
            for h in range(16):
                E[a * 16 + h, a * 8 + b, b * 16 + h] = 1.0
    mask = np.zeros((128, 128), np.float32)
    for s in range(8):
        for t in range(s, 8):
            mask[s * 16:(s + 1) * 16, t * 16:(t + 1) * 16] = 1.0
    k25 = np.array([-7, -6, -5, -4, -3, -2, -1, 0, 0, 1, 2, 3, 4, 5, 6, 7, 8, 7, 6, 5, 4, 3, 2, 1, 0], np.float32)
    j = np.zeros(NCH, np.float32)
    m0 = np.ones(NCH, np.float32)
    for s in range(5):
        c0 = chcol_init(s)
        n = 256 if s == 0 else 8
        j[c0] = -1.0
        m0[c0] = 0.0
        j[c0 + 1:c0 + 1 + n] = np.arange(n)
    icnt = (1.0 / np.arange(1, 17)).astype(np.float32)
    rep = lambda v: np.ascontiguousarray(np.broadcast_to(v[None, :], (128, v.shape[0]))).astype(np.float32)
    return {
        "c_identb": np.eye(128, dtype=np.float32).astype(ml_dtypes.bfloat16),
        "c_identf": np.eye(128, dtype=np.float32),
        "c_E": E.reshape(128, 64 * 128).astype(ml_dtypes.bfloat16),
        "c_mask": mask, "c_k25": rep(k25), "c_j": rep(j), "c_m0": rep(m0), "c_icnt": rep(icnt),
        "c_ones": np.ones((128, 8), np.float32).astype(ml_dtypes.bfloat16),
    }


_NC_CACHE = {}


def make_in_maps(inputs):
    f = lambda a: np.ascontiguousarray(np.asarray(a, dtype=np.float32))
    consts = host_consts()
    shared = {
        "g_mix_norm": f(inputs["g_mix_norm"][0]), "w_in": f(inputs["w_in"][0]),
        "lambda_re": f(inputs["lambda_re"][0]).reshape(16, 128), "lambda_im": f(inputs["lambda_im"][0]).reshape(16, 128),
        "log_dt": f(inputs["log_dt"][0]),
        "b_re": f(inputs["b_re"][0]).reshape(16, 128, 16), "b_im": f(inputs["b_im"][0]).reshape(16, 128, 16),
        "c_re": f(inputs["c_re"][0]).reshape(512, 64), "c_im": f(inputs["c_im"][0]).reshape(512, 64),
        "d_skip": f(inputs["d_skip"][0]), "w_glu": f(inputs["w_glu"][0]), "w_pool": f(inputs["w_pool"][0]),
        "pool_scale": f(inputs["pool_scale"][0]), "g_s5_out": f(inputs["g_s5_out"][0]), "g_pool_out": f(inputs["g_pool_out"][0]),
        "w_out": f(inputs["w_out"][0]), "g_mlp_norm": f(inputs["g_mlp_norm"][0]), "w_up": f(inputs["w_up"][0]),
        "w_down": f(inputs["w_down"][0]), "g_ple_norm": f(inputs["g_ple_norm"][0]), "w_ple_gate": f(inputs["w_ple_gate"][0]),
        "w_ple_proj": f(inputs["w_ple_proj"][0]), "g_final": f(inputs["g_final"]),
    }
    shared.update(consts)
    xp, xs = f(inputs["x_prompt"]), f(inputs["x_sample"])
    pp, psm = f(inputs["p_prompt"][0]), f(inputs["p_sample"][0])
    sre, sim, spl = f(inputs["state_s5_re"][0]), f(inputs["state_s5_im"][0]), f(inputs["state_pool"][0])
    maps = []
    for c in range(NCORES):
        m = dict(shared)
        m["x"] = np.concatenate([xp[c], xs[4 * c:4 * c + 4].reshape(NSQ * LS, D)], axis=0)
        m["p"] = np.concatenate([pp[c], psm[4 * c:4 * c + 4].reshape(NSQ * LS, 256)], axis=0)
        m["st_re"] = np.ascontiguousarray(sre[4 * c:4 * c + 4].reshape(NSQ * 16, 128))
        m["st_im"] = np.ascontiguousarray(sim[4 * c:4 * c + 4].reshape(NSQ * 16, 128))
        m["st_pool"] = np.ascontiguousarray(spl[4 * c:4 * c + 4])
        maps.append(m)
    return maps


def kernel(**inputs):
    if "nc" not in _NC_CACHE:
        _NC_CACHE["nc"] = build()
    nc = _NC_CACHE["nc"]
    maps = make_in_maps(inputs)
    res = run_bass_kernel_spmd(nc, maps, core_ids=list(range(NCORES)))
    R = res.results
    y_p = np.stack([R[c]["y"][:LP] for c in range(NCORES)], 0)
    y_s = np.concatenate([R[c]["y"][LP:].reshape(NSQ, LS, D) for c in range(NCORES)], 0)
    hre = [R[c]["h_re"].reshape(5, 32, 64) for c in range(NCORES)]
    him = [R[c]["h_im"].reshape(5, 32, 64) for c in range(NCORES)]
    pl = [R[c]["pool_new"] for c in range(NCORES)]
    re_p = np.stack([h[0] for h in hre], 0)[None]
    im_p = np.stack([h[0] for h in him], 0)[None]
    pool_p = np.stack([q[0] for q in pl], 0)[None]
    re_s = np.concatenate([h[1:] for h in hre], 0)[None]
    im_s = np.concatenate([h[1:] for h in him], 0)[None]
    pool_s = np.concatenate([q[1:] for q in pl], 0)[None]
    a = lambda v: np.ascontiguousarray(v, dtype=np.float32)
    return (a(y_p), a(y_s), a(re_p), a(im_p), a(pool_p), a(re_s), a(im_s), a(pool_s))
```

```python
import numpy as np
import ml_dtypes
from contextlib import ExitStack
import concourse.bass as bass
import concourse.mybir as mybir
from concourse.bass_utils import run_bass_kernel_spmd

F32, BF16 = mybir.dt.float32, mybir.dt.bfloat16
AF = mybir.ActivationFunctionType
ALU = mybir.AluOpType

NCORES = 8
D = 1024
LP = 2048
NSQ = 4
LS = 64
NTOK = LP + NSQ * LS
NTILE = NTOK // 128
TOKP = 8 + LP + NSQ * (8 + LS)
NCH = TOKP // 8
PW = 16 + LP + NSQ * (16 + LS)
EPS = 1e-6
TWO_PI = 6.283185307179586
MAGIC = 12582912.0
ENGS = ("pe", "act", "dve", "pool", "sp")
LIMIT = 100000000


def tokcol(seq, t=0):
    return 8 + t if seq == 0 else 2064 + 72 * (seq - 1) + t


def chcol_init(seq):
    return 0 if seq == 0 else 257 + 9 * (seq - 1)


def poolcol(seq, t=0):
    return 16 + t if seq == 0 else 2080 + 80 * (seq - 1) + t


class Prog:
    def __init__(self, nc, es):
        self.nc, self.es = nc, es
        self.ops = []
        self.tok = {}
        self.nid = 0
        self.sem = {e: es.enter_context(nc.semaphore("s_" + e)) for e in ENGS}
        self.cnt = {e: 0 for e in ENGS}
        self.waited = {e: {} for e in ENGS}
        self.slots = {}
        self.bank_tok = [[] for _ in range(8)]
        self.bank_ptr = 0
        self.last = {}

    def op(self, eng, fn, deps=(), inc=True, force=False):
        if self.nid >= LIMIT and not force:
            return None
        i = self.nid
        self.nid += 1
        d = []
        for x in deps:
            if x is None:
                continue
            if isinstance(x, (list, tuple)):
                d.extend([y for y in x if y is not None])
            else:
                d.append(x)
        if inc:
            self.cnt[eng] += 1
            self.tok[i] = (self.sem[eng], self.cnt[eng])
            self.last[eng] = i
        self.ops.append((eng, fn, d, inc, None))
        return i if inc else None

    def dma(self, eng, fn, deps=(), slot="misc"):
        if self.nid >= LIMIT:
            return None
        i = self.nid
        self.nid += 1
        d = []
        for x in deps:
            if x is None:
                continue
            if isinstance(x, (list, tuple)):
                d.extend([y for y in x if y is not None])
            else:
                d.append(x)
        if slot not in self.slots:
            self.slots[slot] = [self.es.enter_context(self.nc.semaphore("d_" + str(len(self.slots)))), 0]
        s = self.slots[slot]
        s[1] += 16
        self.tok[i] = (s[0], s[1])
        self.last["slot:" + str(slot)] = i
        self.ops.append((eng, fn, d, True, s[0]))
        return i

    def bank(self):
        b = self.bank_ptr
        self.bank_ptr = (b + 1) % 8
        return b, list(self.bank_tok[b])

    def bank2(self):
        if self.bank_ptr % 2:
            self.bank_ptr = (self.bank_ptr + 1) % 8
        b = self.bank_ptr
        self.bank_ptr = (b + 2) % 8
        return b, list(self.bank_tok[b]) + list(self.bank_tok[b + 1])

    def release(self, b, toks, n=1):
        if not isinstance(toks, (list, tuple)):
            toks = [toks]
        for k in range(n):
            self.bank_tok[b + k] = list(toks)

    def emit(self, barrier=True):
        nc = self.nc
        if barrier:
            lasts = [v for k, v in self.last.items() if k != "slot:wcast"]
            for eng in ENGS:
                self.op(eng, lambda e: e.nop(), lasts, inc=False, force=True)
        ops = self.ops
        self.ops = []
        with nc.Block() as block:
            def make(engname):
                def body(e):
                    w = self.waited[engname]
                    for (eng, fn, deps, inc, dsem) in ops:
                        if eng != engname:
                            continue
                        for dpt in deps:
                            sem, val = self.tok[dpt]
                            if w.get(sem.name, 0) < val:
                                e.wait_ge(sem, val)
                                w[sem.name] = val
                        ins = fn(e)
                        if dsem is not None:
                            ins.then_inc(dsem, 16)
                        elif inc:
                            ins.then_inc(self.sem[engname], 1)
                return body
            block.tensor(make("pe"))
            block.scalar(make("act"))
            block.vector(make("dve"))
            block.gpsimd(make("pool"))
            block.sync(make("sp"))


def bcast(ap, shape, axis):
    return ap.unsqueeze(axis).to_broadcast(shape)


def build(debug=None, stop=None):
    nc = bass.Bass("TRN2", target_bir_lowering=False)
    debug = debug or []

    def din(name, shape, dt=F32):
        return nc.dram_tensor(name, list(shape), dt, kind="ExternalInput").ap()

    def dout(name, shape, dt=F32):
        return nc.dram_tensor(name, list(shape), dt, kind="ExternalOutput").ap()

    x_d = din("x", [NTOK, D])
    p_d = din("p", [NTOK, 256])
    sre_d = din("st_re", [NSQ * 16, 128])
    sim_d = din("st_im", [NSQ * 16, 128])
    spool_d = din("st_pool", [NSQ, 15, 512])
    gmix_d = din("g_mix_norm", [D]); win_d = din("w_in", [D, D])
    lre_d = din("lambda_re", [16, 128]); lim_d = din("lambda_im", [16, 128]); ldt_d = din("log_dt", [32])
    bre_d = din("b_re", [16, 128, 16]); bim_d = din("b_im", [16, 128, 16])
    cre_d = din("c_re", [512, 64]); cim_d = din("c_im", [512, 64])
    dsk_d = din("d_skip", [512]); wglu_d = din("w_glu", [512, 512]); wpool_d = din("w_pool", [4, 128, 128])
    pscale_d = din("pool_scale", [512]); gs5_d = din("g_s5_out", [512]); gpo_d = din("g_pool_out", [512])
    wout_d = din("w_out", [D, D]); gmlp_d = din("g_mlp_norm", [D]); wup_d = din("w_up", [D, 4 * D])
    wdn_d = din("w_down", [4 * D, D]); gple_d = din("g_ple_norm", [D]); wgate_d = din("w_ple_gate", [D, D])
    wple_d = din("w_ple_proj", [256, D]); gfin_d = din("g_final", [D])
    identb_d = din("c_identb", [128, 128], BF16); identf_d = din("c_identf", [128, 128])
    E_d = din("c_E", [128, 64 * 128], BF16); mask_d = din("c_mask", [128, 128])
    k25_d = din("c_k25", [128, 25]); j_d = din("c_j", [128, NCH]); m0_d = din("c_m0", [128, NCH])
    icnt_d = din("c_icnt", [128, 16]); ones_d = din("c_ones", [128, 8], BF16)

    wup16 = nc.dram_tensor("wup16", [D, 4 * D], BF16, kind="Internal").ap()
    wdn16 = nc.dram_tensor("wdn16", [4 * D, D], BF16, kind="Internal").ap()
    wout16 = nc.dram_tensor("wout16", [D, D], BF16, kind="Internal").ap()
    wgate16 = nc.dram_tensor("wgate16", [D, D], BF16, kind="Internal").ap()
    wple16 = nc.dram_tensor("wple16", [256, D], BF16, kind="Internal").ap()
    y_o = dout("y", [NTOK, D])
    hre_o = dout("h_re", [80, 128]); him_o = dout("h_im", [80, 128])
    pool_o = dout("pool_new", [5, 15, 512])
    dbg_o = {}

    es = ExitStack()
    with es:
        P = Prog(nc, es)

        def sb(name, shape, dt=F32, stack=es):
            return stack.enter_context(nc.sbuf_tensor(name, list(shape), dt))

        ps = es.enter_context(nc.psum_tensor("ps", [128, 8, 512], F32))
        psb = ps[:].bitcast(BF16)

        def dump(name, ap, shape, deps):
            if name not in debug:
                return None
            o = dout("dbg_" + name, shape, ap.dtype if hasattr(ap, "dtype") else F32)
            dbg_o[name] = P.dma("sp", lambda e: e.dma_start(out=o, in_=ap), deps, slot="dbg_" + name)
            return dbg_o[name]

        identb = sb("identb", [128, 128], BF16); identf = sb("identf", [128, 128])
        ones_b = sb("ones_b", [128, 8], BF16)
        gbc = sb("gbc", [128, 4, D], F32) if False else None
        gbc3 = sb("gbc3", [128, 3, D])
        y2T = sb("y2T", [128, 4, TOKP + 256], BF16)
        ypT = sb("ypT", [128, 4, TOKP + 256], BF16)
        rstd_s5 = sb("rstd_s5", [128, NTILE]); rstd_po = sb("rstd_po", [128, NTILE])
        colv = sb("colv", [128, 3, 4])
        HL = sb("HL", [128, 2, 5, 16])

        c0 = []
        c0.append(P.dma("sp", lambda e: e.dma_start(out=identb[:], in_=identb_d), slot="c0"))
        c0.append(P.dma("sp", lambda e: e.dma_start(out=identf[:], in_=identf_d), slot="c0"))
        c0.append(P.dma("sp", lambda e: e.dma_start(out=ones_b[:], in_=ones_d), slot="c0"))
        C0 = c0[-1]

        def late_consts():
            c1 = []
            for k, g in enumerate((pscale_d, gs5_d, gpo_d)):
                c1.append(P.dma("sp", lambda e, k=k, g=g: e.dma_start(
                    out=colv[:, k, :], in_=bass.AP(g.tensor, 0, [[1, 128], [128, 4]]),
                    allow_slow_non_contiguous=True), slot="c1"))
            for k, g in enumerate((gmlp_d, gple_d, gfin_d)):
                c1.append(P.dma("sp", lambda e, k=k, g=g: e.dma_start(
                    out=gbc3[:, k, :], in_=bass.AP(g.tensor, 0, [[0, 128], [1, D]])), slot="c1"))
            return c1[-1]
        sis = ExitStack()
        lam = sb("lam", [128, 2, 16], F32, sis); ldt = sb("ldt", [128, 16], F32, sis)
        Bn = sb("Bn", [128, 2, 16, 16], F32, sis)
        Cn = sb("Cn", [128, 2, 2, 2, 64], F32, sis)
        k25 = sb("k25", [128, 25], F32, sis); maskf = sb("maskf", [128, 128], F32, sis)
        dcol = sb("dcol", [128, 32], F32, sis)
        h0n = sb("h0n", [64, 2, 128], F32, sis)

        def tile_cols(buf, kt, i):
            if i < 16:
                c = 8 + 128 * i
                return buf[:, kt, c:c + 128]
            c = tokcol(1 + 2 * (i - 16)) - 8
            return buf[:, kt, c:c + 144].rearrange("p (q c) -> p q c", c=72)[:, :, 8:72]

        def blk_cols(buf, kt, b):
            if b < 4:
                c = 8 + 512 * b
                return buf[:, kt, c:c + 512], 512
            c = tokcol(1) - 8
            return buf[:, kt, c:c + 288].rearrange("p (q c) -> p q c", c=72)[:, :, 8:72], 256

        def otile(buf, kt, i):
            if i < 16:
                c = 8 + 128 * i
                return buf[:, kt, c:c + 128]
            c = TOKP + 128 * (i - 16)
            return buf[:, kt, c:c + 128]

        def oblk(buf, kt, b, three=False):
            if b < 4:
                c = 8 + 512 * b
                return buf[:, kt, c:c + 512]
            v = buf[:, kt, TOKP:TOKP + 256]
            return v.rearrange("p (q c) -> p q c", c=64) if three else v

        def ps_cols(bk, n, b):
            if b < 4:
                return ps[:, bk, 0:n]
            return ps[:, bk, 0:256].rearrange("p (q c) -> p q c", c=64)

        def ps_tile(bk, i, width=128):
            if i < 16:
                return ps[:, bk, 0:width]
            return ps[:, bk, 0:128].rearrange("p (q c) -> p q c", c=64)

        def rstd_chain(ss_ap, out_ap, tmp_ap, scale, deps):
            t1 = P.op("dve", lambda e: e.tensor_scalar(tmp_ap, ss_ap, scale, EPS, ALU.mult, ALU.add), deps)
            t2 = P.op("act", lambda e: e.activation(tmp_ap, tmp_ap, AF.Sqrt), [t1])
            return P.op("dve", lambda e: e.reciprocal(out_ap, tmp_ap), [t2])

        def norm_transpose(x_ap, gk, ss_ap, tmp_ap, rs_ap, junk, xn_ap, xnT_dst, deps, xn_free, dst_free, act_evac=True):
            t_sq = P.op("act", lambda e: e.activation(junk, x_ap, AF.Square, accum_out=ss_ap), deps)
            t_r = rstd_chain(ss_ap, rs_ap, tmp_ap, 1.0 / D, [t_sq])
            t_xn = P.op("dve", lambda e: e.scalar_tensor_tensor(xn_ap, x_ap, rs_ap, gbc[:, gk, :], ALU.mult, ALU.mult),
                        [t_r, C0] + list(xn_free) + list(deps))
            bk, bd = P.bank()
            t_tr = None
            for j in range(8):
                t_tr = P.op("pe", lambda e, j=j: e.transpose(psb[:, bk, j * 128:(j + 1) * 128], xn_ap[:, j * 128:(j + 1) * 128], identb[:]),
                            [t_xn] + bd if j == 0 else [], inc=(j == 7))
            src = psb[:, bk, :].rearrange("p (k c) -> p k c", c=128)
            if act_evac:
                t_ev = P.op("act", lambda e: e.activation(xnT_dst, src, AF.Copy), [t_tr] + list(dst_free))
            else:
                t_ev = P.op("dve", lambda e: e.tensor_copy(xnT_dst, src), [t_tr] + list(dst_free))
            P.release(bk, t_ev)
            return t_xn, t_tr, t_ev

        zs = ExitStack()
        zT = sb("zT", [128, 4, 8, NCH], BF16, zs)
        pa = ExitStack()
        with pa:
            win_b = sb("win_b", [128, 8, D], BF16, pa)
            gmx = sb("gmx", [128, D], F32, pa)
            t_gmx = P.dma("sp", lambda e: e.dma_start(out=gmx[:], in_=bass.AP(gmix_d.tensor, 0, [[0, 128], [1, D]])), slot="gmx")
            zP = sb("zP", [128, 4, PW], F32, pa)
            xt = sb("xt", [128, 4, D], F32, pa)
            ssA = sb("ssA", [128, NTILE], F32, pa); tmpA = sb("tmpA", [128, NTILE], F32, pa); rsA = sb("rsA", [128, NTILE], F32, pa)
            xn = sb("xn", [128, 4, D], BF16, pa)
            xnT = sb("xnT", [128, 2, 8, 512], BF16, pa)
            wpool_b = sb("wpool_b", [128, 4, 128], BF16, pa)
            icnt = sb("icnt", [128, 16], F32, pa)
            pt1 = sb("pt1", [128, PW], F32, pa); pt2 = sb("pt2", [128, PW], F32, pa)
            pooledT = sb("pooledT", [128, 1, TOKP], BF16, pa)
            ysq = sb("ysq", [128, 1, TOKP + 256], BF16, pa)
            ssP = sb("ssP", [128, NTILE], F32, pa); tmpP = sb("tmpP", [128, NTILE], F32, pa)

            t_win = None
            for kt in range(0, 8, 4):
                t_win = P.dma("pool", lambda e, kt=kt: e.dma_start(
                    out=win_b[:, kt:kt + 4, :], in_=win_d.rearrange("(kt p) n -> p kt n", p=128)[:, kt:kt + 4, :]), slot="win")
            t_wpool = P.dma("pool", lambda e: e.dma_start(out=wpool_b[:], in_=wpool_d.rearrange("g k n -> k g n")), slot="wpool")
            t_icnt = P.dma("sp", lambda e: e.dma_start(out=icnt[:], in_=icnt_d), slot="icnt")
            t_ss0 = P.op("dve", lambda e: e.memset(ssA[:], 0.0))
            P.op("dve", lambda e: e.memset(zT[:, :, :, 0:1], 0.0), inc=False)
            t_z0 = P.op("dve", lambda e: e.memset(zT[:, :, :, 257:NCH:9], 0.0))
            P.op("dve", lambda e: e.memset(zP[:, :, 0:16], 0.0), inc=False)
            t_zp0 = P.op("dve", lambda e: e.memset(zP[:, :, 2064:2064 + 320].rearrange("p j (q c) -> p j q c", c=80)[:, :, :, 0:16], 0.0))
            P.op("pool", lambda e: e.memset(pt1[:], 0.0), inc=False)
            t_pt0 = P.op("pool", lambda e: e.memset(pt2[:], 0.0))
            pt_free0 = t_pt0
            deferred = []
            deferred_h = []

            def mk_thunk(eng, fn, deps=(), slot="misc", **kw):
                return lambda: P.dma(eng, fn, deps, slot=slot)
            for r, (ld_, li_) in enumerate(((lre_d, bre_d), (lim_d, bim_d))):
                deferred.append(mk_thunk("sp", lambda e, r=r, ld_=ld_: e.dma_start(
                    out=lam[:, r, :], in_=ld_.rearrange("pr q -> q pr"), allow_slow_non_contiguous=True), slot="su"))
                deferred.append(mk_thunk("sp", lambda e, r=r, li_=li_: e.dma_start(
                    out=Bn[:, r, :, :], in_=li_.rearrange("pr q h -> q pr h")), slot="su"))
            for e_ in range(2):
                deferred.append(mk_thunk("sp", lambda e, e_=e_: e.dma_start(
                    out=ldt[e_ * 64:(e_ + 1) * 64, :], in_=bass.AP(ldt_d.tensor, e_, [[0, 64], [2, 16]]),
                    allow_slow_non_contiguous=True), slot="su"))
            for r, cd in enumerate((cre_d, cim_d)):
                for e_ in range(2):
                    for hi in range(2):
                        for lo in range(8):
                            deferred.append(mk_thunk("sp", lambda e, r=r, cd=cd, e_=e_, hi=hi, lo=lo: e.dma_start(
                                out=Cn[lo * 16:(lo + 1) * 16, r, e_, hi, :],
                                in_=bass.AP(cd.tensor, (hi * 256 + lo * 32 + e_ * 16) * 64, [[64, 16], [1, 64]])), slot="su"))
            deferred.append(mk_thunk("sp", lambda e: e.dma_start(out=k25[:], in_=k25_d), slot="su"))
            deferred.append(mk_thunk("sp", lambda e: e.dma_start(out=maskf[:], in_=mask_d), slot="su"))
            for s_ in range(8):
                deferred.append(mk_thunk("sp", lambda e, s_=s_: e.dma_start(
                    out=dcol[s_ * 16:(s_ + 1) * 16, :], in_=bass.AP(dsk_d.tensor, 0, [[1, 16], [16, 32]]),
                    allow_slow_non_contiguous=True), slot="su"))
            deferred.append(mk_thunk("sp", lambda e: e.dma_start(out=h0n[:, 0, :], in_=sre_d), slot="su"))
            deferred.append(mk_thunk("sp", lambda e: e.dma_start(out=h0n[:, 1, :], in_=sim_d), slot="su"))
            for q in range(NSQ):
                for j in range(4):
                    c = poolcol(1 + q) - 15
                    deferred_h.append(mk_thunk("sp", lambda e, q=q, j=j, c=c: e.dma_start(
                        out=zP[:, j, c:c + 15], in_=spool_d[q, :, j * 128:(j + 1) * 128].rearrange("t c -> c t"),
                        allow_slow_non_contiguous=True), [t_zp0], slot="hist"))

            dq = deferred + deferred_h
            n_su = len(deferred)
            DTOK = []

            def flush(n):
                for _ in range(n):
                    if len(DTOK) < len(dq):
                        DTOK.append(dq[len(DTOK)]())
            xn_rd = [None] * 4
            xt_rd = [None] * 4
            xnT_rd = [None, None]
            z_ev = []
            XN = {}
            EVS = {}

            def chainA(hb):
                tiles = [2 * hb, 2 * hb + 1]
                sqs = []
                lds = []
                for i in tiles:
                    u = i % 4
                    t_ld = P.dma("sp", lambda e, i=i, u=u: e.dma_start(out=xt[:, u, :], in_=x_d[i * 128:(i + 1) * 128, :]),
                                 [xt_rd[u]], slot="xt%d" % u)
                    lds.append(t_ld)
                    sqs.append(P.op("act", lambda e, i=i, u=u: e.activation(xn[:, u, :], xt[:, u, :], AF.Square, accum_out=ssA[:, i:i + 1]),
                                    [t_ld, t_ss0, xn_rd[u]]))
                c0_, c1_ = tiles[0], tiles[1] + 1
                t_r = rstd_chain(ssA[:, c0_:c1_], rsA[:, c0_:c1_], tmpA[:, c0_:c1_], 1.0 / D, sqs)
                for k_, i in enumerate(tiles):
                    u = i % 4
                    t_xn = P.op("dve", lambda e, i=i, u=u: e.scalar_tensor_tensor(
                        xn[:, u, :], xt[:, u, :], rsA[:, i:i + 1], gmx[:], ALU.mult, ALU.mult), [t_r, t_gmx, xn_rd[u], lds[k_]])
                    xt_rd[u] = t_xn
                    XN[i] = t_xn

            def transA(hb):
                for i in (2 * hb, 2 * hb + 1):
                    u = i % 4
                    b = min(i // 4, 4)
                    ti = i - 4 * b
                    bb = b % 2
                    bk, bd = P.bank()
                    t_tr = None
                    for j in range(8):
                        t_tr = P.op("pe", lambda e, j=j, u=u, bk=bk: e.transpose(
                            psb[:, bk, j * 128:(j + 1) * 128], xn[:, u, j * 128:(j + 1) * 128], identb[:]),
                            ([XN[i], C0] + bd) if j == 0 else [], inc=(j == 7))
                    t_ev = P.op("act", lambda e, bk=bk, bb=bb, ti=ti: e.activation(
                        xnT[:, bb, :, ti * 128:(ti + 1) * 128], psb[:, bk, :].rearrange("p (k c) -> p k c", c=128), AF.Copy),
                        [t_tr, xnT_rd[bb]])
                    P.release(bk, t_ev)
                    xn_rd[u] = t_tr
                    EVS.setdefault(b, []).append(t_ev)

            def mmA(b):
                nt = 4 if b < 4 else 2
                bb = b % 2
                n = nt * 128
                evs = EVS[b]
                last_mm = None
                for m in range(8):
                    bk, bd = P.bank()
                    for kt in range(8):
                        last_mm = P.op("pe", lambda e, m=m, kt=kt, bk=bk, n=n, bb=bb: e.matmul(
                            ps[:, bk, 0:n], win_b[:, kt, m * 128:(m + 1) * 128], xnT[:, bb, kt, 0:n],
                            start=(kt == 0), stop=(kt == 7)),
                            (evs + bd + [t_win]) if kt == 0 else [], inc=(kt == 7))
                    if m < 4:
                        if b < 4:
                            dst = zT[:, m, :, 1 + 64 * b:1 + 64 * b + 64]
                            src = ps[:, bk, 0:512].rearrange("p (c s) -> p s c", s=8)
                        else:
                            dst = zT[:, m, :, 257:257 + 36].rearrange("p s (q c) -> p s q c", c=9)[:, :, :, 1:9]
                            src = ps[:, bk, 0:256].rearrange("p (q c s) -> p s q c", s=8, c=8)
                        t_e = P.op("act", lambda e, dst=dst, src=src: e.activation(dst, src, AF.Copy), [last_mm, t_z0])
                    else:
                        j = m - 4
                        if b < 4:
                            dst = zP[:, j, 16 + 512 * b:16 + 512 * b + 512]
                        else:
                            dst = zP[:, j, 2064:2064 + 320].rearrange("p (q c) -> p q c", c=80)[:, :, 16:80]
                        t_e = P.op("dve", lambda e, dst=dst, bk=bk, n=n, b=b: e.tensor_copy(dst, ps_cols(bk, n, b)),
                                   [last_mm, t_zp0])
                    P.release(bk, t_e)
                    z_ev.append(t_e)
                xnT_rd[bb] = last_mm

            chainA(0)
            chainA(1)
            C1 = late_consts()
            for hb in range(9):
                transA(hb)
                if hb + 2 < 9:
                    chainA(hb + 2)
                if hb >= 2:
                    flush(16)
                if hb % 2 == 1:
                    mmA(hb // 2)
            mmA(4)
            flush(len(dq))
            LD = DTOK[n_su - 1]
            t_hist = DTOK[-1]
            dump("zT", zT[:], [128, 4, 8, NCH], z_ev)
            dump("zP", zP[:], [128, 4, PW], z_ev)

            out_tok = []
            pst = xt[:].rearrange("p a d -> p (a d)")[:, 0:2560].rearrange("p (s c) -> p s c", c=512)
            for s_ in range(5):
                L = LP if s_ == 0 else LS
                c = poolcol(s_, L - 15)
                bk, bd = P.bank()
                t_q = None
                for j in range(4):
                    t_q = P.op("pe", lambda e, c=c, j=j, bk=bk: e.transpose(
                        ps[0:15, bk, j * 128:(j + 1) * 128], zP[:, j, c:c + 15], identf[:]),
                        (z_ev + bd + [C0]) if j == 0 else [], inc=(j == 3))
                t_e = P.op("act", lambda e, s_=s_, bk=bk: e.activation(pst[0:15, s_, :], ps[0:15, bk, :], AF.Copy), [t_q])
                P.release(bk, t_e)
                out_tok.append(P.dma("sp", lambda e, s_=s_: e.dma_start(out=pool_o[s_], in_=pst[0:15, s_, :]), [t_e], slot="outs"))
            t_wc = None
            for r8 in range(8):
                t_wc = P.dma("pool", lambda e, r8=r8: e.dma_start(out=wup16[r8 * 128:(r8 + 1) * 128, :], in_=wup_d[r8 * 128:(r8 + 1) * 128, :]), [out_tok[-1], t_hist], slot="wcast")
            pool_ev = []
            t_sqp = None
            pt_free = [pt_free0, pt_free0]
            pooled_free = None
            ysq_rd = None
            psg = sb("psg", [128, 4], F32, pa)
            t_psg = P.op("dve", lambda e: e.tensor_tensor(psg[:], colv[:, 0, :], colv[:, 2, :], ALU.mult), [C0, C1])
            PM = {}
            fxs = [sb("fx%d" % j, [128, 16], F32, pa) for j in range(4)]

            def pm_dve(j):
                w = (2, 4, 8, 16)[j]
                src = zP[:, j, :]
                cur = src
                sh = 1
                k = 0
                tcur = z_ev + [t_hist]
                bufs = [pt1, pt2]
                pt_free = PM.get("pt_free", [pt_free0, pt_free0])
                while sh < w:
                    dstb = bufs[k % 2]
                    tcur = [P.op("dve", lambda e, dstb=dstb, cur=cur, sh=sh: e.tensor_tensor(
                        dstb[:, sh:PW], cur[:, sh:PW], cur[:, 0:PW - sh], ALU.add), list(tcur) + [pt_free[k % 2]])]
                    cur = dstb[:]
                    sh *= 2
                    k += 1
                tp = []
                tp.append(P.op("dve", lambda e: e.scalar_tensor_tensor(
                    pooledT[:, 0, 8:8 + LP], cur[:, 16:16 + LP], 1.0 / w, src[:, 16:16 + LP], ALU.mult, ALU.subtract),
                    tcur + [PM.get("pooled_free")]))
                tp.append(P.op("dve", lambda e: e.scalar_tensor_tensor(
                    pooledT[:, 0, 2056:2056 + 288].rearrange("p (q c) -> p q c", c=72)[:, :, 8:72],
                    cur[:, 2064:2064 + 320].rearrange("p (q c) -> p q c", c=80)[:, :, 16:80], 1.0 / w,
                    src[:, 2064:2064 + 320].rearrange("p (q c) -> p q c", c=80)[:, :, 16:80], ALU.mult, ALU.subtract), tcur))
                nf = w - 1
                fx = fxs[j]
                t_f1 = P.op("dve", lambda e: e.tensor_tensor(fx[:, 0:nf], cur[:, 16:16 + nf], icnt[:, 0:nf], ALU.mult), tcur + [t_icnt])
                t_f2 = P.op("dve", lambda e: e.tensor_tensor(pooledT[:, 0, 8:8 + nf], fx[:, 0:nf], src[:, 16:16 + nf], ALU.subtract), [t_f1] + tp)
                PM["pt_free"] = [t_f2, t_f2]
                PM["f2_%d" % j] = t_f2

            def pm_pe(j):
                t_f2 = PM["f2_%d" % j]
                last_mm = None
                evs = []
                for b in range(5):
                    rhs, n = blk_cols(pooledT, 0, b)
                    bk, bd = P.bank()
                    last_mm = P.op("pe", lambda e, bk=bk, n=n, rhs=rhs, b=b: e.matmul(
                        ps_cols(bk, n, b), wpool_b[:, j, :], rhs, start=True, stop=True), [t_f2, t_wpool] + bd)
                    dst = oblk(ypT, j, b)
                    t_e = P.op("act", lambda e, dst=dst, bk=bk, n=n: e.activation(
                        dst, ps[:, bk, 0:n], AF.Copy, scale=psg[:, j:j + 1]), [last_mm, t_psg])
                    dsq = oblk(ysq, 0, b)
                    t_sqp = P.op("act", lambda e, dsq=dsq, bk=bk, n=n: e.activation(
                        dsq, ps[:, bk, 0:n], AF.Square, scale=colv[:, 0, j:j + 1]), [last_mm, C0, C1, PM.get("ysq_rd")])
                    P.release(bk, t_sqp)
                    pool_ev.append(t_sqp)
                    evs.append(t_sqp)
                PM["pooled_free"] = last_mm
                bk, bd = P.bank()
                t_st = None
                for i in range(NTILE):
                    t_st = P.op("pe", lambda e, i=i, bk=bk: e.matmul(
                        ps[:, bk, i:i + 1], otile(ysq, 0, i), ones_b[:, 0:1], start=True, stop=True),
                        (evs + bd + [C0]) if i == 0 else [], inc=(i == NTILE - 1))
                PM["ysq_rd"] = t_st
                PM["st_%d" % j] = (t_st, bk)

            def pm_acc(j):
                t_st, bk = PM["st_%d" % j]
                if j == 0:
                    t_acc = P.op("dve", lambda e: e.tensor_copy(ssP[:], ps[:, bk, 0:NTILE]), [t_st])
                else:
                    t_acc = P.op("dve", lambda e: e.tensor_tensor(ssP[:], ssP[:], ps[:, bk, 0:NTILE], ALU.add), [t_st, PM["acc"]])
                P.release(bk, t_acc)
                PM["acc"] = t_acc

            pm_dve(0)
            for j in range(4):
                pm_pe(j)
                if j + 1 < 4:
                    pm_dve(j + 1)
                pm_acc(j)
            t_acc = PM["acc"]
            t_rpo = rstd_chain(ssP[:], rstd_po[:], tmpP[:], 1.0 / 512, [t_acc])
            t_gp = pool_ev[-1]
            POOL_DONE = [t_gp, t_rpo]
            dump("ypT", ypT[:], [128, 4, TOKP + 256], POOL_DONE)
            dump("rstd_po", rstd_po[:], [128, NTILE], POOL_DONE)
            if stop == "A":
                P.op("sp", lambda e: e.nop(), [t for t in list(out_tok) + list(dbg_o.values()) if t is not None], inc=False, force=True)
            P.emit()
        if stop == "A":
            zs.close()
            return nc
        pb = ExitStack()
        with pb:
            Kin = sb("Kin", [128, 32, 128], BF16, pb)
            BT = sb("BT", [128, 2, 16, 128], BF16, pb)
            Cw = sb("Cw", [128, 3, 32, 128], BF16, pb)
            Rr = sb("Rr", [128, 16], F32, pb); THT = sb("THT", [128, 16], F32, pb)
            h0 = sb("h0", [128, 2, 64], F32, pb)
            yT = y2T

            for r8 in range(8):
                t_wc = P.dma("pool", lambda e, r8=r8: e.dma_start(out=wdn16[r8 * 512:(r8 + 1) * 512, :], in_=wdn_d[r8 * 512:(r8 + 1) * 512, :]), slot="wcast")
            t_wc = P.dma("pool", lambda e: e.dma_start(out=wout16, in_=wout_d), slot="wcast")
            t_wc = P.dma("pool", lambda e: e.dma_start(out=wgate16, in_=wgate_d), slot="wcast")
            t_wc = P.dma("pool", lambda e: e.dma_start(out=wple16, in_=wple_d), slot="wcast")
            WCAST = t_wc
            su = ExitStack()
            with su:
                Cl = sb("Cl", [128, 2, 16, 16], F32, su)
                dtt = sb("dtt", [128, 16], F32, su); lrdt = sb("lrdt", [128, 16], F32, su); th = sb("th", [128, 16], F32, su)
                amag = sb("amag", [128, 16, 25], F32, su); aph = sb("aph", [128, 16, 25], F32, su)
                trn = sb("trn", [128, 16, 25], F32, su); frs = sb("frs", [128, 16, 25], F32, su); frc = sb("frc", [128, 16, 25], F32, su)
                apw = sb("apw", [128, 2, 16, 25], F32, su)
                s1 = sb("s1", [128, 16], F32, su); s2 = sb("s2", [128, 16], F32, su); s3 = sb("s3", [128, 16], F32, su)
                qq = sb("qq", [128, 2, 16], F32, su)
                Bb = sb("Bb", [128, 2, 16, 16], F32, su)
                tA = sb("tA", [128, 16, 8, 16], F32, su); tB = sb("tB", [128, 16, 8, 16], F32, su)
                Zr = sb("Zr", [128, 16, 8, 16], F32, su); Zi = sb("Zi", [128, 16, 8, 16], F32, su)
                Xr = sb("Xr", [128, 16, 8, 16], F32, su); nXi = sb("nXi", [128, 16, 8, 16], F32, su)
                tC = sb("tC", [128, 16, 8, 16], F32, su); mk = sb("mk", [128, 2], F32, su)
                ktmp = tA[:].rearrange("q a s h -> q (a s h)")


                def dv(fn, deps):
                    return P.op("dve", fn, deps)

                bkc, bd = P.bank()
                t_c = None
                for r in range(2):
                    for e_ in range(2):
                        for hi in range(2):
                            t_c = P.op("pe", lambda e, r=r, e_=e_, hi=hi: e.matmul(
                                ps[e_ * 64:(e_ + 1) * 64, bkc, (r * 2 + hi) * 128:(r * 2 + hi + 1) * 128], Cn[:, r, e_, hi, :], identf[:],
                                start=True, stop=True), [LD, C0] + bd)
                t_cl = dv(lambda e: e.tensor_copy(Cl[:].rearrange("q r a b -> q (r a b)"), ps[:, bkc, 0:512]), [t_c])
                P.release(bkc, t_cl)
                bkh, bd = P.bank()
                t_h = None
                for r in range(2):
                    t_h = P.op("pe", lambda e, r=r: e.transpose(ps[:, bkh, r * 64:(r + 1) * 64], h0n[:, r, :], identf[0:64, 0:64]), [LD, C0] + bd)
                t_h0 = dv(lambda e: e.tensor_copy(h0[:].rearrange("q r a -> q (r a)"), ps[:, bkh, 0:128]), [t_h])
                P.release(bkh, t_h0)

                t = P.op("act", lambda e: e.activation(dtt[:], ldt[:], AF.Exp), [LD])
                t1 = dv(lambda e: e.tensor_tensor(lrdt[:], lam[:, 0, :], dtt[:], ALU.mult), [t])
                t2 = dv(lambda e: e.tensor_tensor(th[:], lam[:, 1, :], dtt[:], ALU.mult), [t])
                sh3 = [128, 16, 25]
                t3 = dv(lambda e: e.tensor_tensor(amag[:], bcast(lrdt[:], sh3, 2), bcast(k25[:], sh3, 1), ALU.mult), [t1])
                t4 = dv(lambda e: e.tensor_tensor(aph[:], bcast(th[:], sh3, 2), bcast(k25[:], sh3, 1), ALU.mult), [t2])
                t5 = P.op("act", lambda e: e.activation(amag[:], amag[:], AF.Exp), [t3])

                def frac_sin(dst, src, shift, deps, tmp):
                    a = dv(lambda e: e.tensor_scalar(trn[:] if tmp is None else tmp, src, 1.0 / TWO_PI, shift, ALU.mult, ALU.add), deps)
                    tt = trn[:] if tmp is None else tmp
                    b_ = dv(lambda e: e.tensor_scalar(dst, tt, MAGIC, MAGIC, ALU.add, ALU.subtract), [a])
                    c_ = dv(lambda e: e.tensor_tensor(dst, tt, dst, ALU.subtract), [b_])
                    return P.op("act", lambda e: e.activation(dst, dst, AF.Sin, scale=TWO_PI), [c_])
                t6 = frac_sin(frs[:], aph[:], 0.0, [t4], None)
                t7 = frac_sin(frc[:], aph[:], 0.25, [t6], None)
                t8 = dv(lambda e: e.tensor_tensor(apw[:, 0, :, :], amag[:], frc[:], ALU.mult), [t5, t7])
                t9 = dv(lambda e: e.tensor_tensor(apw[:, 1, :, :], amag[:], frs[:], ALU.mult), [t5, t7])
                AP_ = [t8, t9]
                t10 = dv(lambda e: e.tensor_copy(Rr[:], amag[:, :, 16]), [t5])
                t11 = dv(lambda e: e.tensor_scalar(THT[:], th[:], 8.0 / TWO_PI, None, ALU.mult), [t2])
                ar1 = apw[:, 0, :, 9]; ai1 = apw[:, 1, :, 9]
                lr = lam[:, 0, :]; li = lam[:, 1, :]
                a1 = dv(lambda e: e.tensor_scalar(s1[:], ar1, -1.0, None, ALU.add), AP_)
                a2 = dv(lambda e: e.tensor_tensor(s2[:], s1[:], lr, ALU.mult), [a1])
                a3 = dv(lambda e: e.tensor_tensor(s3[:], ai1, li, ALU.mult), AP_)
                a4 = dv(lambda e: e.tensor_tensor(qq[:, 0, :], s2[:], s3[:], ALU.add), [a2, a3])
                a5 = dv(lambda e: e.tensor_tensor(s2[:], ai1, lr, ALU.mult), [a4])
                a6 = dv(lambda e: e.tensor_tensor(s3[:], s1[:], li, ALU.mult), [a4])
                a7 = dv(lambda e: e.tensor_tensor(qq[:, 1, :], s2[:], s3[:], ALU.subtract), [a5, a6])
                a8 = dv(lambda e: e.tensor_tensor(s2[:], lr, lr, ALU.mult), [a7])
                a9 = dv(lambda e: e.tensor_tensor(s3[:], li, li, ALU.mult), [a7])
                a10 = dv(lambda e: e.tensor_tensor(s2[:], s2[:], s3[:], ALU.add), [a8, a9])
                a11 = dv(lambda e: e.reciprocal(s2[:], s2[:]), [a10])
                sh2 = [128, 2, 16]
                a12 = dv(lambda e: e.tensor_tensor(qq[:], qq[:], bcast(s2[:], sh2, 1), ALU.mult), [a11])
                shb = [128, 16, 16]
                qr = bcast(qq[:, 0, :], shb, 2); qi = bcast(qq[:, 1, :], shb, 2)
                b1 = dv(lambda e: e.tensor_tensor(tA[:, :, 0, :], qr, Bn[:, 0, :, :], ALU.mult), [a12])
                b2 = dv(lambda e: e.tensor_tensor(tB[:, :, 0, :], qi, Bn[:, 1, :, :], ALU.mult), [a12])
                b3 = dv(lambda e: e.tensor_tensor(Bb[:, 0, :, :], tA[:, :, 0, :], tB[:, :, 0, :], ALU.subtract), [b1, b2])
                b4 = dv(lambda e: e.tensor_tensor(tA[:, :, 0, :], qr, Bn[:, 1, :, :], ALU.mult), [b3])
                b5 = dv(lambda e: e.tensor_tensor(tB[:, :, 0, :], qi, Bn[:, 0, :, :], ALU.mult), [b3])
                b6 = dv(lambda e: e.tensor_tensor(Bb[:, 1, :, :], tA[:, :, 0, :], tB[:, :, 0, :], ALU.add), [b4, b5])
                BB = [b3, b6]
                sh4 = [128, 16, 8, 16]

                def cmul(dr, di, pk0, M, deps, neg_i=False):
                    pr_ = bcast(apw[:, 0, :, pk0:pk0 + 8], sh4, 3); pi_ = bcast(apw[:, 1, :, pk0:pk0 + 8], sh4, 3)
                    mr = bcast(M[:, 0, :, :], sh4, 2); mi = bcast(M[:, 1, :, :], sh4, 2)
                    c1 = dv(lambda e: e.tensor_tensor(tA[:], pr_, mr, ALU.mult), deps)
                    c2 = dv(lambda e: e.tensor_tensor(tB[:], pi_, mi, ALU.mult), deps)
                    c3 = dv(lambda e: e.tensor_tensor(dr, tA[:], tB[:], ALU.subtract), [c1, c2])
                    c4 = dv(lambda e: e.tensor_tensor(tA[:], pr_, mi, ALU.mult), [c3])
                    c5 = dv(lambda e: e.tensor_tensor(tB[:], pi_, mr, ALU.mult), [c3])
                    if neg_i:
                        c6 = dv(lambda e: e.scalar_tensor_tensor(di, tA[:], -1.0, tB[:], ALU.mult, ALU.subtract), [c4, c5])
                    else:
                        c6 = dv(lambda e: e.tensor_tensor(di, tA[:], tB[:], ALU.add), [c4, c5])
                    return c6
                tz = cmul(Zr[:], Zi[:], 17, Bb, AP_ + BB)
                tb_ = tz
                t_bt = tb_
                for r, Zs in enumerate((Zr, Zi)):
                    for p4 in range(4):
                        bk, bd = P.bank()
                        t_mm = None
                        for pp in range(4):
                            pr = p4 * 4 + pp
                            t_mm = P.op("pe", lambda e, bk=bk, pp=pp, pr=pr, Zs=Zs: e.transpose(
                                ps[:, bk, pp * 128:(pp + 1) * 128], Zs[:, pr, :, :].rearrange("q s h -> q (s h)"), identf[:]),
                                ([tb_] + bd) if pp == 0 else [], inc=(pp == 3))
                        t_bt = P.op("act", lambda e, bk=bk, r=r, p4=p4: e.activation(
                            BT[:, r, p4 * 4:(p4 + 1) * 4, :], ps[:, bk, :].rearrange("p (g m) -> p g m", m=128), AF.Copy), [t_mm])
                        P.release(bk, t_bt)
                tx = cmul(Xr[:], nXi[:], 0, Cl, [tz, t_cl], neg_i=True)
                m1 = dv(lambda e: e.memset(mk[:], 0.0), [])
                m2 = dv(lambda e: e.memset(mk[0:64, 0:1], 1.0), [m1])
                m3 = dv(lambda e: e.memset(mk[64:128, 1:2], 1.0), [m1])
                MK = [m2, m3]
                t_k = tx
                par_rd = None
                for e_ in range(2):
                    x1_ = dv(lambda e, e_=e_: e.tensor_scalar(tB[:].rearrange("q a s h -> q (a s h)"), Xr[:].rearrange("q a s h -> q (a s h)"),
                                                              mk[:, e_:e_ + 1], None, ALU.mult), [tx, par_rd] + MK)
                    x2_ = dv(lambda e, e_=e_: e.tensor_scalar(tC[:].rearrange("q a s h -> q (a s h)"), nXi[:].rearrange("q a s h -> q (a s h)"),
                                                              mk[:, e_:e_ + 1], None, ALU.mult), [tx, par_rd] + MK)
                    for p4 in range(4):
                        bk, bd = P.bank()
                        t_mm = None
                        for pp in range(4):
                            pr = p4 * 4 + pp
                            P.op("pe", lambda e, bk=bk, pp=pp, pr=pr: e.matmul(
                                ps[:, bk, pp * 128:(pp + 1) * 128], Zr[:, pr, :, :].rearrange("q s h -> q (s h)"),
                                tB[:, pr, :, :].rearrange("q s h -> q (s h)"), start=True, stop=False),
                                ([x1_, x2_] + bd) if pp == 0 else [], inc=False)
                            t_mm = P.op("pe", lambda e, bk=bk, pp=pp, pr=pr: e.matmul(
                                ps[:, bk, pp * 128:(pp + 1) * 128], Zi[:, pr, :, :].rearrange("q s h -> q (s h)"),
                                tC[:, pr, :, :].rearrange("q s h -> q (s h)"), start=False, stop=True), [], inc=(pp == 3))
                        t_k1 = dv(lambda e, bk=bk: e.tensor_tensor(
                            ktmp[:, 0:512].rearrange("p (g m) -> p g m", m=128), ps[:, bk, :].rearrange("p (g m) -> p g m", m=128),
                            bcast(maskf[:], [128, 4, 128], 1), ALU.mult), [t_mm, LD, t_k])
                        P.release(bk, t_k1)
                        for pp in range(4):
                            g = 2 * (p4 * 4 + pp) + e_
                            t_k = dv(lambda e, g=g, pp=pp: e.scalar_tensor_tensor(
                                Kin[:, g, :], identf[:], dcol[:, g:g + 1], ktmp[:, pp * 128:(pp + 1) * 128], ALU.mult, ALU.add), [t_k1, C0])
                    par_rd = t_mm
                tc_ = cmul(Xr[:], nXi[:], 9, Cl, [t_bt, t_k], neg_i=True)
                cws = []
                for e_ in range(2):
                    xrf = Xr[:].rearrange("q a s h -> q a (s h)"); nxf = nXi[:].rearrange("q a s h -> q a (s h)")
                    cws.append(dv(lambda e, e_=e_, xrf=xrf: e.tensor_scalar(Cw[:, 0, e_:32:2, :], xrf, mk[:, e_:e_ + 1], None, ALU.mult), [tc_] + MK))
                    cws.append(dv(lambda e, e_=e_, xrf=xrf: e.tensor_scalar(Cw[:, 1, e_:32:2, :], xrf, mk[:, e_:e_ + 1], -1.0, ALU.mult, ALU.mult), [tc_] + MK))
                    cws.append(dv(lambda e, e_=e_, nxf=nxf: e.tensor_scalar(Cw[:, 2, e_:32:2, :], nxf, mk[:, e_:e_ + 1], None, ALU.mult), [tc_] + MK))
                c1 = c2 = c3 = cws[-1]
                SETUP = [c1, c2, c3, t_bt, t_k, t10, t11, t_h0]
                dump("Kin", Kin[:], [128, 32, 128], SETUP)
                dump("BT", BT[:], [128, 2, 16, 128], SETUP)
                dump("Cw", Cw[:], [128, 3, 32, 128], SETUP)
                dump("apw", apw[:], [128, 2, 16, 25], SETUP)
                dump("h0", h0[:], [128, 2, 64], SETUP)
                if stop == "SU":
                    P.op("sp", lambda e: e.nop(), list(out_tok) + list(dbg_o.values()), inc=False)
                P.emit()
            if stop == "SU":
                pb.close(); zs.close()
                return nc

            sc = ExitStack()
            with sc:
                NB = 4
                E_b = sb("E_b", [128, 64, 128], BF16, sc)
                jtab = sb("jtab", [128, NCH], F32, sc); m0tab = sb("m0tab", [128, NCH], F32, sc)
                t_E = P.dma("sp", lambda e: e.dma_start(out=E_b[:], in_=E_d.rearrange("p (a m) -> p a m", m=128)), slot="s5c0")
                t_j = P.dma("sp", lambda e: e.dma_start(out=jtab[:], in_=j_d), slot="s5c1")
                t_m0 = P.dma("sp", lambda e: e.dma_start(out=m0tab[:], in_=m0_d), slot="s5c2")

                U = sb("U", [128, 2, 8, NCH], BF16, sc)
                ygl = sb("ygl", [128, 8, NCH], BF16, sc)
                cs = sb("cs", [128, 2, NB, NCH], F32, sc)
                Rm = sb("Rm", [128, NB, NCH], F32, sc)
                trs = sb("trs", [128, NB, NCH], F32, sc); tr2 = sb("tr2", [128, NB, NCH], F32, sc)
                W = sb("W", [128, 2, NB, NCH], F32, sc)
                G = sb("G", [128, 2, NB, NCH], F32, sc)
                w1 = trs; w2 = tr2
                PP = sb("PP", [128, 4, NB, NCH], BF16, sc)
                e0 = sb("e0", [128, 2, NB, NSQ], F32, sc); e1 = sb("e1", [128, NB, NSQ], F32, sc)
                ss5 = sb("ss5", [128, NTILE], F32, sc)
                hl1 = sb("hl1", [128, NB, 5], F32, sc); hl2 = sb("hl2", [128, NB, 5], F32, sc)

                def dv(fn, deps):
                    return P.op("dve", fn, deps)
                shn = [128, NB, NCH]
                prev_batch = []
                yT_w = []
                UEV = {}
                y_done = {}

                def shuf_in(bt):
                    ub = bt % 2
                    free = [y_done.get(bt - 2)]
                    evs = []
                    for gi in range(8):
                        bk, bd = P.bank()
                        t_mm = None
                        for s_ in range(8):
                            t_mm = P.op("pe", lambda e, bk=bk, gi=gi, s_=s_, bt=bt: e.matmul(
                                ps[:, bk, 0:NCH], E_b[:, gi * 8 + s_, :], zT[:, bt, s_, :], start=(s_ == 0), stop=(s_ == 7)),
                                (z_ev + [t_E] + bd) if s_ == 0 else [], inc=(s_ == 7))
                        t_e = P.op("act", lambda e, bk=bk, gi=gi, ub=ub: e.activation(U[:, ub, gi, :], ps[:, bk, 0:NCH], AF.Copy), [t_mm] + free)
                        P.release(bk, t_e)
                        evs.append(t_e)
                    return evs
                B = {}

                def prevd(bt, *keys):
                    out = []
                    if bt - 1 in B:
                        for k in keys:
                            v = B[bt - 1].get(k)
                            if v is None:
                                continue
                            out += v if isinstance(v, list) else [v]
                    return out

                def s1a(bt):
                    p0 = bt * NB
                    ub = bt % 2
                    D_ = B.setdefault(bt, {})
                    dprev = prevd(bt, "pd", "fin", "w_done", "E0", "g_done", "inj")
                    a = dv(lambda e, p0=p0: e.tensor_tensor(trs[:], bcast(THT[:, p0:p0 + NB], shn, 2), bcast(jtab[:], shn, 1), ALU.mult),
                           SETUP + [t_j] + dprev)
                    b_ = dv(lambda e: e.tensor_scalar(tr2[:], trs[:], MAGIC, MAGIC, ALU.add, ALU.subtract), [a])
                    c_ = dv(lambda e: e.tensor_tensor(tr2[:], trs[:], tr2[:], ALU.subtract), [b_])
                    t_sin = P.op("act", lambda e: e.activation(cs[:, 1, :, :], tr2[:], AF.Sin, scale=TWO_PI), [c_] + dprev)
                    a2 = dv(lambda e: e.tensor_scalar(trs[:], trs[:], 0.25, None, ALU.add), [c_])
                    b2 = dv(lambda e: e.tensor_scalar(tr2[:], trs[:], MAGIC, MAGIC, ALU.add, ALU.subtract), [a2, t_sin])
                    c2_ = dv(lambda e: e.tensor_tensor(tr2[:], trs[:], tr2[:], ALU.subtract), [b2])
                    t_cos = P.op("act", lambda e: e.activation(cs[:, 0, :, :], tr2[:], AF.Sin, scale=TWO_PI), [c2_])
                    t_rm = dv(lambda e, p0=p0: e.tensor_tensor(Rm[:], bcast(Rr[:, p0:p0 + NB], shn, 2), bcast(m0tab[:], shn, 1), ALU.mult),
                              SETUP + [t_m0] + dprev)
                    TAB = [t_sin, t_cos, t_rm]
                    hre = h0[:, 0, :].rearrange("q (s a) -> q a s", a=16)[:, p0:p0 + NB, :]
                    him = h0[:, 1, :].rearrange("q (s a) -> q a s", a=16)[:, p0:p0 + NB, :]
                    shq = [128, NB, NSQ]
                    c1b = bcast(cs[:, 0, :, 2], shq, 2); s1b_ = bcast(cs[:, 1, :, 2], shq, 2)
                    x1 = dv(lambda e: e.tensor_tensor(e0[:, 0, :, :], c1b, hre, ALU.mult), TAB + dprev)
                    x2 = dv(lambda e: e.tensor_tensor(e1[:], s1b_, him, ALU.mult), TAB + dprev)
                    x3 = dv(lambda e: e.tensor_tensor(e0[:, 0, :, :], e0[:, 0, :, :], e1[:], ALU.subtract), [x1, x2])
                    x4 = dv(lambda e: e.tensor_tensor(e0[:, 1, :, :], s1b_, hre, ALU.mult), TAB + dprev)
                    x5 = dv(lambda e: e.tensor_tensor(e1[:], c1b, him, ALU.mult), [x3])
                    x6 = dv(lambda e: e.tensor_tensor(e0[:, 1, :, :], e0[:, 1, :, :], e1[:], ALU.add), [x4, x5])
                    E0 = [x3, x6]
                    u_ev = UEV[bt]
                    w_done = []
                    for pp in range(NB):
                        pr = p0 + pp
                        bk2, bd = P.bank2()
                        t_mm = None
                        for r in range(2):
                            for e_ in range(2):
                                t_mm = P.op("pe", lambda e, bk2=bk2, r=r, e_=e_, pr=pr, pp=pp: e.matmul(
                                    ps[e_ * 64:(e_ + 1) * 64, bk2 + r, 0:NCH], BT[:, r, pr, e_ * 64:(e_ + 1) * 64], U[:, ub, 2 * pp + e_, :],
                                    start=True, stop=True), ([u_ev[2 * pp], u_ev[2 * pp + 1]] + SETUP + bd) if (r == 0 and e_ == 0) else [],
                                    inc=(r == 1 and e_ == 1))
                        sre = ps[:, bk2, 0:NCH]; sim = ps[:, bk2 + 1, 0:NCH]
                        co = cs[:, 0, pp, :]; si = cs[:, 1, pp, :]
                        r1 = dv(lambda e, pp=pp, sre=sre, co=co: e.tensor_tensor(w1[:, pp, :], co, sre, ALU.mult), [t_mm] + TAB)
                        r2 = dv(lambda e, pp=pp, sim=sim, si=si: e.tensor_tensor(w2[:, pp, :], si, sim, ALU.mult), [t_mm] + TAB)
                        r3 = dv(lambda e, pp=pp: e.tensor_tensor(W[:, 0, pp, :], w1[:, pp, :], w2[:, pp, :], ALU.add), [r1, r2] + dprev)
                        r4 = dv(lambda e, pp=pp, sim=sim, co=co: e.tensor_tensor(w1[:, pp, :], co, sim, ALU.mult), [r3])
                        r5 = dv(lambda e, pp=pp, sre=sre, si=si: e.tensor_tensor(w2[:, pp, :], si, sre, ALU.mult), [r3])
                        P.release(bk2, r5, 2)
                        r6 = dv(lambda e, pp=pp: e.tensor_tensor(W[:, 1, pp, :], w1[:, pp, :], w2[:, pp, :], ALU.subtract), [r4, r5])
                        w_done += [r3, r6]
                    icols = slice(257, 257 + 36, 9)
                    i1 = dv(lambda e: e.tensor_copy(W[:, 0, :, icols], e0[:, 0, :, :]), w_done + E0)
                    i2 = dv(lambda e: e.tensor_copy(W[:, 1, :, icols], e0[:, 1, :, :]), w_done + E0)
                    g_done = []
                    for r in range(2):
                        for pp in range(NB):
                            g_done.append(dv(lambda e, r=r, pp=pp: e.tensor_tensor_scan(
                                G[:, r, pp, :], Rm[:, pp, :], W[:, r, pp, :], 0.0, ALU.mult, ALU.add), [i1, i2] + TAB + dprev))
                    D_.update(TAB=TAB, E0=E0, w_done=w_done, inj=[i1, i2], g_done=g_done, u_ev=u_ev)

                def s1b(bt):
                    p0 = bt * NB
                    D_ = B[bt]
                    g_done = D_["g_done"]
                    yprev = [y_done[bt - 1]] if (bt - 1) in y_done else []
                    pd = []
                    pd.append(dv(lambda e: e.tensor_tensor(PP[:, 0, :, :], cs[:, 0, :, :], G[:, 0, :, :], ALU.mult), g_done + yprev))
                    pd.append(dv(lambda e: e.tensor_tensor(PP[:, 1, :, :], cs[:, 1, :, :], G[:, 1, :, :], ALU.mult), g_done + yprev))
                    pd.append(dv(lambda e: e.tensor_tensor(PP[:, 2, :, :], cs[:, 1, :, :], G[:, 0, :, :], ALU.mult), g_done + yprev))
                    pd.append(dv(lambda e: e.tensor_tensor(PP[:, 3, :, :], cs[:, 0, :, :], G[:, 1, :, :], ALU.mult), g_done + yprev))
                    lsl = slice(256, NCH, 9)
                    fprev = prevd(bt, "fin")
                    f1 = dv(lambda e: e.tensor_tensor(hl1[:], cs[:, 0, :, lsl], G[:, 0, :, lsl], ALU.mult), g_done + fprev)
                    f2 = dv(lambda e: e.tensor_tensor(hl2[:], cs[:, 1, :, lsl], G[:, 1, :, lsl], ALU.mult), g_done + fprev)
                    f3 = dv(lambda e, p0=p0: e.tensor_tensor(HL[:, 0, :, p0:p0 + NB].rearrange("q s a -> q a s"), hl1[:], hl2[:], ALU.subtract), [f1, f2])
                    f4 = dv(lambda e: e.tensor_tensor(hl1[:], cs[:, 1, :, lsl], G[:, 0, :, lsl], ALU.mult), [f3])
                    f5 = dv(lambda e: e.tensor_tensor(hl2[:], cs[:, 0, :, lsl], G[:, 1, :, lsl], ALU.mult), [f3])
                    f6 = dv(lambda e, p0=p0: e.tensor_tensor(HL[:, 1, :, p0:p0 + NB].rearrange("q s a -> q a s"), hl1[:], hl2[:], ALU.add), [f4, f5])
                    D_.update(pd=pd, fin=[f6])

                def s2(bt):
                    p0 = bt * NB
                    ub = bt % 2
                    D_ = B[bt]
                    pd = D_["pd"]
                    soprev = prevd(bt, "t_so")
                    y_ev = []
                    t_mm = None
                    for gi in range(8):
                        pp = gi // 2
                        bk, bd = P.bank()
                        P.op("pe", lambda e, bk=bk, gi=gi: e.matmul(
                            ps[:, bk, 1:NCH], Kin[:, bt * 8 + gi, :], U[:, ub, gi, 1:NCH], start=True, stop=False), pd + bd + SETUP, inc=False)
                        for k, (ci, pi) in enumerate(((0, 0), (1, 1), (2, 2), (2, 3))):
                            t_mm = P.op("pe", lambda e, bk=bk, ci=ci, pi=pi, gg_=bt * 8 + gi, pp=pp, k=k: e.matmul(
                                ps[:, bk, 1:NCH], Cw[:, ci, gg_, :], PP[:, pi, pp, 0:NCH - 1], start=False, stop=(k == 3)), [], inc=(k == 3))
                        t_e = P.op("act", lambda e, bk=bk, gi=gi: e.activation(ygl[:, gi, 1:NCH], ps[:, bk, 1:NCH], AF.Gelu_apprx_tanh), [t_mm] + soprev)
                        P.release(bk, t_e)
                        y_ev.append(t_e)
                    y_done[bt] = t_mm
                    last = []
                    for t_ in range(8):
                        bk, bd = P.bank()
                        for gi in range(8):
                            t_mm = P.op("pe", lambda e, bk=bk, gi=gi, t_=t_: e.matmul(
                                ps[:, bk, 1:NCH], E_b[:, t_ * 8 + gi, :], ygl[:, gi, 1:NCH], start=(gi == 0), stop=(gi == 7)),
                                (y_ev + bd) if gi == 0 else [], inc=(gi == 7))
                        t_e = P.op("act", lambda e, bk=bk, t_=t_: e.activation(yT[:, bt, 8 + t_:TOKP:8], ps[:, bk, 1:NCH], AF.Copy), [t_mm])
                        P.release(bk, t_e)
                        last.append(t_e)
                    D_.update(y_ev=y_ev, last=last, t_so=[t_mm])
                    return last

                UEV[0] = shuf_in(0)
                s1a(0)
                s1b(0)
                UEV[1] = shuf_in(1)
                for bt in range(4):
                    if bt + 1 < 4:
                        s1a(bt + 1)
                    yT_w += s2(bt)
                    if bt + 1 < 4:
                        s1b(bt + 1)
                    if bt + 2 < 4:
                        UEV[bt + 2] = shuf_in(bt + 2)
                prev_batch = []
                for bt in range(4):
                    for k in ("last", "fin", "t_so", "pd", "g_done", "w_done"):
                        prev_batch += B[bt][k]
                dump("yT", yT[:], [128, 4, TOKP + 256], yT_w)
                dump("HL", HL[:], [128, 2, 5, 16], prev_batch)
                hlo = trs[:].rearrange("p a c -> p (a c)")[:, 0:256].rearrange("p (r m) -> p r m", m=128)
                bkh, bd = P.bank()
                t_h = None
                for r in range(2):
                    t_h = P.op("pe", lambda e, r=r: e.transpose(ps[0:80, bkh, r * 128:(r + 1) * 128], HL[:, r, :, :].rearrange("q s a -> q (s a)"), identf[:]),
                               prev_batch + bd)
                t_ho = P.op("act", lambda e: e.activation(hlo[0:80, :, :], ps[0:80, bkh, 0:256].rearrange("p (r m) -> p r m", m=128), AF.Copy), [t_h])
                P.release(bkh, t_ho)
                out_tok.append(P.dma("sp", lambda e: e.dma_start(out=hre_o, in_=hlo[0:80, 0, :]), [t_ho], slot="outs"))
                out_tok.append(P.dma("sp", lambda e: e.dma_start(out=him_o, in_=hlo[0:80, 1, :]), [t_ho], slot="outs"))

                wglu_b = W[:].rearrange("p a b c -> p (a b c)").bitcast(BF16)[:, 0:2048].rearrange("p (k n) -> p k n", n=512)
                t_wglu = P.dma("pool", lambda e: e.dma_start(out=wglu_b, in_=wglu_d.rearrange("(kt p) n -> p kt n", p=128)), prev_batch, slot="wglu")
                sg = PP[:].rearrange("p a b c -> p (a b c)")[:, 0:4096].rearrange("p (u m n) -> p u m n", m=4, n=512)
                ysq5 = G[:].rearrange("p a b c -> p (a b c)").bitcast(BF16)[:, 0:4096].rearrange("p (u m n) -> p u m n", m=4, n=512)
                glu_ev = []
                bks, bds = P.bank()
                G1 = {}
                TST = {}
                SGEV = {}

                def glu_gate(b):
                    n = 512 if b < 4 else 256
                    u = b % 2
                    free = G1.get(b - 2, [])
                    evs = []
                    for m in range(4):
                        bk, bd = P.bank()
                        if bk == bks:
                            bk, bd = P.bank()
                        t_mm = None
                        for kt in range(4):
                            rhs, _ = blk_cols(yT, kt, b)
                            t_mm = P.op("pe", lambda e, bk=bk, m=m, kt=kt, rhs=rhs, n=n, b=b: e.matmul(
                                ps_cols(bk, n, b), wglu_b[:, kt, m * 128:(m + 1) * 128], rhs, start=(kt == 0), stop=(kt == 3)),
                                (yT_w + [t_wglu] + bd) if kt == 0 else [], inc=(kt == 3))
                        t_e = P.op("act", lambda e, bk=bk, m=m, n=n, u=u: e.activation(sg[:, u, m, 0:n], ps[:, bk, 0:n], AF.Sigmoid),
                                   [t_mm] + free + prev_batch)
                        P.release(bk, t_e)
                        evs.append(t_e)
                    SGEV[b] = evs

                def glu_dve(b):
                    n = 512 if b < 4 else 256
                    u = b % 2
                    blk_ev = []
                    g1s = []
                    tst_prev = [TST[b - 2]] if (b - 2) in TST else []
                    for m in range(4):
                        src, _ = blk_cols(yT, m, b)
                        dst = oblk(y2T, m, b)
                        dst3 = oblk(y2T, m, b, three=True)
                        sgv = sg[:, u, m, 0:n] if b < 4 else sg[:, u, m, 0:256].rearrange("p (q c) -> p q c", c=64)
                        g1 = dv(lambda e, src=src, dst3=dst3, sgv=sgv: e.tensor_tensor(dst3, src, sgv, ALU.mult), SGEV[b])
                        g2 = dv(lambda e, dst=dst, m=m, n=n, u=u: e.tensor_tensor(ysq5[:, u, m, 0:n], dst, dst, ALU.mult), [g1] + tst_prev + prev_batch)
                        g3 = P.op("act", lambda e, dst=dst, m=m: e.activation(dst, dst, AF.Copy, scale=colv[:, 1, m:m + 1]), [g2, C0, C1])
                        glu_ev.append(g3)
                        blk_ev.append(g2)
                        g1s.append(g1)
                    G1[b] = g1s
                    return blk_ev

                def glu_stats(b, blk_ev, tile0):
                    n = 512 if b < 4 else 256
                    nt = n // 128
                    u = b % 2
                    t_st = None
                    for ti in range(nt):
                        i = tile0 + ti
                        for j in range(4):
                            t_st = P.op("pe", lambda e, i=i, j=j, ti=ti, u=u: e.matmul(
                                ps[:, bks, i:i + 1], ysq5[:, u, j, ti * 128:(ti + 1) * 128], ones_b[:, 0:1], start=(j == 0), stop=(j == 3)),
                                (blk_ev + bds + [C0]) if (ti == 0 and j == 0) else [], inc=(ti == nt - 1 and j == 3))
                    TST[b] = t_st
                    return t_st

                glu_gate(0)
                tile0 = 0
                t_st = None
                for b in range(5):
                    if b + 1 < 5:
                        glu_gate(b + 1)
                    bev = glu_dve(b)
                    t_st = glu_stats(b, bev, tile0)
                    tile0 += 4 if b < 4 else 2
                t_rs5 = rstd_chain(ps[:, bks, 0:NTILE], rstd_s5[:], ss5[:], 1.0 / 512, [t_st])
                P.release(bks, t_rs5)
                S5_DONE = glu_ev + [t_rs5]
                dump("y2T", y2T[:], [128, 4, TOKP + 256], S5_DONE)
                dump("rstd_s5", rstd_s5[:], [128, NTILE], S5_DONE)
                if stop == "B":
                    P.op("sp", lambda e: e.nop(), list(out_tok) + list(dbg_o.values()), inc=False)
                P.emit()
            if stop == "B":
                pb.close(); zs.close()
                return nc
        zs.close()
        sis.close()
        pc = ExitStack()
        with pc:
            wout_b = sb("wout_b", [128, 8, D], BF16, pc); wgate_b = sb("wgate_b", [128, 8, D], BF16, pc)
            wple_b = sb("wple_b", [128, 2, D], BF16, pc)
            wres = []
            for kt in range(0, 8, 4):
                wres.append(P.dma("sp", lambda e, kt=kt: e.dma_start(
                    out=wout_b[:, kt:kt + 4, :], in_=wout16.rearrange("(kt p) n -> p kt n", p=128)[:, kt:kt + 4, :]), [WCAST], slot="wout"))
            WOUT = wres[-1]
            xr = sb("xr", [128, 2, 4, D], F32, pc)
            ssC = sb("ssC", [128, 4 * NTILE], F32, pc); tmpC = sb("tmpC", [128, 4 * NTILE], F32, pc); rsC = sb("rsC", [128, 4 * NTILE], F32, pc)
            xnc = sb("xnc", [128, 2, D], BF16, pc)
            xnT2 = sb("xnT2", [128, 8, 512], BF16, pc)
            xnT3 = sb("xnT3", [128, 2, 8, 128], BF16, pc)
            aT = sb("aT", [128, 32, 512], BF16, pc)
            wupb = sb("wupb", [128, 3, 8, 256], BF16, pc)
            wdnb = sb("wdnb", [128, 3, 4, 512], BF16, pc)
            pb16 = sb("pb16", [128, 4, 256], BF16, pc)
            pT = sb("pT", [128, 2, 512], BF16, pc)
            sgc = sb("sgc", [128, 2, D], F32, pc)
            t_ssc = P.op("dve", lambda e: e.memset(ssC[:], 0.0))
            wup_r = wup16.rearrange("(kt p) n -> p kt n", p=128)
            wdn_r = wdn16.rearrange("(f p) n -> p f n", p=128)

            class St:
                pass
            st = St()
            st.wup_rd = [None] * 3; st.wdn_rd = [None] * 3
            st.nup = 0; st.ndn = 0
            st.xr_free = [[None] * 4, [None] * 4]
            st.xnc_rd = [None] * 2
            st.xnc_sq = [None] * 2
            st.xnT2_rd = None; st.xnT3_rd = [None, None]; st.aT_rd = None; st.pT_rd = None; st.pb_rd = [None] * 4
            st.sgc_rd = [None, None]
            st.wq = []

            blocks = []
            t0_ = 0
            for b in range(5):
                nt = 4 if b < 4 else 2
                blocks.append((b, nt, list(range(t0_, t0_ + nt))))
                t0_ += nt

            def take(bk):
                return list(P.bank_tok[bk])

            def take2(bk):
                return list(P.bank_tok[bk]) + list(P.bank_tok[bk + 1])

            pieces = []
            for (b, nt, tiles) in blocks:
                for pc_ in range(16):
                    pieces.append(("up", b, pc_))
                for h in range(2):
                    for pc_ in range(8):
                        pieces.append(("dn", b, h, pc_))
            st.pidx = 0
            st.ptok = {}

            def issue_piece():
                if st.pidx >= len(pieces):
                    return
                pz = pieces[st.pidx]
                st.pidx += 1
                if pz[0] == "up":
                    u = st.nup % 3
                    st.nup += 1
                    pc_ = pz[2]
                    t = P.dma("pool", lambda e, u=u, pc_=pc_: e.dma_start(out=wupb[:, u, :, :], in_=wup_r[:, :, pc_ * 256:(pc_ + 1) * 256]),
                              [st.wup_rd[u], WCAST], slot="wup%d" % u)
                    st.ptok[pz] = (t, u)
                else:
                    u = st.ndn % 3
                    st.ndn += 1
                    h, pc_ = pz[2], pz[3]
                    t = P.dma("pool", lambda e, u=u, pc_=pc_, h=h: e.dma_start(
                        out=wdnb[:, u, :, :], in_=wdn_r[:, pc_ * 4:(pc_ + 1) * 4, h * 512:(h + 1) * 512]), [st.wdn_rd[u], WCAST], slot="wdn%d" % u)
                    st.ptok[pz] = (t, u)

            def load_tile(blk, ti):
                b, nt, tiles = blk
                i = tiles[ti]
                return P.dma("sp", lambda e, ti=ti, i=i, b=b: e.dma_start(out=xr[:, b % 2, ti, :], in_=x_d[i * 128:(i + 1) * 128, :]),
                             [st.xr_free[b % 2][ti]], slot="xr%d_%d" % (b % 2, ti))

            def load_block(blk):
                return [load_tile(blk, ti) for ti in range(blk[1])]

            def load_p(blk):
                b, nt, tiles = blk
                tp = []
                for ti, i in enumerate(tiles):
                    tp.append(P.dma("pool", lambda e, ti=ti, i=i: e.dma_start(out=pb16[:, ti, :], in_=p_d[i * 128:(i + 1) * 128, :]),
                                    [st.pb_rd[ti]], slot="pb%d" % ti))
                return tp

            def p_transposes(blk, tp, bk):
                b, nt, tiles = blk
                evs = []
                for ti, i in enumerate(tiles):
                    bd = take(bk)
                    t_tr = None
                    for j in range(2):
                        t_tr = P.op("pe", lambda e, j=j, ti=ti: e.transpose(
                            psb[:, bk, j * 128:(j + 1) * 128], pb16[:, ti, j * 128:(j + 1) * 128], identb[:]),
                            ([tp[ti], C0] + bd) if j == 0 else [], inc=(j == 1))
                    t_e = P.op("act", lambda e, ti=ti: e.activation(
                        pT[:, :, ti * 128:(ti + 1) * 128], psb[:, bk, 0:256].rearrange("p (k c) -> p k c", c=128), AF.Copy), [t_tr, st.pT_rd])
                    P.release(bk, t_e)
                    st.pb_rd[ti] = t_tr
                    evs.append(t_e)
                return evs

            def s1a(blk, ti, tx, bk2):
                b, nt, tiles = blk
                i = tiles[ti]
                bd = take2(bk2)
                t_a = None
                for h in range(2):
                    for kt in range(4):
                        t_a = P.op("pe", lambda e, h=h, kt=kt, i=i: e.matmul(
                            ps[:, bk2 + h, :], otile(y2T, kt, i), wout_b[:, kt, h * 512:(h + 1) * 512],
                            start=(kt == 0), stop=(kt == 3)), (S5_DONE + POOL_DONE + [WOUT] + bd) if (h == 0 and kt == 0) else [],
                            inc=(h == 1 and kt == 3))
                pa_ = ps[:, bk2:bk2 + 2, :].rearrange("p a c -> p (a c)")
                t1 = P.op("dve", lambda e, ti=ti, i=i, b=b: e.scalar_tensor_tensor(
                    xr[:, b % 2, ti, :], pa_, rstd_s5[:, i:i + 1], xr[:, b % 2, ti, :], ALU.mult, ALU.add), [t_a, tx[ti]])
                P.release(bk2, t1, 2)
                return t1

            def s1b(blk, ti, t1, bk2):
                b, nt, tiles = blk
                i = tiles[ti]
                bd = take2(bk2)
                t_b = None
                for h in range(2):
                    for kt in range(4):
                        t_b = P.op("pe", lambda e, h=h, kt=kt, i=i: e.matmul(
                            ps[:, bk2 + h, :], otile(ypT, kt, i), wout_b[:, 4 + kt, h * 512:(h + 1) * 512],
                            start=(kt == 0), stop=(kt == 3)), bd if (h == 0 and kt == 0) else [], inc=(h == 1 and kt == 3))
                pb_ = ps[:, bk2:bk2 + 2, :].rearrange("p a c -> p (a c)")
                t2 = P.op("dve", lambda e, ti=ti, i=i, b=b: e.scalar_tensor_tensor(
                    xr[:, b % 2, ti, :], pb_, rstd_po[:, i:i + 1], xr[:, b % 2, ti, :], ALU.mult, ALU.add), [t_b, t1])
                P.release(bk2, t2, 2)
                return t2

            def norm_a1(x_ap, k, xn_ap, deps, xn_free):
                t_sq = P.op("dve", lambda e: e.scalar_tensor_tensor(xn_ap, x_ap, 1.0, x_ap, ALU.mult, ALU.mult, accum_out=ssC[:, k:k + 1]),
                            deps + [t_ssc] + list(xn_free))
                return P.op("dve", lambda e: e.tensor_scalar(tmpC[:, k:k + 1], ssC[:, k:k + 1], 1.0 / D, EPS, ALU.mult, ALU.add), [t_sq])

            def norm_a2(x_ap, gk, k, xn_ap, t1):
                t2 = P.op("act", lambda e: e.activation(tmpC[:, k:k + 1], tmpC[:, k:k + 1], AF.Sqrt), [t1])
                t_r = P.op("dve", lambda e: e.reciprocal(rsC[:, k:k + 1], tmpC[:, k:k + 1]), [t2])
                return P.op("dve", lambda e: e.scalar_tensor_tensor(xn_ap, x_ap, rsC[:, k:k + 1], gbc3[:, gk - 1, :], ALU.mult, ALU.mult),
                            [t_r, C0, C1])

            def norm_a(x_ap, gk, k, xn_ap, deps, xn_free):
                return norm_a2(x_ap, gk, k, xn_ap, norm_a1(x_ap, k, xn_ap, deps, xn_free))

            def norm_b(xn_ap, dst, t_xn, bk, dst_free):
                bd = take(bk)
                t_tr = None
                for j in range(8):
                    t_tr = P.op("pe", lambda e, j=j: e.transpose(psb[:, bk, j * 128:(j + 1) * 128], xn_ap[:, j * 128:(j + 1) * 128], identb[:]),
                                ([t_xn] + bd) if j == 0 else [], inc=(j == 7))
                t_ev = P.op("act", lambda e: e.activation(dst, psb[:, bk, :].rearrange("p (k c) -> p k c", c=128), AF.Copy), [t_tr] + list(dst_free))
                P.release(bk, t_ev)
                return t_tr, t_ev

            def s4_gate(blk, ti, t_ev, pt_ev, bkg, bkp):
                b, nt, tiles = blk
                u = ti % 2
                bdg = take2(bkg)
                t_g = None
                for h in range(2):
                    for kt in range(8):
                        t_g = P.op("pe", lambda e, h=h, kt=kt, u=u: e.matmul(
                            ps[:, bkg + h, :], xnT3[:, u, kt, :], wgate_b[:, kt, h * 512:(h + 1) * 512],
                            start=(kt == 0), stop=(kt == 7)), ([t_ev, WRES] + bdg) if (h == 0 and kt == 0) else [], inc=(h == 1 and kt == 7))
                st.xnT3_rd[u] = t_g
                bdp = take2(bkp)
                t_pp = None
                for h in range(2):
                    for kt in range(2):
                        t_pp = P.op("pe", lambda e, h=h, kt=kt, ti=ti: e.matmul(
                            ps[:, bkp + h, :], pT[:, kt, ti * 128:(ti + 1) * 128], wple_b[:, kt, h * 512:(h + 1) * 512],
                            start=(kt == 0), stop=(kt == 1)), ([pt_ev[ti], WRES] + bdp) if (h == 0 and kt == 0) else [], inc=(h == 1 and kt == 1))
                st.pT_rd = t_pp
                return t_g, t_pp

            def s4_tail1(blk, ti, t_g, t_pp, t_xn, bkg, bkp):
                b, nt, tiles = blk
                i = tiles[ti]
                u = ti % 2
                xa = xr[:, b % 2, ti, :]
                t_sg = P.op("act", lambda e: e.activation(
                    sgc[:, u, :], ps[:, bkg:bkg + 2, :].rearrange("p a c -> p (a c)"), AF.Sigmoid), [t_g, st.sgc_rd[u]])
                P.release(bkg, t_sg, 2)
                t_m = P.op("dve", lambda e: e.tensor_tensor(
                    sgc[:, u, :], sgc[:, u, :], ps[:, bkp:bkp + 2, :].rearrange("p a c -> p (a c)"), ALU.mult), [t_pp, t_sg])
                P.release(bkp, t_m, 2)
                t_x3 = P.op("dve", lambda e: e.tensor_tensor(xa, xa, sgc[:, u, :], ALU.add), [t_m, t_xn])
                k = 4 * i + 2
                t_sq = P.op("dve", lambda e: e.scalar_tensor_tensor(xnc[:, ti % 2, :], xa, 1.0, xa, ALU.mult, ALU.mult, accum_out=ssC[:, k:k + 1]),
                            [t_x3, t_ssc, st.xnc_rd[ti % 2]])
                return P.op("dve", lambda e: e.tensor_scalar(tmpC[:, k:k + 1], ssC[:, k:k + 1], 1.0 / D, EPS, ALU.mult, ALU.add), [t_sq])

            def s4_tail2(blk, ti, t1):
                b, nt, tiles = blk
                i = tiles[ti]
                u = ti % 2
                xa = xr[:, b % 2, ti, :]
                k = 4 * i + 2
                t2 = P.op("act", lambda e: e.activation(tmpC[:, k:k + 1], tmpC[:, k:k + 1], AF.Sqrt), [t1])
                t_r = P.op("dve", lambda e: e.reciprocal(rsC[:, k:k + 1], tmpC[:, k:k + 1]), [t2])
                t_y = P.op("dve", lambda e: e.scalar_tensor_tensor(
                    sgc[:, u, :], xa, rsC[:, k:k + 1], gbc3[:, 2, :], ALU.mult, ALU.mult), [t_r, C0, C1])
                st.xr_free[b % 2][ti] = t_y
                t_o = P.dma("sp", lambda e: e.dma_start(out=y_o[i * 128:(i + 1) * 128, :], in_=sgc[:, u, :]), [t_y], slot="yo%d" % u)
                st.sgc_rd[u] = t_o
                out_tok.append(t_o)

            def s4_tail(blk, ti, t_g, t_pp, t_xn, bkg, bkp):
                s4_tail2(blk, ti, s4_tail1(blk, ti, t_g, t_pp, t_xn, bkg, bkp))

            for kt in range(0, 8, 4):
                wres.append(P.dma("sp", lambda e, kt=kt: e.dma_start(
                    out=wgate_b[:, kt:kt + 4, :], in_=wgate16.rearrange("(kt p) n -> p kt n", p=128)[:, kt:kt + 4, :]), [WCAST], slot="wres"))
            wres.append(P.dma("sp", lambda e: e.dma_start(
                out=wple_b[:], in_=wple16.rearrange("(kt p) n -> p kt n", p=128)), [WCAST], slot="wres"))
            WRES = wres[-1]
            for _ in range(3):
                issue_piece()
            TX = {0: dict(enumerate(load_block(blocks[0])))}
            X1 = {}
            N2 = {}

            def s1_full(blk, ti, part, ctx):
                b, nt, tiles = blk
                i = tiles[ti]
                if part == 0:
                    ctx["t1"] = s1a(blk, ti, TX[b], 0)
                elif part == 1:
                    ctx["t2"] = s1b(blk, ti, ctx["t1"], 6)
                elif part == 2:
                    ctx["a1"] = norm_a1(xr[:, b % 2, ti, :], 4 * i, xnc[:, ti % 2, :], [ctx["t2"]], [st.xnc_rd[ti % 2]])
                elif part == 3:
                    ctx["xn"] = norm_a2(xr[:, b % 2, ti, :], 1, 4 * i, xnc[:, ti % 2, :], ctx["a1"])
                else:
                    t_tr, t_ev = norm_b(xnc[:, ti % 2, :], xnT2[:, :, ti * 128:(ti + 1) * 128], ctx["xn"], 0, [st.xnT2_rd])
                    st.xnc_rd[ti % 2] = t_tr
                    N2.setdefault(b, []).append(t_ev)
                    X1.setdefault(b, {})[ti] = ctx["t2"]

            b0 = blocks[0]
            c0x = [{} for _ in range(b0[1])]
            for t0p in range(0, b0[1], 2):
                for part in range(5):
                    for ti in range(t0p, min(t0p + 2, b0[1])):
                        s1_full(b0, ti, part, c0x[ti])

            PT_EV = {}
            X2 = {}
            for bi, blk in enumerate(blocks):
                b, nt, tiles = blk
                n = nt * 128
                prev = blocks[bi - 1] if bi > 0 else None
                nxt = blocks[bi + 1] if bi + 1 < len(blocks) else None
                s4ctx = {}
                if prev is not None:
                    pnt = prev[1]
                    sched = {}
                    for ti in range(pnt):
                        sched.setdefault(5 * ti, []).append(("a1", ti))
                        sched.setdefault(5 * ti + 3, []).append(("a2", ti))
                        sched.setdefault(5 * ti + 7, []).append(("b1", ti))
                        sched.setdefault(5 * ti + 8, []).append(("b2", ti))
                        sched.setdefault(5 * ti + 11, []).append(("c1", ti))
                        sched.setdefault(5 * ti + 16, []).append(("c2", ti))
                else:
                    sched = {}
                a_ev = []
                mm_last = None
                for pc_ in range(16):
                    t_w, u = st.ptok[("up", b, pc_)]
                    for fl in range(2):
                        f = pc_ * 2 + fl
                        bk = (0, 1, 7)[f % 3]
                        bd = take(bk)
                        for kt in range(8):
                            mm_last = P.op("pe", lambda e, bk=bk, u=u, fl=fl, kt=kt, n=n: e.matmul(
                                ps[:, bk, 0:n], wupb[:, u, kt, fl * 128:(fl + 1) * 128], xnT2[:, kt, 0:n], start=(kt == 0), stop=(kt == 7)),
                                (N2[b] + [t_w] + bd) if kt == 0 else [], inc=(kt == 7))
                        t_r = P.op("act", lambda e, bk=bk, f=f, n=n: e.activation(aT[:, f, 0:n], ps[:, bk, 0:n], AF.Relu), [mm_last, st.aT_rd])
                        P.release(bk, t_r)
                        t_e = P.op("dve", lambda e, f=f, n=n: e.tensor_tensor(aT[:, f, 0:n], aT[:, f, 0:n], aT[:, f, 0:n], ALU.mult), [t_r])
                        a_ev.append(t_e)
                        for (kind, ti) in sched.get(f, []):
                            pb_, pnt_, ptiles = prev
                            pi = ptiles[ti]
                            u2 = ti % 2
                            if kind == "a1":
                                s4ctx[ti] = {}
                                s4ctx[ti]["a1"] = norm_a1(xr[:, pb_ % 2, ti, :], 4 * pi + 1, xnc[:, ti % 2, :], [X2[pb_][ti]], [st.xnc_rd[ti % 2]])
                            elif kind == "a2":
                                s4ctx[ti]["xn"] = norm_a2(xr[:, pb_ % 2, ti, :], 2, 4 * pi + 1, xnc[:, ti % 2, :], s4ctx[ti]["a1"])
                            elif kind == "b1":
                                t_tr, t_ev = norm_b(xnc[:, ti % 2, :], xnT3[:, u2, :, :], s4ctx[ti]["xn"], 6, [st.xnT3_rd[u2]])
                                st.xnc_rd[ti % 2] = t_tr
                                s4ctx[ti]["ev"] = t_ev
                            elif kind == "b2":
                                s4ctx[ti]["g"], s4ctx[ti]["pp"] = s4_gate(prev, ti, s4ctx[ti]["ev"], PT_EV[pb_], 2, 4)
                            elif kind == "c1":
                                s4ctx[ti]["c1"] = s4_tail1(prev, ti, s4ctx[ti]["g"], s4ctx[ti]["pp"], s4ctx[ti]["xn"], 2, 4)
                            else:
                                s4_tail2(prev, ti, s4ctx[ti]["c1"])
                                if nxt is not None and ti < nxt[1]:
                                    TX.setdefault(nxt[0], {})[ti] = load_tile(nxt, ti)
                    st.wup_rd[u] = mm_last
                    issue_piece()
                st.xnT2_rd = mm_last
                tp = load_p(blk)
                if nxt is not None:
                    d_ = TX.setdefault(nxt[0], {})
                    for ti in range(nxt[1]):
                        if ti not in d_:
                            d_[ti] = load_tile(nxt, ti)
                x2_ev = [None] * nt
                if nxt is not None:
                    sched1 = {}
                    for ti in range(nxt[1]):
                        for part, off in enumerate((0, 6, 12, 16, 26)):
                            sched1.setdefault(2 + 10 * ti + off, []).append((ti, part))
                else:
                    sched1 = {}
                s1ctx = {}
                step = 0
                for h in range(2):
                    banks = [2, 3, 4, 5][:nt]
                    bdall = []
                    for bk in banks:
                        bdall += take(bk)
                    for pc_ in range(8):
                        t_w, u = st.ptok[("dn", b, h, pc_)]
                        if pc_ < 7:
                            order = [(fl, ti) for fl in range(4) for ti in range(nt)]
                        else:
                            order = [(fl, ti) for ti in range(nt) for fl in range(4)]
                        tile_done = {}
                        for oi, (fl, ti) in enumerate(order):
                            f = pc_ * 4 + fl
                            first = (pc_ == 0 and oi == 0)
                            is_inc = (oi == len(order) - 1) or (pc_ == 7 and fl == 3)
                            mm_last = P.op("pe", lambda e, bk=banks[ti], u=u, fl=fl, f=f, ti=ti: e.matmul(
                                ps[:, bk, :], aT[:, f, ti * 128:(ti + 1) * 128], wdnb[:, u, fl, :], start=(f == 0), stop=(f == 31)),
                                ((a_ev + [t_w]) if first else ([t_w] if oi == 0 else [])) + (take(banks[ti]) if (pc_ == 0 and fl == 0) else []),
                                inc=is_inc)
                            if pc_ == 7 and fl == 3:
                                tile_done[ti] = mm_last
                            if oi % nt == nt - 1:
                                for (ti1, part) in sched1.get(step, []):
                                    s1_full(nxt, ti1, part, s1ctx.setdefault(ti1, {}))
                                step += 1
                        st.wdn_rd[u] = mm_last
                        issue_piece()
                    for ti in range(nt):
                        xa = xr[:, b % 2, ti, h * 512:(h + 1) * 512]
                        t_e = P.op("dve", lambda e, xa=xa, bk=banks[ti]: e.tensor_tensor(xa, xa, ps[:, bk, :], ALU.add), [tile_done[ti], X1[b][ti]])
                        P.release(banks[ti], t_e)
                        x2_ev[ti] = t_e
                for stp in sorted(k for k in sched1 if k >= step):
                    for (ti1, part) in sched1[stp]:
                        s1_full(nxt, ti1, part, s1ctx.setdefault(ti1, {}))
                st.aT_rd = mm_last
                X2[b] = x2_ev
                PT_EV[b] = p_transposes(blk, tp, 6)
            last = blocks[-1]
            lb, lnt, ltiles = last
            ec = [{} for _ in range(lnt)]
            for ti in range(lnt):
                ec[ti]["a1"] = norm_a1(xr[:, lb % 2, ti, :], 4 * ltiles[ti] + 1, xnc[:, ti % 2, :], [X2[lb][ti]], [st.xnc_rd[ti % 2]])
            for ti in range(lnt):
                ec[ti]["xn"] = norm_a2(xr[:, lb % 2, ti, :], 2, 4 * ltiles[ti] + 1, xnc[:, ti % 2, :], ec[ti]["a1"])
            for ti in range(lnt):
                u2 = ti % 2
                t_tr, t_ev = norm_b(xnc[:, ti % 2, :], xnT3[:, u2, :, :], ec[ti]["xn"], 6 + ti % 2, [st.xnT3_rd[u2]])
                st.xnc_rd[ti % 2] = t_tr
                ec[ti]["g"], ec[ti]["pp"] = s4_gate(last, ti, t_ev, PT_EV[lb], (2, 0)[ti % 2], 4 if ti % 2 == 0 else 4)
                ec[ti]["c1"] = s4_tail1(last, ti, ec[ti]["g"], ec[ti]["pp"], ec[ti]["xn"], (2, 0)[ti % 2], 4)
            for ti in range(lnt):
                s4_tail2(last, ti, ec[ti]["c1"])
            fin_deps = [t for t in list(out_tok) + list(dbg_o.values()) if t is not None]
            P.op("sp", lambda e: e.nop(), fin_deps, inc=False)
            P.emit()
    return nc


def host_consts():
    E = np.zeros((128, 64, 128), np.float32)
    for a in range(8):
        for b in range(8):
            for h in range(16):
                E[a * 16 + h, a * 8 + b, b * 16 + h] = 1.0
    mask = np.zeros((128, 128), np.float32)
    for s in range(8):
        for t in range(s, 8):
            mask[s * 16:(s + 1) * 16, t * 16:(t + 1) * 16] = 1.0
    k25 = np.array([-7, -6, -5, -4, -3, -2, -1, 0, 0, 1, 2, 3, 4, 5, 6, 7, 8, 7, 6, 5, 4, 3, 2, 1, 0], np.float32)
    j = np.zeros(NCH, np.float32)
    m0 = np.ones(NCH, np.float32)
    for s in range(5):
        c0 = chcol_init(s)
        n = 256 if s == 0 else 8
        j[c0] = -1.0
        m0[c0] = 0.0
        j[c0 + 1:c0 + 1 + n] = np.arange(n)
    icnt = (1.0 / np.arange(1, 17)).astype(np.float32)
    rep = lambda v: np.ascontiguousarray(np.broadcast_to(v[None, :], (128, v.shape[0]))).astype(np.float32)
    return {
        "c_identb": np.eye(128, dtype=np.float32).astype(ml_dtypes.bfloat16),
        "c_identf": np.eye(128, dtype=np.float32),
        "c_E": E.reshape(128, 64 * 128).astype(ml_dtypes.bfloat16),
        "c_mask": mask, "c_k25": rep(k25), "c_j": rep(j), "c_m0": rep(m0), "c_icnt": rep(icnt),
        "c_ones": np.ones((128, 8), np.float32).astype(ml_dtypes.bfloat16),
    }


_NC_CACHE = {}


def make_in_maps(inputs):
    f = lambda a: np.ascontiguousarray(np.asarray(a, dtype=np.float32))
    consts = host_consts()
    shared = {
        "g_mix_norm": f(inputs["g_mix_norm"][0]), "w_in": f(inputs["w_in"][0]),
        "lambda_re": f(inputs["lambda_re"][0]).reshape(16, 128), "lambda_im": f(inputs["lambda_im"][0]).reshape(16, 128),
        "log_dt": f(inputs["log_dt"][0]),
        "b_re": f(inputs["b_re"][0]).reshape(16, 128, 16), "b_im": f(inputs["b_im"][0]).reshape(16, 128, 16),
        "c_re": f(inputs["c_re"][0]).reshape(512, 64), "c_im": f(inputs["c_im"][0]).reshape(512, 64),
        "d_skip": f(inputs["d_skip"][0]), "w_glu": f(inputs["w_glu"][0]), "w_pool": f(inputs["w_pool"][0]),
        "pool_scale": f(inputs["pool_scale"][0]), "g_s5_out": f(inputs["g_s5_out"][0]), "g_pool_out": f(inputs["g_pool_out"][0]),
        "w_out": f(inputs["w_out"][0]), "g_mlp_norm": f(inputs["g_mlp_norm"][0]), "w_up": f(inputs["w_up"][0]),
        "w_down": f(inputs["w_down"][0]), "g_ple_norm": f(inputs["g_ple_norm"][0]), "w_ple_gate": f(inputs["w_ple_gate"][0]),
        "w_ple_proj": f(inputs["w_ple_proj"][0]), "g_final": f(inputs["g_final"]),
    }
    shared.update(consts)
    xp, xs = f(inputs["x_prompt"]), f(inputs["x_sample"])
    pp, psm = f(inputs["p_prompt"][0]), f(inputs["p_sample"][0])
    sre, sim, spl = f(inputs["state_s5_re"][0]), f(inputs["state_s5_im"][0]), f(inputs["state_pool"][0])
    maps = []
    for c in range(NCORES):
        m = dict(shared)
        m["x"] = np.concatenate([xp[c], xs[4 * c:4 * c + 4].reshape(NSQ * LS, D)], axis=0)
        m["p"] = np.concatenate([pp[c], psm[4 * c:4 * c + 4].reshape(NSQ * LS, 256)], axis=0)
        m["st_re"] = np.ascontiguousarray(sre[4 * c:4 * c + 4].reshape(NSQ * 16, 128))
        m["st_im"] = np.ascontiguousarray(sim[4 * c:4 * c + 4].reshape(NSQ * 16, 128))
        m["st_pool"] = np.ascontiguousarray(spl[4 * c:4 * c + 4])
        maps.append(m)
    return maps


def kernel(**inputs):
    if "nc" not in _NC_CACHE:
        _NC_CACHE["nc"] = build()
    nc = _NC_CACHE["nc"]
    maps = make_in_maps(inputs)
    res = run_bass_kernel_spmd(nc, maps, core_ids=list(range(NCORES)))
    R = res.results
    y_p = np.stack([R[c]["y"][:LP] for c in range(NCORES)], 0)
    y_s = np.concatenate([R[c]["y"][LP:].reshape(NSQ, LS, D) for c in range(NCORES)], 0)
    hre = [R[c]["h_re"].reshape(5, 32, 64) for c in range(NCORES)]
    him = [R[c]["h_im"].reshape(5, 32, 64) for c in range(NCORES)]
    pl = [R[c]["pool_new"] for c in range(NCORES)]
    re_p = np.stack([h[0] for h in hre], 0)[None]
    im_p = np.stack([h[0] for h in him], 0)[None]
    pool_p = np.stack([q[0] for q in pl], 0)[None]
    re_s = np.concatenate([h[1:] for h in hre], 0)[None]
    im_s = np.concatenate([h[1:] for h in him], 0)[None]
    pool_s = np.concatenate([q[1:] for q in pl], 0)[None]
    a = lambda v: np.ascontiguousarray(v, dtype=np.float32)
    return (a(y_p), a(y_s), a(re_p), a(im_p), a(pool_p), a(re_s), a(im_s), a(pool_s))
```
